# Optimizing a Trainium2 kernel written in Bass

```python
import jax, jax.numpy as jnp
from jax import lax
import numpy as np

D_MODEL = 1024
BATCH = 2
SEQ = 8192
DEPTH = 4

POOL_WINDOWS = (2, 4, 8, 16)
POOL_GROUPS = 4
POOL_GC = 64
POOL_W = POOL_GROUPS * POOL_GC
M_HEADS = 4
M_DH = 64
M_W = M_HEADS * M_DH
M_CONV = 4
M_CHUNK = 128
A_HEADS = 8
A_DH = 64
A_W = A_HEADS * A_DH
MOBA_BLOCK = 256
MOBA_TOPK = 3
Q_BLOCK = 128
MIX_W = POOL_W + M_W + A_W
OFF_MQK = POOL_W
OFF_MV = OFF_MQK + 2 * M_W
OFF_MO = OFF_MV + M_W
OFF_MI = OFF_MO + M_W
OFF_MF = OFF_MI + M_HEADS
OFF_A = OFF_MF + M_HEADS
IN_W = OFF_A + 3 * A_W
D_FF = 2816
FFN_CONV = 3
EPS = 1e-6

kernel_name = 'hybrid_pool_mlstm_moba_convffn'


def rmsnorm(x, g):
    xf = x.astype(jnp.float32)
    y = xf * lax.rsqrt(jnp.mean(xf * xf, axis=-1, keepdims=True) + EPS) * g.astype(jnp.float32)
    return y.astype(x.dtype)


def causal_dwconv(x, w):
    K = w.shape[0]
    S = x.shape[1]
    xp = jnp.pad(x, ((0, 0), (K - 1, 0), (0, 0)))
    y = xp[:, 0:S] * w[0]
    for j in range(1, K):
        y = y + xp[:, j:j + S] * w[j]
    return y


def pool_mixer(xp, w_pool, scale):
    B, S, _ = xp.shape
    xg = xp.astype(jnp.float32).reshape(B, S, POOL_GROUPS, POOL_GC)
    cs = jnp.concatenate([jnp.zeros((B, 1, POOL_GROUPS, POOL_GC), jnp.float32), jnp.cumsum(xg, axis=1)], axis=1)
    t = jnp.arange(S)
    outs = []
    for g, w in enumerate(POOL_WINDOWS):
        lo = jnp.maximum(t + 1 - w, 0)
        cnt = (t + 1 - lo).astype(jnp.float32)
        win_sum = cs[:, 1:, g] - cs[:, lo, g]
        outs.append(win_sum / cnt[None, :, None] - xg[:, :, g])
    d = jnp.stack(outs, axis=2)
    y = jnp.einsum('bsgc,gcd->bsgd', d, w_pool.astype(jnp.float32)).reshape(B, S, POOL_W)
    return y * scale.astype(jnp.float32)


def mlstm_chunkwise(q, k, v, i_pre, f_pre):
    B, S, H, Dh = q.shape
    L = M_CHUNK
    NC = S // L
    def ch5(t):
        return t.reshape(B, NC, L, H, Dh).transpose(1, 0, 3, 2, 4)
    def ch4(t):
        return t.reshape(B, NC, L, H).transpose(1, 0, 3, 2)
    logf = jax.nn.log_sigmoid(f_pre)
    causal = jnp.tril(jnp.ones((L, L), dtype=bool))

    def step(carry, inp):
        C, n, m = carry
        qc, kc, vc, ic, lf = inp
        g = jnp.cumsum(lf, axis=-1)
        logd = jnp.where(causal, g[..., :, None] - g[..., None, :] + ic[..., None, :], -jnp.inf)
        inter = g + m[..., None]
        mt = jnp.maximum(inter, jnp.max(logd, axis=-1))
        w = jnp.einsum('bhtd,bhsd->bhts', qc, kc) * jnp.exp(logd - mt[..., None])
        a = jnp.exp(inter - mt)
        num = jnp.einsum('bhts,bhse->bhte', w, vc) + a[..., None] * jnp.einsum('bhtd,bhde->bhte', qc, C)
        den = jnp.sum(w, axis=-1) + a * jnp.einsum('bhtd,bhd->bht', qc, n)
        hc = num / jnp.maximum(jnp.abs(den), jnp.exp(-mt))[..., None]
        gl = g[..., -1]
        wl = gl[..., None] - g + ic
        m_new = jnp.maximum(gl + m, jnp.max(wl, axis=-1))
        wk = jnp.exp(wl - m_new[..., None])
        dec = jnp.exp(gl + m - m_new)
        C = dec[..., None, None] * C + jnp.einsum('bhs,bhsd,bhse->bhde', wk, kc, vc)
        n = dec[..., None] * n + jnp.einsum('bhs,bhsd->bhd', wk, kc)
        return (C, n, m_new), hc

    init = (jnp.zeros((B, H, Dh, Dh), jnp.float32), jnp.zeros((B, H, Dh), jnp.float32), jnp.zeros((B, H), jnp.float32))
    _, hs = lax.scan(step, init, (ch5(q), ch5(k), ch5(v), ch4(i_pre), ch4(logf)))
    return hs.transpose(1, 0, 3, 2, 4).reshape(B, S, H, Dh)


def moba_attention(q, k, v):
    q = q.astype(jnp.float32).transpose(0, 2, 1, 3)
    k = k.astype(jnp.float32).transpose(0, 2, 1, 3)
    v = v.astype(jnp.float32).transpose(0, 2, 1, 3)
    B, H, S, Dh = q.shape
    NB = -(-S // MOBA_BLOCK)
    pad = NB * MOBA_BLOCK - S
    kp = jnp.pad(k, ((0, 0), (0, 0), (0, pad), (0, 0)))
    vp = jnp.pad(v, ((0, 0), (0, 0), (0, pad), (0, 0)))
    kb = kp.reshape(B, H, NB, MOBA_BLOCK, Dh)
    vb = vp.reshape(B, H, NB, MOBA_BLOCK, Dh)
    kmean = jnp.mean(kb, axis=3)
    topk = min(MOBA_TOPK, NB)
    NQ = S // Q_BLOCK
    scale = A_DH ** -0.5
    qs = q.reshape(B, H, NQ, Q_BLOCK, Dh).transpose(2, 0, 1, 3, 4)
    bi = jnp.arange(B)[:, None, None, None]
    hi = jnp.arange(H)[None, :, None, None]
    blk_ids = jnp.arange(NB)

    def one(args):
        qi, ci = args
        q0 = ci * Q_BLOCK
        own = q0 // MOBA_BLOCK
        gate = jnp.einsum('bhqd,bhnd->bhqn', qi, kmean)
        gate = jnp.where((blk_ids < own)[None, None, None, :], gate, -jnp.inf)
        gval, gidx = lax.top_k(gate, topk)
        valid = jnp.isfinite(gval)
        ksel = kb[bi, hi, gidx]
        vsel = vb[bi, hi, gidx]
        s_sel = jnp.einsum('bhqd,bhqjkd->bhqjk', qi, ksel) * scale
        s_sel = jnp.where(valid[..., None], s_sel, -jnp.inf).reshape(B, H, Q_BLOCK, topk * MOBA_BLOCK)
        k_own = lax.dynamic_slice_in_dim(kp, own * MOBA_BLOCK, MOBA_BLOCK, axis=2)
        v_own = lax.dynamic_slice_in_dim(vp, own * MOBA_BLOCK, MOBA_BLOCK, axis=2)
        s_own = jnp.einsum('bhqd,bhkd->bhqk', qi, k_own) * scale
        kpos = own * MOBA_BLOCK + jnp.arange(MOBA_BLOCK)
        qpos = q0 + jnp.arange(Q_BLOCK)
        s_own = jnp.where((kpos[None, :] <= qpos[:, None])[None, None], s_own, -jnp.inf)
        p = jax.nn.softmax(jnp.concatenate([s_sel, s_own], axis=-1), axis=-1)
        p_sel = p[..., :topk * MOBA_BLOCK].reshape(B, H, Q_BLOCK, topk, MOBA_BLOCK)
        p_own = p[..., topk * MOBA_BLOCK:]
        return jnp.einsum('bhqjk,bhqjkd->bhqd', p_sel, vsel) + jnp.einsum('bhqk,bhkd->bhqd', p_own, v_own)

    outs = lax.map(one, (qs, jnp.arange(NQ)))
    return outs.transpose(1, 0, 3, 2, 4).reshape(B, S, H, Dh)


def setup_inputs(seed: int = 0) -> dict:
    key = jax.random.key(seed)
    ks = jax.random.split(key, 18)
    nrm = jax.random.normal
    f32 = jnp.float32
    return {
        'x': nrm(ks[0], (BATCH, SEQ, D_MODEL), f32),
        'ln1_g': 1.0 + 0.02 * nrm(ks[1], (DEPTH, D_MODEL), f32),
        'w_in': nrm(ks[2], (DEPTH, D_MODEL, IN_W), f32) * D_MODEL ** -0.5,
        'pool_w': nrm(ks[3], (DEPTH, POOL_GROUPS, POOL_GC, POOL_GC), f32) * POOL_GC ** -0.5,
        'pool_scale': 1.0 + 0.1 * nrm(ks[4], (DEPTH, POOL_W), f32),
        'm_conv': nrm(ks[5], (DEPTH, M_CONV, 2 * M_W), f32) * M_CONV ** -0.5,
        'm_b_i': 0.1 * nrm(ks[6], (DEPTH, M_HEADS), f32),
        'm_b_f': jnp.linspace(3.0, 6.0, M_HEADS, dtype=f32)[None, :] + 0.01 * nrm(ks[7], (DEPTH, M_HEADS), f32),
        'm_norm_g': 1.0 + 0.02 * nrm(ks[8], (DEPTH, M_W), f32),
        'a_q_g': 1.0 + 0.02 * nrm(ks[9], (DEPTH, A_DH), f32),
        'a_k_g': 1.0 + 0.02 * nrm(ks[10], (DEPTH, A_DH), f32),
        'w_out': nrm(ks[11], (DEPTH, MIX_W, D_MODEL), f32) * MIX_W ** -0.5,
        'ln2_g': 1.0 + 0.02 * nrm(ks[12], (DEPTH, D_MODEL), f32),
        'w_up': nrm(ks[13], (DEPTH, D_MODEL, 2 * D_FF), f32) * D_MODEL ** -0.5,
        'ffn_conv': nrm(ks[14], (DEPTH, FFN_CONV, 2 * D_FF), f32) * FFN_CONV ** -0.5,
        'w_down': nrm(ks[15], (DEPTH, D_FF, D_MODEL), f32) * D_FF ** -0.5,
    }


def reference(x, ln1_g, w_in, pool_w, pool_scale, m_conv, m_b_i, m_b_f, m_norm_g,
              a_q_g, a_k_g, w_out, ln2_g, w_up, ffn_conv, w_down):
    B, S, _ = x.shape
    for l in range(DEPTH):
        h = rmsnorm(x, ln1_g[l])
        z = h @ w_in[l]
        po = pool_mixer(z[..., :OFF_MQK], pool_w[l], pool_scale[l])
        qk = jax.nn.silu(causal_dwconv(z[..., OFF_MQK:OFF_MV], m_conv[l])).astype(jnp.float32)
        mq = qk[..., :M_W].reshape(B, S, M_HEADS, M_DH)
        mk = qk[..., M_W:].reshape(B, S, M_HEADS, M_DH) * (M_DH ** -0.5)
        mv = z[..., OFF_MV:OFF_MO].astype(jnp.float32).reshape(B, S, M_HEADS, M_DH)
        mo = jax.nn.sigmoid(z[..., OFF_MO:OFF_MI])
        mi = (z[..., OFF_MI:OFF_MF] + m_b_i[l]).astype(jnp.float32)
        mf = (z[..., OFF_MF:OFF_A] + m_b_f[l]).astype(jnp.float32)
        hm = mlstm_chunkwise(mq, mk, mv, mi, mf)
        hm = rmsnorm(hm, m_norm_g[l].reshape(M_HEADS, M_DH)).reshape(B, S, M_W).astype(x.dtype) * mo
        aq = rmsnorm(z[..., OFF_A:OFF_A + A_W].reshape(B, S, A_HEADS, A_DH), a_q_g[l])
        ak = rmsnorm(z[..., OFF_A + A_W:OFF_A + 2 * A_W].reshape(B, S, A_HEADS, A_DH), a_k_g[l])
        av = z[..., OFF_A + 2 * A_W:].reshape(B, S, A_HEADS, A_DH)
        ao = moba_attention(aq, ak, av).reshape(B, S, A_W)
        mix = jnp.concatenate([po.astype(x.dtype), hm, ao.astype(x.dtype)], axis=-1)
        x = x + mix @ w_out[l]
        h2 = rmsnorm(x, ln2_g[l])
        u = causal_dwconv(h2 @ w_up[l], ffn_conv[l])
        x = x + (jax.nn.silu(u[..., :D_FF]) * u[..., D_FF:]) @ w_down[l]
    return x
```

```python
from contextlib import ExitStack
import numpy as np
import concourse.bass as bass
import concourse.mybir as mybir
from concourse.bass_utils import run_bass_kernel_spmd

F32 = mybir.dt.float32
BF16 = mybir.dt.bfloat16
AF = mybir.ActivationFunctionType
ALU = mybir.AluOpType
AX = mybir.AxisListType

D = 1024
T = 8192
NCORE = 8
TQ = 2048
DFF = 2816
EPS = 1e-6
NEG = -10000.0


class Buf:
    __slots__ = ("name", "w", "r")

    def __init__(self, name):
        self.name = name
        self.w = None
        self.r = {}


class Prog:
    ENG = ("pe", "act", "dve", "pool", "sp")

    def __init__(self, nc, same_engine_sync=True):
        self.nc = nc
        self.lists = {e: [] for e in self.ENG}
        self.cnt = {}
        self.seen = {e: {} for e in self.ENG}
        self.same = same_engine_sync
        self.nbuf = 0

    def buf(self, name=None):
        self.nbuf += 1
        return Buf(name or f"b{self.nbuf}")

    def bufs(self, n, name="b"):
        return [self.buf(f"{name}{i}") for i in range(n)]

    def _dep(self, eng, tok):
        if tok is None:
            return
        k, v = tok
        if k == eng and (eng == "pe" or not self.same):
            return
        if self.seen[eng].get(k, 0) >= v:
            return
        self.seen[eng][k] = v
        self.lists[eng].append(("wait", k, v))

    def _issue(self, eng, key, inc, fn, reads, writes):
        for b in reads:
            self._dep(eng, b.w)
        for b in writes:
            self._dep(eng, b.w)
            for k, v in b.r.items():
                self._dep(eng, (k, v))
        self.cnt[key] = self.cnt.get(key, 0) + inc
        tok = (key, self.cnt[key])
        self.lists[eng].append(("op", fn, key, inc))
        for b in reads:
            if b.r.get(key, 0) < tok[1]:
                b.r[key] = tok[1]
        for b in writes:
            b.w = tok
            b.r = {}
        return tok

    def op(self, eng, fn, reads=(), writes=()):
        return self._issue(eng, eng, 1, fn, reads, writes)

    def dma(self, eng, ch, fn, reads=(), writes=()):
        return self._issue(eng, "d_" + ch, 16, fn, reads, writes)

    def cc(self, fn, reads=(), writes=()):
        return self._issue("pool", "cc", 1, fn, reads, writes)

    def mm(self, fns, reads=(), writes=()):
        for b in reads:
            self._dep("pe", b.w)
        for b in writes:
            self._dep("pe", b.w)
            for k, v in b.r.items():
                self._dep("pe", (k, v))
        for fn in fns[:-1]:
            self.lists["pe"].append(("op", fn, None, 0))
        self.cnt["pe"] = self.cnt.get("pe", 0) + 1
        tok = ("pe", self.cnt["pe"])
        self.lists["pe"].append(("op", fns[-1], "pe", 1))
        for b in reads:
            if b.r.get("pe", 0) < tok[1]:
                b.r["pe"] = tok[1]
        for b in writes:
            b.w = tok
            b.r = {}
        return tok

    def finish(self, eng, bufs):
        for b in bufs:
            self._dep(eng, b.w)
        self.finish_all(eng)

    def barrier(self, exclude=()):
        for e in self.ENG:
            for k, v in list(self.cnt.items()):
                if k not in exclude:
                    self._dep(e, (k, v))

    def wait_dma_all(self, eng):
        for k, v in list(self.cnt.items()):
            if k.startswith("d_"):
                self._dep(eng, (k, v))

    def finish_all(self, eng):
        for k, v in list(self.cnt.items()):
            if k.startswith("d_"):
                self._dep(eng, (k, v))

    def emit(self):
        nc = self.nc
        keys = sorted(self.cnt.keys())
        with ExitStack() as st:
            sems = {k: st.enter_context(nc.semaphore("s_" + k)) for k in keys}
            block = st.enter_context(nc.Block())

            def run(e, lst):
                for it in lst:
                    if it[0] == "wait":
                        e.wait_ge(sems[it[1]], it[2])
                    else:
                        ins = it[1](e)
                        if it[3]:
                            if it[2].startswith("cc"):
                                ins.then_inc(sems[it[2]])
                            else:
                                ins.then_inc(sems[it[2]], it[3])

            lists = self.lists

            @block.tensor
            def _(e):
                run(e, lists["pe"])

            @block.scalar
            def _(e):
                run(e, lists["act"])

            @block.vector
            def _(e):
                run(e, lists["dve"])

            @block.gpsimd
            def _(e):
                run(e, lists["pool"])

            @block.sync
            def _(e):
                run(e, lists["sp"])


class Ctx:
    N = [0]

    def __init__(self, nc, st, p):
        self.nc, self.st, self.p = nc, st, p

    def sb(self, shape, dt, name=None):
        Ctx.N[0] += 1
        return self.st.enter_context(self.nc.sbuf_tensor(name or f"sb{Ctx.N[0]}", list(shape), dt))

    def ps(self, shape=(128, 512), dt=F32, name=None):
        Ctx.N[0] += 1
        return self.st.enter_context(self.nc.psum_tensor(name or f"ps{Ctx.N[0]}", list(shape), dt))

    def dram_in(self, name, shape, dt=F32):
        return self.nc.dram_tensor(name, list(shape), dt, kind="ExternalInput").ap()

    def dram_out(self, name, shape, dt=F32):
        return self.nc.dram_tensor(name, list(shape), dt, kind="ExternalOutput").ap()


def rmsnorm_fm(p, c, x_sb, xb, col0, n, g_sb, gb, ones_bf, onesb, hT, hb, sq, sqb, ps, psb, rstd, rstdb, xbs=None):
    if xbs is None:
        xbs = [xb] * 8
    for k in range(8):
        p.op("act", lambda e, k=k: e.activation(out=sq[:, k, :n], in_=x_sb[:, k, col0:col0 + n], func=AF.Square),
             reads=[xbs[k]], writes=[sqb[k]])
    p.mm([lambda e, k=k: e.matmul(ps[:, :n], lhsT=ones_bf[:, :], rhs=sq[:, k, :n], start=(k == 0), stop=(k == 7))
          for k in range(8)], reads=[onesb] + list(sqb), writes=[psb])
    p.op("act", lambda e: e.activation(out=rstd[:, :n], in_=ps[:, :n], func=AF.Ln, bias=EPS), reads=[psb], writes=[rstdb])
    p.op("act", lambda e: e.activation(out=rstd[:, :n], in_=rstd[:, :n], func=AF.Exp, scale=-0.5), reads=[rstdb], writes=[rstdb])
    for k in range(8):
        p.op("dve", lambda e, k=k: e.scalar_tensor_tensor(out=hT[:, k, :n], in0=x_sb[:, k, col0:col0 + n],
                                                          scalar=g_sb[:, k:k + 1], in1=rstd[:, :n],
                                                          op0=ALU.mult, op1=ALU.mult),
             reads=[xbs[k], gb, rstdb], writes=[hb[k]])


NFM = 2048
NTM = 776


def build_A():
    nc = bass.Bass("TRN2", target_bir_lowering=False)
    p = Prog(nc)
    with ExitStack() as st:
        c = Ctx(nc, st, p)
        xT = c.dram_in("xT", [D, TQ])
        wfm = c.dram_in("wfm", [D, NFM])
        wtm = c.dram_in("wtm", [D, NTM])
        g1 = c.dram_in("g1", [128, 8])
        zfm = c.dram_out("zfm", [NFM, TQ])
        ztm = c.dram_out("ztm", [TQ, NTM])

        x_sb = c.sb([128, 8, TQ], F32); xb = p.buf()
        wfm_sb = c.sb([128, 8, NFM], BF16); wfmb = p.buf()
        wtm_sb = c.sb([128, 8, NTM], BF16); wtmb = p.buf()
        g_sb = c.sb([128, 8], F32); gb = p.buf()
        ones_bf = c.sb([128, 128], BF16); onesb = p.buf()
        sq = c.sb([128, 8, 512], BF16); sqb = p.bufs(8)
        hT = c.sb([128, 8, 512], BF16); hb = p.bufs(8)
        rstd = c.sb([128, 512], F32); rstdb = p.buf()
        NST = 4
        stg = [c.sb([128, 512], F32) for _ in range(NST)]; stgb = p.bufs(NST)
        psn = c.ps(); psnb = p.buf()
        NPS = 4
        pss = [c.ps() for _ in range(NPS)]; pssb = p.bufs(NPS)
        zfmb = p.buf(); ztmb = p.buf()

        p.dma("sp", "x", lambda e: e.dma_start(out=x_sb[:, :, :], in_=xT.rearrange("(k q) t -> q k t", q=128)), writes=[xb])
        p.dma("sp", "g", lambda e: e.dma_start(out=g_sb[:, :], in_=g1), writes=[gb])
        for k in range(8):
            p.dma("pool", "wfm", lambda e, k=k: e.dma_start(out=wfm_sb[:, k, :], in_=wfm[k * 128:(k + 1) * 128, :]), writes=[wfmb])
        for k in range(8):
            p.dma("pool", "wtm", lambda e, k=k: e.dma_start(out=wtm_sb[:, k, :], in_=wtm[k * 128:(k + 1) * 128, :]), writes=[wtmb])
        p.op("dve", lambda e: e.memset(ones_bf[:, :], 1.0 / D), writes=[onesb])

        i_ps = 0
        i_st = 0
        for tt in range(TQ // 512):
            col0 = tt * 512
            rmsnorm_fm(p, c, x_sb, xb, col0, 512, g_sb, gb, ones_bf, onesb, hT, hb, sq, sqb, psn, psnb, rstd, rstdb)
            for oc in range(NFM // 128):
                ps, psb = pss[i_ps % NPS], pssb[i_ps % NPS]; i_ps += 1
                p.mm([lambda e, k=k, oc=oc, ps=ps: e.matmul(ps[:, :], lhsT=wfm_sb[:, k, oc * 128:(oc + 1) * 128], rhs=hT[:, k, :],
                                                             start=(k == 0), stop=(k == 7)) for k in range(8)],
                     reads=[wfmb] + list(hb), writes=[psb])
                sg, sgb = stg[i_st % NST], stgb[i_st % NST]; i_st += 1
                p.op("act", lambda e, ps=ps, sg=sg: e.copy(out=sg[:, :], in_=ps[:, :]), reads=[psb], writes=[sgb])
                p.dma("sp", "o%d" % (i_st % NST), lambda e, sg=sg, oc=oc, col0=col0: e.dma_start(
                    out=zfm[oc * 128:(oc + 1) * 128, col0:col0 + 512], in_=sg[:, :]), reads=[sgb], writes=[zfmb])
            for s in range(4):
                for (c0, n) in ((0, 512), (512, NTM - 512)):
                    ps, psb = pss[i_ps % NPS], pssb[i_ps % NPS]; i_ps += 1
                    p.mm([lambda e, k=k, s=s, c0=c0, n=n, ps=ps: e.matmul(ps[:, :n], lhsT=hT[:, k, s * 128:(s + 1) * 128],
                                                                         rhs=wtm_sb[:, k, c0:c0 + n], start=(k == 0), stop=(k == 7))
                          for k in range(8)], reads=[wtmb] + list(hb), writes=[psb])
                    sg, sgb = stg[i_st % NST], stgb[i_st % NST]; i_st += 1
                    p.op("act", lambda e, ps=ps, sg=sg, n=n: e.copy(out=sg[:, :n], in_=ps[:, :n]), reads=[psb], writes=[sgb])
                    r0 = col0 + s * 128
                    p.dma("sp", "o%d" % (i_st % NST), lambda e, sg=sg, r0=r0, c0=c0, n=n: e.dma_start(
                        out=ztm[r0:r0 + 128, c0:c0 + n], in_=sg[:, :n]), reads=[sgb], writes=[ztmb])
        p.finish("sp", [zfmb, ztmb])
        p.emit()
    return nc


def build_moba(nc, p, prm_sb, prmb, ident_bf, identb, ones_bf, onesbb, aqT, akT, av, caus_d, ind_d, mixT, outb):
    import os
    STAGE = int(os.environ.get('MOBA_STAGE', '9'))
    NP = T // 512
    with ExitStack() as st:
        c = Ctx(nc, st, p)
        Kaug = [c.sb([128, T], BF16, name="Kaug%d" % _) for _ in range(2)]; Kb = [p.bufs(NP, "K%d" % h) for h in range(2)]
        Qaug = [c.sb([128, T], BF16, name="Qaug%d" % _) for _ in range(2)]; Qb = [p.bufs(NP, "Q%d" % h) for h in range(2)]
        vaug = c.sb([128, T // 128, 128], BF16, name="vaug"); vb = p.buf()
        caus = c.sb([128, 2048], BF16, name="caus_sb"); causb = p.buf()
        kms = c.sb([128, 64], F32, name="kms_sb"); kmsb = p.buf()
        AUX = ((64, 96), (0, 32))
        DAT = ((0, 64), (64, 128))
        for h in range(2):
            p.op("dve", lambda e, h=h: e.memset(Kaug[h][:, :], 0.0), writes=Kb[h])
            p.op("pool", lambda e, h=h: e.memset(Qaug[h][:, :], 0.0), writes=Qb[h])
        for h in range(2):
            a0, a1 = AUX[h]
            for q in range(T // 2048):
                p.dma("pool", "ind%d" % h, lambda e, h=h, a0=a0, a1=a1, q=q: e.dma_start(out=Kaug[h][a0:a1, q * 2048:(q + 1) * 2048],
                                                                               in_=ind_d[:, q * 2048:(q + 1) * 2048]), writes=Kb[h])
        p.dma("pool", "caus", lambda e: e.dma_start(out=caus[:, :], in_=caus_d), writes=[causb])
        for q in range(T // 2048):
            p.dma("pool", "vaug", lambda e, q=q: e.dma_start(out=vaug[:, q * 16:(q + 1) * 16, :],
                                                           in_=av.rearrange("(c q) e -> q c e", q=128)[:, q * 16:(q + 1) * 16, :]), writes=[vb])
        p.op("dve", lambda e: e.memset(kms[:, :], 0.0), writes=[kmsb])
        with ExitStack() as st2:
            c2 = Ctx(nc, st2, p)
            blk = c2.sb([128, 128], BF16); blkb = p.buf()
            p.op("dve", lambda e: e.memset(blk[:, :], 0.0), writes=[blkb])
            p.op("dve", lambda e: e.memset(blk[0:64, 0:64], 1.0), writes=[blkb])
            p.op("dve", lambda e: e.memset(blk[64:128, 64:128], 1.0), writes=[blkb])
            xin = [c2.sb([128, 512], F32) for _ in range(2)]; xinb = p.bufs(2)
            sq = c2.sb([128, 512], BF16); sqb = p.buf()
            rstd = c2.sb([128, 512], F32); rstdb = p.buf()
            xn = [c2.sb([128, 512], F32) for _ in range(2)]; xnb = p.bufs(2)
            gsb = [c2.sb([128, 64], F32) for _ in range(2)]; gsbb = p.bufs(2)
            top8 = [c2.sb([128, 16], F32) for _ in range(2)]; top8b = p.bufs(2)
            nm = [c2.sb([128, 4, 128], BF16) for _ in range(2)]; nmb = p.bufs(2)
            psM = c2.ps(); psMb = p.buf()
            psGt = [c2.ps() for _ in range(2)]; psGtb = p.bufs(2)
            psT = [c2.ps() for _ in range(2)]; psTb = p.bufs(2)
            for q in range(2):
                p.op("pool", lambda e, q=q: e.memset(nm[q][:, :, :], 0.0), writes=[nmb[q]])
            n_in = 0
            n_g = 0
            for pc in range(NP if STAGE >= 1 else 0):
                t0 = pc * 512
                for which in range(2):
                    src = akT if which == 0 else aqT
                    gcol = 17 if which == 0 else 16
                    X, Xb = xin[n_in % 2], xinb[n_in % 2]
                    p.dma("sp", "ax%d" % (n_in % 2), lambda e, X=X, src=src, t0=t0: e.dma_start(out=X[:, :], in_=src[:, t0:t0 + 512]), writes=[Xb])
                    n_in += 1
                    p.op("act", lambda e, X=X: e.activation(out=sq[:, :], in_=X[:, :], func=AF.Square), reads=[Xb], writes=[sqb])
                    p.mm([lambda e: e.matmul(psM[:, :], lhsT=blk[:, :], rhs=sq[:, :], start=True, stop=True)], reads=[blkb, sqb], writes=[psMb])
                    p.op("act", lambda e: e.activation(out=rstd[:, :], in_=psM[:, :], func=AF.Ln, bias=EPS, scale=1.0 / 64), reads=[psMb], writes=[rstdb])
                    p.op("act", lambda e: e.activation(out=rstd[:, :], in_=rstd[:, :], func=AF.Exp, scale=-0.5), reads=[rstdb], writes=[rstdb])
                    N_, Nb_ = xn[which], xnb[which]
                    p.op("dve", lambda e, X=X, N_=N_, gcol=gcol: e.scalar_tensor_tensor(out=N_[:, :], in0=X[:, :], scalar=prm_sb[:, gcol:gcol + 1], in1=rstd[:, :],
                                                                                         op0=ALU.mult, op1=ALU.mult), reads=[Xb, prmb, rstdb], writes=[Nb_])
                    dst = Kaug if which == 0 else Qaug
                    dstb = Kb if which == 0 else Qb
                    for h in range(2):
                        d0, d1 = DAT[h]
                        p.op("act", lambda e, h=h, d0=d0, d1=d1, dst=dst, N_=N_, t0=t0: e.copy(out=dst[h][d0:d1, t0:t0 + 512], in_=N_[d0:d1, :]),
                             reads=[Nb_], writes=[dstb[h][pc]])
                    if which == 0:
                        for h in range(2):
                            p.op("dve", lambda e, N_=N_, pc=pc, h=h: e.tensor_reduce(
                                out=kms[h * 64:(h + 1) * 64, h * 32 + 2 * pc:h * 32 + 2 * pc + 2],
                                in_=N_[h * 64:(h + 1) * 64, :].rearrange("q (b k) -> q b k", k=256), axis=AX.X, op=ALU.add), reads=[Nb_], writes=[kmsb])
                if STAGE < 2:
                    continue
                QN, QNb = xn[1], xnb[1]
                nq = pc % 2
                for qb in range(4):
                    own = 2 * pc + qb // 2
                    gq = n_g % 2; n_g += 1
                    p.mm([lambda e, gq=gq, qb=qb: e.matmul(psGt[gq][:, 0:64], lhsT=QN[:, qb * 128:(qb + 1) * 128], rhs=kms[:, :], start=True, stop=True)],
                         reads=[QNb, kmsb], writes=[psGtb[gq]])
                    p.op("dve", lambda e, gq=gq: e.memset(gsb[gq][:, :], -1e30), writes=[gsbb[gq]])
                    if own > 0:
                        p.op("dve", lambda e, gq=gq, own=own: e.tensor_copy(out=gsb[gq].rearrange("q (h n) -> q h n", h=2)[:, :, 0:own],
                                                                            in_=psGt[gq][:, 0:64].rearrange("q (h n) -> q h n", h=2)[:, :, 0:own]),
                             reads=[psGtb[gq]], writes=[gsbb[gq]])
                    for h in range(2):
                        p.op("dve", lambda e, gq=gq, h=h: e.max(out=top8[gq][:, h * 8:(h + 1) * 8], in_=gsb[gq][:, h * 32:(h + 1) * 32]),
                             reads=[gsbb[gq]], writes=[top8b[gq]])
                        c0 = 64 if h == 0 else 96
                        p.op("dve", lambda e, gq=gq, h=h: e.tensor_scalar(out=gsb[gq][:, h * 32:(h + 1) * 32], in0=gsb[gq][:, h * 32:(h + 1) * 32],
                                                                          scalar1=top8[gq][:, h * 8 + 2:h * 8 + 3], scalar2=None, op0=ALU.is_ge),
                             reads=[gsbb[gq], top8b[gq]], writes=[gsbb[gq]])
                        p.op("dve", lambda e, gq=gq, h=h, c0=c0, qb=qb, nq=nq: e.tensor_scalar(out=nm[nq][:, qb, c0:c0 + 32], in0=gsb[gq][:, h * 32:(h + 1) * 32],
                                                                                       scalar1=-1.0, scalar2=-NEG, op0=ALU.add, op1=ALU.mult),
                             reads=[gsbb[gq]], writes=[nmb[nq]])
                        p.op("dve", lambda e, h=h, c0=c0, qb=qb, nq=nq, own=own: e.memset(nm[nq][:, qb, c0 + own:c0 + own + 1], 0.0), writes=[nmb[nq]])
                        if own < 31:
                            p.op("dve", lambda e, h=h, c0=c0, qb=qb, nq=nq, own=own: e.memset(nm[nq][:, qb, c0 + own + 1:c0 + 32], NEG), writes=[nmb[nq]])
                if STAGE < 3:
                    continue
                tq = pc % 2
                p.mm([lambda e, tq=tq, qb=qb, nq=nq: e.matmul(psT[tq][0:96, qb * 128:(qb + 1) * 128], lhsT=nm[nq][:, qb, 0:96], rhs=ident_bf[:, :], start=True, stop=True)
                      for qb in range(4)], reads=[nmb[nq], identb], writes=[psTb[tq]])
                p.op("act", lambda e, tq=tq, t0=t0: e.copy(out=Qaug[0][64:96, t0:t0 + 512], in_=psT[tq][64:96, :]), reads=[psTb[tq]], writes=[Qb[0][pc]])
                p.mm([lambda e, tq=tq, qb=qb, nq=nq: e.matmul(psT[tq][0:32, qb * 128:(qb + 1) * 128], lhsT=nm[nq][:, qb, 96:128], rhs=ident_bf[:, :], start=True, stop=True)
                      for qb in range(4)], reads=[nmb[nq], identb], writes=[psTb[tq]])
                p.op("act", lambda e, tq=tq, t0=t0: e.copy(out=Qaug[1][0:32, t0:t0 + 512], in_=psT[tq][0:32, :]), reads=[psTb[tq]], writes=[Qb[1][pc]])
        p.barrier()
        with ExitStack() as st3:
            c3 = Ctx(nc, st3, p)
            NPT = 3
            PT = [c3.sb([128, 512], BF16) for _ in range(NPT)]; PTb = p.bufs(NPT)
            psS = [c3.ps() for _ in range(2)]; psSb = p.bufs(2)
            psO = [c3.ps() for _ in range(2)]; psOb = p.bufs(2)
            psR = [c3.ps() for _ in range(2)]; psRb = p.bufs(2)
            rr = [c3.sb([64, 512], F32) for _ in range(2)]; rrb = p.bufs(2)
            ao = [c3.sb([64, 512], F32) for _ in range(2)]; aob = p.bufs(2)
            n_s = 0
            n_o = 0
            for pc in range(NP if STAGE >= 4 else 0):
                t0 = pc * 512
                nkt = 4 * (pc + 1)
                for h in range(2):
                    rows = 96 if h == 0 else 128
                    oq = n_o % 2; n_o += 1
                    for kt in range(nkt):
                        sq_ = n_s % 2
                        pq = n_s % NPT
                        n_s += 1
                        p.mm([lambda e, sq_=sq_, h=h, rows=rows, kt=kt, t0=t0: e.matmul(psS[sq_][:, :], lhsT=Kaug[h][0:rows, kt * 128:(kt + 1) * 128],
                                                                                         rhs=Qaug[h][0:rows, t0:t0 + 512], start=True, stop=True)],
                             reads=[Kb[h][kt // 4], Qb[h][pc]], writes=[psSb[sq_]])
                        p.op("act", lambda e, sq_=sq_, pq=pq: e.activation(out=PT[pq][:, :], in_=psS[sq_][:, :], func=AF.Exp, scale=0.125),
                             reads=[psSb[sq_]], writes=[PTb[pq]])
                        if kt >= 4 * pc:
                            r = kt - 4 * pc
                            p.op("dve", lambda e, pq=pq, r=r: e.tensor_tensor(out=PT[pq][:, :], in0=PT[pq][:, :], in1=caus[:, r * 512:(r + 1) * 512], op=ALU.mult),
                                 reads=[PTb[pq], causb], writes=[PTb[pq]])
                        p.mm([lambda e, oq=oq, h=h, kt=kt, pq=pq, nkt=nkt: e.matmul(psO[oq][0:64, :], lhsT=vaug[:, kt, h * 64:(h + 1) * 64], rhs=PT[pq][:, :],
                                                                                     start=(kt == 0), stop=(kt == nkt - 1))],
                             reads=[vb, PTb[pq]], writes=[psOb[oq]])
                        p.mm([lambda e, oq=oq, kt=kt, pq=pq, nkt=nkt: e.matmul(psR[oq][0:64, :], lhsT=ones_bf[:, 0:64], rhs=PT[pq][:, :],
                                                                                start=(kt == 0), stop=(kt == nkt - 1))],
                             reads=[onesbb, PTb[pq]], writes=[psRb[oq]])
                    p.op("dve", lambda e, oq=oq: e.reciprocal(out=rr[oq][:, :], in_=psR[oq][0:64, :]), reads=[psRb[oq]], writes=[rrb[oq]])
                    p.op("dve", lambda e, oq=oq: e.tensor_tensor(out=ao[oq][:, :], in0=psO[oq][0:64, :], in1=rr[oq][:, :], op=ALU.mult),
                         reads=[psOb[oq], rrb[oq]], writes=[aob[oq]])
                    p.dma("sp", "ao%d" % oq, lambda e, oq=oq, h=h, t0=t0: e.dma_start(out=mixT[128 + h * 64:192 + h * 64, t0:t0 + 512], in_=ao[oq][:, :]),
                          reads=[aob[oq]], writes=[outb])


def w_in_cols():
    fm = list(range(0, 256)) + list(range(256, 768)) + list(range(1024, 1280)) + list(range(1288, 1288 + 1024))
    tm = list(range(768, 1024)) + list(range(1288 + 1024, 1288 + 1536)) + list(range(1280, 1288))
    return np.array(fm), np.array(tm)


def split_w_in(w_in):
    fm, tm = w_in_cols()
    return np.ascontiguousarray(w_in[:, fm]), np.ascontiguousarray(w_in[:, tm])


TQH = TQ + 2
C_TILES = [(0, 2)] + [(2 + 512 * i, 512) for i in range(4)]


def build_C():
    nc = bass.Bass("TRN2", target_bir_lowering=False)
    p = Prog(nc)
    with ExitStack() as st:
        c = Ctx(nc, st, p)
        xT = c.dram_in("xT", [D, TQH])
        mixT = c.dram_in("mixT", [D, TQH])
        w_out = c.dram_in("w_out", [D, D])
        g2 = c.dram_in("g2", [128, 8])
        w_up = c.dram_in("w_up", [D, 2 * DFF])
        convw = c.dram_in("convw", [128, 44, 3])
        w_down = c.dram_in("w_down", [DFF, D])
        xoT = c.dram_out("xoT", [D, TQ])

        x_sb = c.sb([128, 8, TQH], F32); xb = p.bufs(8, "x")
        wo_sb = c.sb([128, 8, D], BF16); wob = p.buf()
        wd_sb = c.sb([128, 22, D], BF16); wdb = p.buf()
        g_sb = c.sb([128, 8], F32); gb = p.buf()
        cw_sb = c.sb([128, 44, 3], F32); cwb = p.buf()
        ones_bf = c.sb([128, 128], BF16); onesb = p.buf()
        mixb = [c.sb([128, 8, 512], BF16) for _ in range(2)]; mixbb = p.bufs(2)
        wu = [c.sb([128, 8, 256], BF16) for _ in range(2)]; wub = p.bufs(2)
        act = c.sb([128, 22, 512], BF16); actb = p.bufs(22, "act")
        hT = c.sb([128, 8, 512], BF16); hb = p.bufs(8, "h")
        rstd = c.sb([128, 512], F32); rstdb = p.buf()
        carry = c.sb([128, 44, 2], F32); carryb = p.bufs(44, "cy")
        ug = [c.sb([128, 514], F32) for _ in range(2)]; ugb = p.bufs(2)
        uv = [c.sb([128, 514], F32) for _ in range(2)]; uvb = p.bufs(2)
        tg = [c.sb([128, 512], F32) for _ in range(2)]; tgb = p.bufs(2)
        tv = [c.sb([128, 512], F32) for _ in range(2)]; tvb = p.bufs(2)
        psn = c.ps(); psnb = p.buf()
        psg = [c.ps() for _ in range(2)]; psgb = p.bufs(2)
        psv = [c.ps() for _ in range(2)]; psvb = p.bufs(2)
        pso = [c.ps() for _ in range(2)]; psob = p.bufs(2)
        outb = p.buf()

        for k in range(8):
            p.dma("sp", "x%d" % k, lambda e, k=k: e.dma_start(out=x_sb[:, k, :], in_=xT[k * 128:(k + 1) * 128, :]), writes=[xb[k]])
        p.dma("sp", "g", lambda e: e.dma_start(out=g_sb[:, :], in_=g2), writes=[gb])
        p.dma("sp", "cw", lambda e: e.dma_start(out=cw_sb[:, :, :], in_=convw), writes=[cwb])
        for k in range(8):
            p.dma("pool", "wo", lambda e, k=k: e.dma_start(out=wo_sb[:, k, :], in_=w_out[k * 128:(k + 1) * 128, :]), writes=[wob])
        p.op("dve", lambda e: e.memset(ones_bf[:, :], 1.0 / D), writes=[onesb])
        p.op("dve", lambda e: e.memset(carry[:, :, :], 0.0), writes=list(carryb))
        wd_loaded = False

        i_o = 0
        i_u = 0
        for ti, (col0, n) in enumerate(C_TILES):
            mb, mbb = mixb[ti % 2], mixbb[ti % 2]
            p.dma("pool", "mix%d" % (ti % 2), lambda e, mb=mb, col0=col0, n=n: e.dma_start(
                out=mb[:, :, :n], in_=mixT.rearrange("(k q) t -> q k t", q=128)[:, :, col0:col0 + n]), writes=[mbb])
            for oc in range(8):
                ps, psb = pso[i_o % 2], psob[i_o % 2]; i_o += 1
                p.mm([lambda e, k=k, oc=oc, ps=ps, mb=mb, n=n: e.matmul(ps[:, :n], lhsT=wo_sb[:, k, oc * 128:(oc + 1) * 128], rhs=mb[:, k, :n],
                                                                       start=(k == 0), stop=(k == 7)) for k in range(8)],
                     reads=[wob, mbb], writes=[psb])
                p.op("dve", lambda e, oc=oc, ps=ps, col0=col0, n=n: e.tensor_tensor(
                    out=x_sb[:, oc, col0:col0 + n], in0=x_sb[:, oc, col0:col0 + n], in1=ps[:, :n], op=ALU.add),
                    reads=[psb, xb[oc]], writes=[xb[oc]])
            rmsnorm_fm(p, c, x_sb, None, col0, n, g_sb, gb, ones_bf, onesb, hT, hb, act, actb[:8], psn, psnb, rstd, rstdb, xbs=xb)
            if not wd_loaded:
                for fc in range(22):
                    p.dma("pool", "wd", lambda e, fc=fc: e.dma_start(out=wd_sb[:, fc, :], in_=w_down[fc * 128:(fc + 1) * 128, :]), writes=[wdb])
                wd_loaded = True
            for fc in range(22):
                q = i_u % 2; i_u += 1
                p.dma("pool", "wu%d" % q, lambda e, q=q, fc=fc: e.dma_start(
                    out=wu[q][:, :, 0:128], in_=w_up.rearrange("(k q) c -> q k c", q=128)[:, :, fc * 128:(fc + 1) * 128]), writes=[wub[q]])
                p.dma("pool", "wu%d" % q, lambda e, q=q, fc=fc: e.dma_start(
                    out=wu[q][:, :, 128:256], in_=w_up.rearrange("(k q) c -> q k c", q=128)[:, :, DFF + fc * 128:DFF + (fc + 1) * 128]), writes=[wub[q]])
                p.mm([lambda e, k=k, q=q, n=n: e.matmul(psg[q][:, :n], lhsT=wu[q][:, k, 0:128], rhs=hT[:, k, :n], start=(k == 0), stop=(k == 7))
                      for k in range(8)], reads=[wub[q]] + list(hb), writes=[psgb[q]])
                p.mm([lambda e, k=k, q=q, n=n: e.matmul(psv[q][:, :n], lhsT=wu[q][:, k, 128:256], rhs=hT[:, k, :n], start=(k == 0), stop=(k == 7))
                      for k in range(8)], reads=[wub[q]] + list(hb), writes=[psvb[q]])
                for (ps_, psb_, u_, ub_, t_, tb_, ch, eng) in ((psg[q], psgb[q], ug[q], ugb[q], tg[q], tgb[q], fc, "dve"),
                                                              (psv[q], psvb[q], uv[q], uvb[q], tv[q], tvb[q], 22 + fc, "dve")):
                    p.op("act", lambda e, u_=u_, ch=ch: e.copy(out=u_[:, 0:2], in_=carry[:, ch, :]), reads=[carryb[ch]], writes=[ub_])
                    p.op("act", lambda e, u_=u_, ps_=ps_, n=n: e.copy(out=u_[:, 2:2 + n], in_=ps_[:, :n]), reads=[psb_], writes=[ub_])
                    p.op("act", lambda e, u_=u_, ch=ch, n=n: e.copy(out=carry[:, ch, :], in_=u_[:, n:n + 2]), reads=[ub_], writes=[carryb[ch]])
                    p.op(eng, lambda e, u_=u_, t_=t_, ch=ch, n=n: e.tensor_scalar(
                        out=t_[:, :n], in0=u_[:, 0:n], scalar1=cw_sb[:, ch, 0:1], scalar2=None, op0=ALU.mult), reads=[ub_, cwb], writes=[tb_])
                    for j in (1, 2):
                        p.op(eng, lambda e, u_=u_, t_=t_, ch=ch, n=n, j=j: e.scalar_tensor_tensor(
                            out=t_[:, :n], in0=u_[:, j:j + n], scalar=cw_sb[:, ch, j:j + 1], in1=t_[:, :n], op0=ALU.mult, op1=ALU.add),
                            reads=[ub_, cwb, tb_], writes=[tb_])
                p.op("act", lambda e, q=q, n=n: e.activation(out=tg[q][:, :n], in_=tg[q][:, :n], func=AF.Silu), reads=[tgb[q]], writes=[tgb[q]])
                p.op("dve", lambda e, q=q, fc=fc, n=n: e.tensor_tensor(out=act[:, fc, :n], in0=tg[q][:, :n], in1=tv[q][:, :n], op=ALU.mult),
                     reads=[tgb[q], tvb[q]], writes=[actb[fc]])
            for oc in range(8):
                ps, psb = pso[i_o % 2], psob[i_o % 2]; i_o += 1
                p.mm([lambda e, fc=fc, oc=oc, ps=ps, n=n: e.matmul(ps[:, :n], lhsT=wd_sb[:, fc, oc * 128:(oc + 1) * 128], rhs=act[:, fc, :n],
                                                                   start=(fc == 0), stop=(fc == 21)) for fc in range(22)],
                     reads=[wdb] + list(actb), writes=[psb])
                p.op("dve", lambda e, oc=oc, ps=ps, col0=col0, n=n: e.tensor_tensor(
                    out=x_sb[:, oc, col0:col0 + n], in0=x_sb[:, oc, col0:col0 + n], in1=ps[:, :n], op=ALU.add),
                    reads=[psb, xb[oc]], writes=[xb[oc]])
            if col0 >= 2:
                for k in range(8):
                    p.dma("sp", "out", lambda e, k=k, col0=col0, n=n: e.dma_start(
                        out=xoT[k * 128:(k + 1) * 128, col0 - 2:col0 - 2 + n], in_=x_sb[:, k, col0:col0 + n]), reads=[xb[k]], writes=[outb])
        p.finish("sp", [outb])
        p.emit()
    return nc


def halo_T(a, b, j):
    out = np.zeros((a.shape[2], TQH), np.float32)
    out[:, 2:] = a[b, j * TQ:(j + 1) * TQ, :].T
    if j > 0:
        out[:, :2] = a[b, j * TQ - 2:j * TQ, :].T
    return out


def host_C_inputs(x, mix, w_out, g2, w_up, ffn_conv, w_down):
    g2l = np.ascontiguousarray(g2.reshape(8, 128).T)
    cw = np.ascontiguousarray(ffn_conv.reshape(3, 44, 128).transpose(2, 1, 0))
    maps = []
    for c in range(NCORE):
        b, j = c // 4, c % 4
        maps.append({"xT": halo_T(x, b, j), "mixT": halo_T(mix, b, j), "w_out": w_out, "g2": g2l,
                     "w_up": w_up, "convw": cw, "w_down": w_down})
    return maps


NPRM = 40
POOL_WINDOWS = (2, 4, 8, 16)


def build_B(do_pool=True, do_mlstm=True, do_moba=True):
    nc = bass.Bass("TRN2", target_bir_lowering=False)
    p = Prog(nc)
    with ExitStack() as st0:
        c0 = Ctx(nc, st0, p)
        pT = c0.dram_in("pT", [64, T])
        mqT = c0.dram_in("mqT", [64, T])
        mkT = c0.dram_in("mkT", [64, T])
        moT = c0.dram_in("moT", [64, T])
        aqT = c0.dram_in("aqT", [128, T])
        akT = c0.dram_in("akT", [128, T])
        mv = c0.dram_in("mv", [T, 64])
        av = c0.dram_in("av", [T, 128])
        gates = c0.dram_in("gates", [T, 2])
        prm = c0.dram_in("prm", [128, NPRM])
        poolw = c0.dram_in("poolw", [64, 64])
        triU_d = c0.dram_in("triU", [128, 128])
        ident_d = c0.dram_in("ident", [128, 128])
        caus_d = c0.dram_in("caus", [128, 2048])
        ind_d = c0.dram_in("ind", [32, T])
        mixT = c0.dram_out("mixT", [256, T])
        outb = p.buf()

        prm_sb = c0.sb([128, NPRM], F32); prmb = p.buf()
        triU = c0.sb([128, 128], F32); triUb = p.buf()
        ident_bf = c0.sb([128, 128], BF16); identb = p.buf()
        ones_f = c0.sb([128, 128], F32); onesfb = p.buf()
        ones_bf = c0.sb([128, 128], BF16); onesbb = p.buf()
        p.dma("sp", "prm", lambda e: e.dma_start(out=prm_sb[:, :], in_=prm), writes=[prmb])
        p.dma("sp", "triU", lambda e: e.dma_start(out=triU[:, :], in_=triU_d), writes=[triUb])
        p.dma("pool", "ident", lambda e: e.dma_start(out=ident_bf[:, :], in_=ident_d), writes=[identb])
        p.op("dve", lambda e: e.memset(ones_f[:, :], 1.0), writes=[onesfb])
        p.op("dve", lambda e: e.memset(ones_bf[:, :], 1.0), writes=[onesbb])

        if do_pool:
            with ExitStack() as st:
                c = Ctx(nc, st, p)
                PW = 2048
                xa = [c.sb([64, 16 + PW], F32) for _ in range(2)]; xab = p.bufs(2)
                s_a = c.sb([64, 16 + PW], F32); sab = p.buf()
                s_b = c.sb([64, 16 + PW], F32); sbb = p.buf()
                acc = c.sb([64, 16 + PW], F32); accb = p.buf()
                d_bf = c.sb([64, PW], BF16); dbb = p.buf()
                wp_bf = c.sb([64, 64], BF16); wpb = p.buf()
                stg = [c.sb([64, 512], F32) for _ in range(2)]; stgb = p.bufs(2)
                pps = [c.ps() for _ in range(2)]; ppsb = p.bufs(2)
                p.dma("pool", "wp", lambda e: e.dma_start(out=wp_bf[:, :], in_=poolw), writes=[wpb])
                p.op("dve", lambda e: e.memset(xa[0][:, 0:16], 0.0), writes=[xab[0]])
                i_s = 0
                for pc in range(T // PW):
                    X, Xb = xa[pc % 2], xab[pc % 2]
                    if pc > 0:
                        Xp, Xpb = xa[(pc - 1) % 2], xab[(pc - 1) % 2]
                        p.op("act", lambda e, X=X, Xp=Xp: e.copy(out=X[:, 0:16], in_=Xp[:, PW:PW + 16]), reads=[Xpb], writes=[Xb])
                    p.dma("sp", "px%d" % (pc % 2), lambda e, X=X, pc=pc: e.dma_start(out=X[:, 16:16 + PW], in_=pT[:, pc * PW:(pc + 1) * PW]), writes=[Xb])
                    W = 16 + PW
                    p.op("dve", lambda e, X=X: e.tensor_tensor(out=s_a[:, 1:W], in0=X[:, 1:W], in1=X[:, 0:W - 1], op=ALU.add), reads=[Xb], writes=[sab])
                    p.op("dve", lambda e: e.tensor_scalar(out=acc[:, 16:W], in0=s_a[:, 16:W], scalar1=prm_sb[0:64, 1:2], scalar2=None, op0=ALU.mult),
                         reads=[sab, prmb], writes=[accb])
                    src, srcb, dst, dstb = s_a, sab, s_b, sbb
                    lo = 1
                    for k, sh in ((1, 2), (2, 4), (3, 8)):
                        lo2 = lo + sh
                        p.op("dve", lambda e, src=src, dst=dst, lo2=lo2, sh=sh: e.tensor_tensor(
                            out=dst[:, lo2:W], in0=src[:, lo2:W], in1=src[:, lo2 - sh:W - sh], op=ALU.add), reads=[srcb], writes=[dstb])
                        p.op("dve", lambda e, dst=dst, k=k: e.scalar_tensor_tensor(
                            out=acc[:, 16:W], in0=dst[:, 16:W], scalar=prm_sb[0:64, 1 + k:2 + k], in1=acc[:, 16:W], op0=ALU.mult, op1=ALU.add),
                            reads=[dstb, prmb, accb], writes=[accb])
                        src, srcb, dst, dstb = dst, dstb, src, srcb
                        lo = lo2
                    if pc == 0:
                        p.op("dve", lambda e: e.tensor_tensor(out=acc[:, 16:32], in0=acc[:, 16:32], in1=prm_sb[0:64, 19:35], op=ALU.mult),
                             reads=[accb, prmb], writes=[accb])
                    p.op("dve", lambda e, X=X: e.tensor_tensor(out=d_bf[:, :], in0=acc[:, 16:W], in1=X[:, 16:W], op=ALU.subtract), reads=[accb, Xb], writes=[dbb])
                    for q in range(PW // 512):
                        ps, psb = pps[i_s % 2], ppsb[i_s % 2]
                        sg, sgb = stg[i_s % 2], stgb[i_s % 2]
                        ch = "po%d" % (i_s % 2); i_s += 1
                        p.mm([lambda e, ps=ps, q=q: e.matmul(ps[0:64, :], lhsT=wp_bf[:, :], rhs=d_bf[:, q * 512:(q + 1) * 512], start=True, stop=True)],
                             reads=[wpb, dbb], writes=[psb])
                        p.op("act", lambda e, ps=ps, sg=sg: e.activation(out=sg[:, :], in_=ps[0:64, :], func=AF.Copy, scale=prm_sb[0:64, 0:1]),
                             reads=[psb, prmb], writes=[sgb])
                        t0 = pc * PW + q * 512
                        p.dma("sp", ch, lambda e, sg=sg, t0=t0: e.dma_start(out=mixT[0:64, t0:t0 + 512], in_=sg[:, :]), reads=[sgb], writes=[outb])

        p.barrier()
        if do_mlstm:
            with ExitStack() as st:
                c = Ctx(nc, st, p)
                NCH = T // 128
                qT_bf = c.sb([64, T], BF16); qTb = p.bufs(4, "qT")
                kT_bf = c.sb([64, T], BF16); kTb = p.bufs(4, "kT")
                vones = c.sb([128, NCH, 128], BF16); vonesb = p.buf()
                g_sb = c.sb([128, NCH, 2], F32); gsbb = p.buf()
                iv = c.sb([128, NCH], F32); ivb = p.buf()
                lf = c.sb([128, NCH], F32); lfb_ = p.buf()
                bias_s = c.sb([128, NCH], F32); biasb = p.buf()
                wk = c.sb([128, NCH], F32); wkb = p.buf()
                dec = c.sb([128, NCH], F32); decb = p.buf()
                St = c.sb([64, 128], F32); Stb = p.buf()
                St_bf = c.sb([64, 128], BF16); Stbfb = p.buf()
                PW = 2048
                xin = [c.sb([64, 3 + PW], F32) for _ in range(2)]; xinb = p.bufs(2)
                cacc = c.sb([64, PW], F32); caccb = p.buf()
                n_x = 0
                for (src_d, dst, dstbufs, c0col, scl) in ((mqT, qT_bf, qTb, 5, 1.0), (mkT, kT_bf, kTb, 9, 0.125)):
                    for pc in range(T // PW):
                        X, Xb = xin[n_x % 2], xinb[n_x % 2]
                        if pc == 0:
                            p.op("dve", lambda e, X=X: e.memset(X[:, 0:3], 0.0), writes=[Xb])
                        else:
                            Xp, Xpb = xin[(n_x - 1) % 2], xinb[(n_x - 1) % 2]
                            p.op("act", lambda e, X=X, Xp=Xp: e.copy(out=X[:, 0:3], in_=Xp[:, PW:PW + 3]), reads=[Xpb], writes=[Xb])
                        p.dma("sp", "mx%d" % (n_x % 2), lambda e, X=X, pc=pc, src_d=src_d: e.dma_start(
                            out=X[:, 3:3 + PW], in_=src_d[:, pc * PW:(pc + 1) * PW]), writes=[Xb])
                        n_x += 1
                        p.op("dve", lambda e, X=X, c0col=c0col: e.tensor_scalar(out=cacc[:, :], in0=X[:, 0:PW], scalar1=prm_sb[0:64, c0col:c0col + 1],
                                                                                 scalar2=None, op0=ALU.mult), reads=[Xb, prmb], writes=[caccb])
                        for j in (1, 2, 3):
                            p.op("dve", lambda e, X=X, c0col=c0col, j=j: e.scalar_tensor_tensor(
                                out=cacc[:, :], in0=X[:, j:j + PW], scalar=prm_sb[0:64, c0col + j:c0col + j + 1], in1=cacc[:, :],
                                op0=ALU.mult, op1=ALU.add), reads=[Xb, prmb, caccb], writes=[caccb])
                        p.op("act", lambda e: e.activation(out=cacc[:, :], in_=cacc[:, :], func=AF.Silu), reads=[caccb], writes=[caccb])
                        p.op("dve", lambda e, dst=dst, pc=pc, scl=scl: e.tensor_scalar(out=dst[:, pc * PW:(pc + 1) * PW], in0=cacc[:, :], scalar1=scl,
                                                                                       scalar2=None, op0=ALU.mult), reads=[caccb], writes=[dstbufs[pc]])
                p.op("dve", lambda e: e.memset(vones[:, :, 64:128], 1.0), writes=[vonesb])
                for q in range(4):
                    p.dma("pool", "vones", lambda e, q=q: e.dma_start(out=vones[:, q * 16:(q + 1) * 16, 0:64],
                                                                   in_=mv.rearrange("(c q) e -> q c e", q=128)[:, q * 16:(q + 1) * 16, :]), writes=[vonesb])
                for q in range(4):
                    p.dma("sp", "gates", lambda e, q=q: e.dma_start(out=g_sb[:, q * 16:(q + 1) * 16, :],
                                                                in_=gates.rearrange("(c q) two -> q c two", q=128)[:, q * 16:(q + 1) * 16, :]), writes=[gsbb])
                p.op("dve", lambda e: e.tensor_scalar(out=iv[:, :], in0=g_sb[:, :, 0], scalar1=prm_sb[:, 13:14], scalar2=None, op0=ALU.add),
                     reads=[gsbb, prmb], writes=[ivb])
                p.op("dve", lambda e: e.tensor_scalar(out=lf[:, :], in0=g_sb[:, :, 1], scalar1=prm_sb[:, 14:15], scalar2=None, op0=ALU.add),
                     reads=[gsbb, prmb], writes=[lfb_])
                p.op("act", lambda e: e.activation(out=lf[:, :], in_=lf[:, :], func=AF.Exp, scale=-1.0), reads=[lfb_], writes=[lfb_])
                p.op("act", lambda e: e.activation(out=lf[:, :], in_=lf[:, :], func=AF.Ln, bias=1.0), reads=[lfb_], writes=[lfb_])
                p.op("dve", lambda e: e.tensor_scalar(out=lf[:, :], in0=lf[:, :], scalar1=-1.0, scalar2=None, op0=ALU.mult), reads=[lfb_], writes=[lfb_])
                psA = c.ps(); psAb = p.buf()
                psB = c.ps(); psBb = p.buf()
                p.mm([lambda e: e.matmul(psA[:, 0:NCH], lhsT=triU[:, :], rhs=lf[:, :], start=True, stop=True)], reads=[triUb, lfb_], writes=[psAb])
                p.mm([lambda e: e.matmul(psB[:, 0:NCH], lhsT=ones_f[:, :], rhs=lf[:, :], start=True, stop=True)], reads=[onesfb, lfb_], writes=[psBb])
                p.op("dve", lambda e: e.tensor_tensor(out=bias_s[:, :], in0=iv[:, :], in1=psA[:, 0:NCH], op=ALU.subtract), reads=[ivb, psAb], writes=[biasb])
                p.op("dve", lambda e: e.tensor_tensor(out=wk[:, :], in0=bias_s[:, :], in1=psB[:, 0:NCH], op=ALU.add), reads=[biasb, psBb], writes=[wkb])
                p.op("act", lambda e: e.activation(out=wk[:, :], in_=wk[:, :], func=AF.Exp), reads=[wkb], writes=[wkb])
                p.op("act", lambda e: e.activation(out=dec[:, :], in_=psB[:, 0:NCH], func=AF.Exp), reads=[psBb], writes=[decb])
                p.op("dve", lambda e: e.memset(St[:, :], 0.0), writes=[Stb])
                p.op("dve", lambda e: e.memset(St_bf[:, :], 0.0), writes=[Stbfb])

                lfrep = [c.sb([128, 128], F32) for _ in range(2)]; lfrepb = p.bufs(2)
                DT = [c.sb([128, 128], F32) for _ in range(2)]; DTb = p.bufs(2)
                WT = [c.sb([128, 128], BF16) for _ in range(2)]; WTb = p.bufs(2)
                eG = [c.sb([64, 128], F32) for _ in range(2)]; eGb = p.bufs(2)
                qs = [c.sb([64, 128], BF16) for _ in range(2)]; qsb = p.bufs(2)
                ksc = [c.sb([128, 64], BF16) for _ in range(2)]; kscb = p.bufs(2)
                dn = [c.sb([64, 128], F32) for _ in range(2)]; dnb = p.bufs(2)
                hT = [c.sb([64, 512], F32) for _ in range(2)]; hTb = p.bufs(2)
                sq = c.sb([64, 512], BF16); sqb = p.buf()
                rstd = c.sb([64, 512], F32); rstdb = p.buf()
                mo_sb = [c.sb([64, 512], F32) for _ in range(2)]; mob = p.bufs(2)
                ho = [c.sb([64, 512], F32) for _ in range(2)]; hob = p.bufs(2)
                psG = [c.ps() for _ in range(2)]; psGb = p.bufs(2)
                psN = [c.ps() for _ in range(2)]; psNb = p.bufs(2)
                for ch in range(NCH):
                    q = ch % 2
                    t0 = ch * 128
                    pcq = ch // 16
                    p.op("dve", lambda e, q=q, ch=ch: e.tensor_scalar(out=lfrep[q][:, :], in0=ones_f[:, :], scalar1=lf[:, ch:ch + 1], scalar2=None, op0=ALU.mult),
                         reads=[onesfb, lfb_], writes=[lfrepb[q]])
                    p.mm([lambda e, q=q: e.matmul(psG[q][:, 0:128], lhsT=lfrep[q][:, :], rhs=triU[:, :], start=True, stop=True)],
                         reads=[lfrepb[q], triUb], writes=[psGb[q]])
                    p.mm([lambda e, q=q, t0=t0: e.matmul(psG[q][:, 128:256], lhsT=kT_bf[:, t0:t0 + 128], rhs=qT_bf[:, t0:t0 + 128], start=True, stop=True)],
                         reads=[kTb[pcq], qTb[pcq]], writes=[psGb[q]])
                    p.op("act", lambda e, q=q, ch=ch: e.activation(out=DT[q][:, :], in_=psG[q][:, 0:128], func=AF.Exp, bias=bias_s[:, ch:ch + 1]),
                         reads=[psGb[q], biasb], writes=[DTb[q]])
                    p.op("act", lambda e, q=q: e.activation(out=eG[q][:, :], in_=psG[q][0:64, 0:128], func=AF.Exp), reads=[psGb[q]], writes=[eGb[q]])
                    p.op("dve", lambda e, q=q: e.tensor_tensor(out=DT[q][:, :], in0=DT[q][:, :], in1=triU[:, :], op=ALU.mult), reads=[DTb[q], triUb], writes=[DTb[q]])
                    p.op("dve", lambda e, q=q: e.tensor_tensor(out=WT[q][:, :], in0=DT[q][:, :], in1=psG[q][:, 128:256], op=ALU.mult), reads=[DTb[q], psGb[q]], writes=[WTb[q]])
                    p.op("dve", lambda e, q=q, t0=t0: e.tensor_tensor(out=qs[q][:, :], in0=eG[q][:, :], in1=qT_bf[:, t0:t0 + 128], op=ALU.mult),
                         reads=[eGb[q], qTb[pcq]], writes=[qsb[q]])
                    p.mm([lambda e, q=q, ch=ch: e.matmul(psN[q][0:64, 0:128], lhsT=vones[:, ch, 0:64], rhs=WT[q][:, :], start=True, stop=False),
                          lambda e, q=q: e.matmul(psN[q][0:64, 0:128], lhsT=St_bf[:, 0:64], rhs=qs[q][:, :], start=False, stop=True),
                          lambda e, q=q, ch=ch: e.matmul(psN[q][0:64, 128:256], lhsT=vones[:, ch, 64:128], rhs=WT[q][:, :], start=True, stop=False),
                          lambda e, q=q: e.matmul(psN[q][0:64, 128:256], lhsT=St_bf[:, 64:128], rhs=qs[q][:, :], start=False, stop=True),
                          lambda e, q=q, t0=t0: e.matmul(psN[q][:, 256:320], lhsT=kT_bf[:, t0:t0 + 128], rhs=ident_bf[0:64, 0:64], start=True, stop=True)],
                         reads=[vonesb, WTb[q], Stbfb, qsb[q], kTb[pcq], identb], writes=[psNb[q]])
                    hq = (ch // 4) % 2
                    hc = (ch % 4) * 128
                    p.op("act", lambda e, q=q: e.activation(out=dn[q][:, :], in_=psN[q][0:64, 128:256], func=AF.Abs), reads=[psNb[q]], writes=[dnb[q]])
                    p.op("dve", lambda e, q=q: e.tensor_scalar(out=dn[q][:, :], in0=dn[q][:, :], scalar1=1.0, scalar2=None, op0=ALU.max), reads=[dnb[q]], writes=[dnb[q]])
                    p.op("dve", lambda e, q=q: e.reciprocal(out=dn[q][:, :], in_=dn[q][:, :]), reads=[dnb[q]], writes=[dnb[q]])
                    p.op("dve", lambda e, q=q, hq=hq, hc=hc: e.tensor_tensor(out=hT[hq][:, hc:hc + 128], in0=psN[q][0:64, 0:128], in1=dn[q][:, :], op=ALU.mult),
                         reads=[psNb[q], dnb[q]], writes=[hTb[hq]])
                    p.op("act", lambda e, q=q, ch=ch: e.activation(out=ksc[q][:, :], in_=psN[q][:, 256:320], func=AF.Copy, scale=wk[:, ch:ch + 1]),
                         reads=[psNb[q], wkb], writes=[kscb[q]])
                    p.mm([lambda e, q=q, ch=ch: e.matmul(psN[q][0:64, 320:448], lhsT=ksc[q][:, :], rhs=vones[:, ch, :], start=True, stop=True)],
                         reads=[kscb[q], vonesb], writes=[psNb[q]])
                    p.op("dve", lambda e, q=q, ch=ch: e.scalar_tensor_tensor(out=St[:, :], in0=St[:, :], scalar=dec[0:64, ch:ch + 1], in1=psN[q][0:64, 320:448],
                                                                             op0=ALU.mult, op1=ALU.add), reads=[Stb, decb, psNb[q]], writes=[Stb])
                    p.op("act", lambda e: e.copy(out=St_bf[:, :], in_=St[:, :]), reads=[Stb], writes=[Stbfb])
                    if ch % 4 == 3:
                        pt0 = (ch // 4) * 512
                        H_, Hb_ = hT[hq], hTb[hq]
                        p.dma("sp", "mo%d" % hq, lambda e, hq=hq, pt0=pt0: e.dma_start(out=mo_sb[hq][:, :], in_=moT[:, pt0:pt0 + 512]), writes=[mob[hq]])
                        p.op("act", lambda e, H_=H_: e.activation(out=sq[:, :], in_=H_[:, :], func=AF.Square), reads=[Hb_], writes=[sqb])
                        p.mm([lambda e: e.matmul(psA[0:64, :], lhsT=ones_bf[0:64, 0:64], rhs=sq[:, :], start=True, stop=True)], reads=[onesbb, sqb], writes=[psAb])
                        p.op("act", lambda e: e.activation(out=rstd[:, :], in_=psA[0:64, :], func=AF.Ln, bias=EPS, scale=1.0 / 64), reads=[psAb], writes=[rstdb])
                        p.op("act", lambda e: e.activation(out=rstd[:, :], in_=rstd[:, :], func=AF.Exp, scale=-0.5), reads=[rstdb], writes=[rstdb])
                        p.op("act", lambda e, hq=hq: e.activation(out=mo_sb[hq][:, :], in_=mo_sb[hq][:, :], func=AF.Sigmoid), reads=[mob[hq]], writes=[mob[hq]])
                        p.op("dve", lambda e, H_=H_, hq=hq: e.scalar_tensor_tensor(out=ho[hq][:, :], in0=H_[:, :], scalar=prm_sb[0:64, 15:16], in1=rstd[:, :],
                                                                                   op0=ALU.mult, op1=ALU.mult), reads=[Hb_, prmb, rstdb], writes=[hob[hq]])
                        p.op("dve", lambda e, hq=hq: e.tensor_tensor(out=ho[hq][:, :], in0=ho[hq][:, :], in1=mo_sb[hq][:, :], op=ALU.mult),
                             reads=[hob[hq], mob[hq]], writes=[hob[hq]])
                        p.dma("sp", "ho%d" % hq, lambda e, hq=hq, pt0=pt0: e.dma_start(out=mixT[64:128, pt0:pt0 + 512], in_=ho[hq][:, :]), reads=[hob[hq]], writes=[outb])

        p.barrier()
        if do_moba:
            build_moba(nc, p, prm_sb, prmb, ident_bf, identb, ones_bf, onesbb, aqT, akT, av, caus_d, ind_d, mixT, outb)
        p.finish_all("sp")
        p.emit()
    return nc


def static_consts():
    triU = np.triu(np.ones((128, 128), np.float32))
    ident = np.eye(128, dtype=np.float32)
    k = np.arange(128)[:, None]
    q = np.arange(512)[None, :]
    caus = np.concatenate([((r * 128 + k) <= q).astype(np.float32) for r in range(4)], axis=1)
    ind = (np.arange(T)[None, :] // 256 == np.arange(32)[:, None]).astype(np.float32)
    return triU, ident, np.ascontiguousarray(caus), np.ascontiguousarray(ind)


def host_B_inputs(zfm_b, ztm_b, prm_l):
    triU, ident, caus, ind = static_consts()
    maps = []
    for cidx in range(NCORE):
        b, g = cidx // 4, cidx % 4
        zf, zt = zfm_b[b], ztm_b[b]
        w = POOL_WINDOWS[g]
        prm = np.zeros((128, NPRM), np.float32)
        prm[0:64, 0] = prm_l["pool_scale"][g * 64:(g + 1) * 64]
        prm[:, 1 + g] = 1.0 / w
        prm[0:64, 5:9] = prm_l["m_conv"][:, g * 64:(g + 1) * 64].T
        prm[0:64, 9:13] = prm_l["m_conv"][:, 256 + g * 64:256 + (g + 1) * 64].T
        prm[:, 13] = prm_l["m_b_i"][g]
        prm[:, 14] = prm_l["m_b_f"][g]
        prm[0:64, 15] = prm_l["m_norm_g"][g * 64:(g + 1) * 64]
        prm[:, 16] = np.tile(prm_l["a_q_g"], 2)
        prm[:, 17] = np.tile(prm_l["a_k_g"], 2)
        prm[:, 19:35] = (w / np.minimum(np.arange(16) + 1, w)).astype(np.float32)[None, :]
        maps.append({
            "pT": np.ascontiguousarray(zf[g * 64:(g + 1) * 64]),
            "mqT": np.ascontiguousarray(zf[256 + g * 64:256 + (g + 1) * 64]),
            "mkT": np.ascontiguousarray(zf[512 + g * 64:512 + (g + 1) * 64]),
            "moT": np.ascontiguousarray(zf[768 + g * 64:768 + (g + 1) * 64]),
            "aqT": np.ascontiguousarray(zf[1024 + g * 128:1024 + (g + 1) * 128]),
            "akT": np.ascontiguousarray(zf[1536 + g * 128:1536 + (g + 1) * 128]),
            "mv": np.ascontiguousarray(zt[:, g * 64:(g + 1) * 64]),
            "av": np.ascontiguousarray(zt[:, 256 + g * 128:256 + (g + 1) * 128]),
            "gates": np.ascontiguousarray(zt[:, [768 + g, 772 + g]]),
            "prm": prm, "poolw": np.ascontiguousarray(prm_l["pool_w"][g]),
            "triU": triU, "ident": ident, "caus": caus, "ind": ind,
        })
    return maps


_PROGS = {}


def _prog(name):
    if name not in _PROGS:
        _PROGS[name] = {"A": build_A, "B": build_B, "C": build_C}[name]()
    return _PROGS[name]


def kernel_unfused(**inputs):
    inp = {k: np.ascontiguousarray(np.asarray(v), dtype=np.float32) for k, v in inputs.items()}
    x = inp["x"].copy()
    cores = list(range(NCORE))
    for l in range(4):
        wfm, wtm = split_w_in(inp["w_in"][l])
        g1l = np.ascontiguousarray(inp["ln1_g"][l].reshape(8, 128).T)
        maps = []
        for c in cores:
            b, j = c // 4, c % 4
            maps.append({"xT": np.ascontiguousarray(x[b, j * TQ:(j + 1) * TQ, :].T), "wfm": wfm, "wtm": wtm, "g1": g1l})
        res = run_bass_kernel_spmd(_prog("A"), maps, core_ids=cores).results
        zfm_b = [np.concatenate([res[b * 4 + j]["zfm"] for j in range(4)], axis=1) for b in range(2)]
        ztm_b = [np.concatenate([res[b * 4 + j]["ztm"] for j in range(4)], axis=0) for b in range(2)]
        prm_l = {k: inp[k][l] for k in ("pool_scale", "m_conv", "m_b_i", "m_b_f", "m_norm_g", "a_q_g", "a_k_g", "pool_w")}
        res = run_bass_kernel_spmd(_prog("B"), host_B_inputs(zfm_b, ztm_b, prm_l), core_ids=cores).results
        mix = np.empty((2, T, D), np.float32)
        for c in cores:
            b, g = c // 4, c % 4
            m = res[c]["mixT"]
            mix[b, :, g * 64:(g + 1) * 64] = m[0:64].T
            mix[b, :, 256 + g * 64:256 + (g + 1) * 64] = m[64:128].T
            mix[b, :, 512 + g * 128:512 + (g + 1) * 128] = m[128:256].T
        res = run_bass_kernel_spmd(_prog("C"), host_C_inputs(x, mix, inp["w_out"][l], inp["ln2_g"][l], inp["w_up"][l],
                                                             inp["ffn_conv"][l], inp["w_down"][l]), core_ids=cores).results
        xn = np.empty_like(x)
        for c in cores:
            b, j = c // 4, c % 4
            xn[b, j * TQ:(j + 1) * TQ, :] = res[c]["xoT"].T
        x = xn
    return x


U32 = mybir.dt.uint32
RZ = 2816
RF = 10
RM = 1025
CHR = 256
GROUPS = [[0, 1, 2, 3], [4, 5, 6, 7]]
IDX_POOL, IDX_MQ, IDX_MK = 0, 4, 8
IDX_MO, IDX_AQ, IDX_AK = 12, 28, 44
IDX_TM = 60
IDX_XT = 64
IDX_MX = 65
IDX_MH = 97
IDX_AV = 105
IDX_GT = 109
NIDX = 113


def build_fused(nlayers=4, upto=3):
    nc = bass.Bass("TRN2", target_bir_lowering=False)
    p = Prog(nc)
    with ExitStack() as st0:
        c0 = Ctx(nc, st0, p)
        xT = c0.dram_in("xT", [D, TQ])
        wfm = c0.dram_in("wfm", [4, D, NFM])
        wtm = c0.dram_in("wtm", [4, D, NTM])
        g1 = c0.dram_in("g1", [4, 128, 8])
        w_out = c0.dram_in("w_out", [4, D, D])
        g2 = c0.dram_in("g2", [4, 128, 8])
        w_up = c0.dram_in("w_up", [4, 22, 128, 8, 256])
        convw = c0.dram_in("convw", [4, 128, 44, 3])
        w_down = c0.dram_in("w_down", [4, DFF, D])
        prm = c0.dram_in("prm", [4, 128, NPRM])
        poolw = c0.dram_in("poolw", [4, 64, 64])
        triU_d = c0.dram_in("triU", [128, 128])
        ident_d = c0.dram_in("ident", [128, 128])
        caus_d = c0.dram_in("caus", [128, 2048])
        ind_d = c0.dram_in("ind", [32, T])
        idx_d = c0.dram_in("idx", [128, NIDX], U32)
        xoT = c0.dram_out("xoT", [D, TQ])
        zsrc_t = nc.dram_tensor("zsrc", [RZ, 2048], BF16, kind="Internal")
        zall_t = nc.dram_tensor("zall", [4 * RZ, 2048], BF16, kind="Internal")
        zsrcf_t = nc.dram_tensor("zsrcf", [RF, 2048], F32, kind="Internal")
        zallf_t = nc.dram_tensor("zallf", [4 * RF, 2048], F32, kind="Internal")
        msrc_t = nc.dram_tensor("msrc", [RM, 2048], BF16, kind="Internal")
        mall_t = nc.dram_tensor("mall", [4 * RM, 2048], BF16, kind="Internal")
        zsrcb, zallb, msrcb, mallb, outb = p.buf(), p.buf(), p.buf(), p.buf(), p.buf()
        zall2048 = zall_t.ap()
        zall512 = zall_t.ap().rearrange("r (f w) -> (r f) w", w=512)
        zall1024 = zall_t.ap().rearrange("r (f w) -> (r f) w", w=1024)
        zallf32 = zallf_t.ap().rearrange("r (f w) -> (r f) w", w=32)
        zall16 = zallf_t.ap().rearrange("r (f w) -> (r f) w", w=16)
        mall512 = mall_t.ap().rearrange("r (f w) -> (r f) w", w=512)
        mall2 = mall_t.ap().rearrange("r (f w) -> (r f) w", w=2)

        x_sb = c0.sb([128, 8, TQH], F32); xb = p.bufs(8, "x")
        idx_sb = c0.sb([128, NIDX], U32); idxb = p.buf()
        onesD_bf = c0.sb([128, 128], BF16); onesDb = p.buf()
        ones_f = c0.sb([128, 128], F32); onesfb = p.buf()
        ones_bf = c0.sb([128, 128], BF16); onesbb = p.buf()
        zero_sb = c0.sb([128, 16], F32); zerob = p.buf()
        zero_bf = c0.sb([128, 16], BF16); zerobb = p.buf()
        triU = c0.sb([128, 128], F32); triUb = p.buf()
        ident_bf = c0.sb([128, 128], BF16); identb = p.buf()
        caus = c0.sb([128, 2048], BF16); causb = p.buf()
        blk = c0.sb([128, 128], BF16); blkb = p.buf()
        for k in range(8):
            p.dma("sp", "x%d" % k, lambda e, k=k: e.dma_start(out=x_sb[:, k, 2:TQH], in_=xT[k * 128:(k + 1) * 128, :]), writes=[xb[k]])
        p.dma("sp", "idx", lambda e: e.dma_start(out=idx_sb[:, :], in_=idx_d), writes=[idxb])
        p.dma("sp", "triU", lambda e: e.dma_start(out=triU[:, :], in_=triU_d), writes=[triUb])
        p.dma("pool", "ident", lambda e: e.dma_start(out=ident_bf[:, :], in_=ident_d), writes=[identb])
        p.dma("pool", "caus", lambda e: e.dma_start(out=caus[:, :], in_=caus_d), writes=[causb])
        p.op("dve", lambda e: e.memset(onesD_bf[:, :], 1.0 / D), writes=[onesDb])
        p.op("dve", lambda e: e.memset(ones_f[:, :], 1.0), writes=[onesfb])
        p.op("dve", lambda e: e.memset(ones_bf[:, :], 1.0), writes=[onesbb])
        p.op("dve", lambda e: e.memset(zero_sb[:, :], 0.0), writes=[zerob])
        p.op("dve", lambda e: e.memset(zero_bf[:, :], 0.0), writes=[zerobb])
        p.op("dve", lambda e: e.memset(blk[:, :], 0.0), writes=[blkb])
        p.op("dve", lambda e: e.memset(blk[0:64, 0:64], 1.0), writes=[blkb])
        p.op("dve", lambda e: e.memset(blk[64:128, 64:128], 1.0), writes=[blkb])
        for k in range(8):
            p.op("dve", lambda e, k=k: e.memset(x_sb[:, k, 0:2], 0.0), writes=[xb[k]])

        zsrc_cb = p.bufs(12, "zs")
        zall_cb = p.bufs(12, "za")
        msrc_cb = p.bufs(5, "ms")
        mall_cb = p.bufs(5, "ma")

        def ag_chunk(src_t, dst_t, nrows, i, scb, dcb):
            r0 = i * CHR
            n = min(CHR, nrows - r0)
            p.wait_dma_all("pool")
            p.cc(lambda e: e.collective_compute("AllGather", ALU.bypass, replica_groups=GROUPS,
                                                ins=[src_t.ap()[r0:r0 + n, :]], outs=[dst_t.ap()[4 * r0:4 * r0 + 4 * n, :]]),
                 reads=[scb[i]], writes=[dcb[i]])

        def ag_f():
            p.wait_dma_all("pool")
            p.cc(lambda e: e.collective_compute("AllGather", ALU.bypass, replica_groups=GROUPS, ins=[zsrcf_t.ap()], outs=[zallf_t.ap()]),
                 reads=[zsrc_cb[11]], writes=[zall_cb[11]])

        def gather(ch, out_ap, view, col, nparts, srcbs, writes):
            p.dma("pool", ch, lambda e: e.indirect_dma_start(
                out=out_ap, out_offset=None, in_=view,
                in_offset=bass.IndirectOffsetOnAxis(ap=idx_sb[0:nparts, col:col + 1], axis=0)), reads=list(srcbs) + [idxb], writes=writes)

        def out_piece(ch, sg_ap, sgb, r0, nrows, t0):
            j, cc0 = t0 // 2048, t0 % 2048
            if r0 < 128:
                dst = bass.AP(msrc_t, (r0 * 4 + j) * 2048 + cc0, [[4 * 2048, nrows], [1, 512]])
                wb = [msrc_cb[0]] if r0 < 64 else [msrc_cb[1]]
            else:
                dst = bass.AP(msrc_t, (512 + j * 128 + (r0 - 128)) * 2048 + cc0, [[2048, nrows], [1, 512]])
                wb = [msrc_cb[2 + j // 2]]
            p.dma("sp", ch, lambda e: e.dma_start(out=dst, in_=sg_ap), reads=[sgb], writes=wb)
            if cc0 + 512 == 2048 and j < 3:
                hd = bass.AP(msrc_t, 1024 * 2048 + r0 * 8 + (j + 1) * 2, [[8, nrows], [1, 2]])
                p.dma("sp", ch, lambda e: e.dma_start(out=hd, in_=sg_ap[:, 510:512]), reads=[sgb], writes=[msrc_cb[4]])

        def phase_A(l):
            with ExitStack() as st:
                c = Ctx(nc, st, p)
                wfm_sb = c.sb([128, 8, NFM], BF16); wfmb = p.buf()
                wtm_sb = c.sb([128, 8, NTM], BF16); wtmb = p.buf()
                g_sb = c.sb([128, 8], F32); gb = p.buf()
                sq = c.sb([128, 8, 512], BF16); sqb = p.bufs(8)
                hTs = [c.sb([128, 8, 512], BF16) for _ in range(4)]; hbs = [p.bufs(8) for _ in range(4)]
                rstd = c.sb([128, 512], F32); rstdb = p.buf()
                NST = 12
                stg = [c.sb([128, 512], BF16) for _ in range(NST)]; stgb = p.bufs(NST)
                stgt = [c.sb([128, 768], BF16) for _ in range(2)]; stgtb = p.bufs(2)
                stgg = [c.sb([128, 8], F32) for _ in range(2)]; stggb = p.bufs(2)
                psn = c.ps(); psnb = p.buf()
                NPS = 7
                pss = [c.ps() for _ in range(NPS)]; pssb = p.bufs(NPS)
                p.dma("sp", "g", lambda e: e.dma_start(out=g_sb[:, :], in_=g1[l]), writes=[gb])
                for k in range(8):
                    p.dma("pool", "wtm", lambda e, k=k: e.dma_start(out=wtm_sb[:, k, :], in_=wtm[l, k * 128:(k + 1) * 128, :]), writes=[wtmb])
                for k in range(8):
                    p.dma("pool", "wfm", lambda e, k=k: e.dma_start(out=wfm_sb[:, k, :], in_=wfm[l, k * 128:(k + 1) * 128, :]), writes=[wfmb])
                p.dma("sp", "xt", lambda e: e.dma_start(out=bass.AP(zsrcf_t, 8 * 2048, [[16, 128], [2, 8], [1, 2]]), in_=x_sb[:, :, TQH - 2:TQH]),
                      reads=list(xb), writes=[zsrc_cb[11]])
                p.dma("sp", "xz", lambda e: e.dma_start(out=bass.AP(zsrcf_t, 9 * 2048, [[16, 128], [1, 16]]), in_=zero_sb[:, :]), reads=[zerob], writes=[zsrc_cb[11]])
                for tt in range(4):
                    rmsnorm_fm(p, c, x_sb, None, 2 + tt * 512, 512, g_sb, gb, onesD_bf, onesDb, hTs[tt], hbs[tt], sq, sqb, psn, psnb, rstd, rstdb, xbs=xb)
                i_ps = 0
                i_st = 0
                i_tt = 0
                for tt in range(4):
                    hT, hb = hTs[tt], hbs[tt]
                    for s in range(4):
                        q = i_tt % 2; i_tt += 1
                        sgt, sgtb = stgt[q], stgtb[q]
                        sgg, sggb = stgg[q], stggb[q]
                        ps, psb = pss[i_ps % NPS], pssb[i_ps % NPS]; i_ps += 1
                        p.mm([lambda e, k=k, s=s, ps=ps, hT=hT: e.matmul(ps[:, :512], lhsT=hT[:, k, s * 128:(s + 1) * 128],
                                                                        rhs=wtm_sb[:, k, 0:512], start=(k == 0), stop=(k == 7))
                              for k in range(8)], reads=[wtmb] + list(hb), writes=[psb])
                        p.op("act", lambda e, ps=ps, sgt=sgt: e.copy(out=sgt[:, 0:512], in_=ps[:, :512]), reads=[psb], writes=[sgtb])
                        ps, psb = pss[i_ps % NPS], pssb[i_ps % NPS]; i_ps += 1
                        p.mm([lambda e, k=k, s=s, ps=ps, hT=hT: e.matmul(ps[:, :NTM - 512], lhsT=hT[:, k, s * 128:(s + 1) * 128],
                                                                        rhs=wtm_sb[:, k, 512:NTM], start=(k == 0), stop=(k == 7))
                              for k in range(8)], reads=[wtmb] + list(hb), writes=[psb])
                        p.op("act", lambda e, ps=ps, sgt=sgt: e.copy(out=sgt[:, 512:768], in_=ps[:, 0:256]), reads=[psb], writes=[sgtb])
                        p.op("act", lambda e, ps=ps, sgg=sgg: e.copy(out=sgg[:, 0:8], in_=ps[:, 256:264]), reads=[psb], writes=[sggb])
                        cidx = tt * 4 + s
                        dmv = bass.AP(zsrc_t, cidx * 64, [[16 * 64, 128], [128 * 16 * 64, 4], [1, 64]])
                        p.dma("sp", "ot%d" % q, lambda e, dmv=dmv, sgt=sgt: e.dma_start(
                            out=dmv, in_=sgt[:, 0:256].rearrange("p (g e) -> p g e", g=4)), reads=[sgtb], writes=[zsrc_cb[0]])
                        dav = bass.AP(zsrc_t, 256 * 2048 + cidx * 128, [[16 * 128, 128], [128 * 16 * 128, 4], [1, 128]])
                        p.dma("sp", "ot%d" % q, lambda e, dav=dav, sgt=sgt: e.dma_start(
                            out=dav, in_=sgt[:, 256:768].rearrange("p (g e) -> p g e", g=4)), reads=[sgtb], writes=[zsrc_cb[1], zsrc_cb[2]])
                        dgt = bass.AP(zsrcf_t, cidx * 2, [[32, 128], [128 * 32, 4], [1, 2]])
                        p.dma("sp", "og%d" % q, lambda e, dgt=dgt, sgg=sgg: e.dma_start(
                            out=dgt, in_=sgg[:, 0:8].rearrange("p (g e) -> p g e", g=4)), reads=[sggb], writes=[zsrc_cb[11]])
                ag_f()
                for i in range(3):
                    ag_chunk(zsrc_t, zall_t, RZ, i, zsrc_cb, zall_cb)
                for oc in range(NFM // 128):
                    for tt in range(4):
                        hT, hb = hTs[tt], hbs[tt]
                        ps, psb = pss[i_ps % NPS], pssb[i_ps % NPS]; i_ps += 1
                        p.mm([lambda e, k=k, oc=oc, ps=ps, hT=hT: e.matmul(ps[:, :], lhsT=wfm_sb[:, k, oc * 128:(oc + 1) * 128], rhs=hT[:, k, :],
                                                                            start=(k == 0), stop=(k == 7)) for k in range(8)],
                             reads=[wfmb] + list(hb), writes=[psb])
                        q = i_st % NST; i_st += 1
                        sg, sgb = stg[q], stgb[q]
                        p.op("act", lambda e, ps=ps, sg=sg: e.copy(out=sg[:, :], in_=ps[:, :]), reads=[psb], writes=[sgb])
                        p.dma("sp", "o%d" % q, lambda e, sg=sg, oc=oc, tt=tt: e.dma_start(
                            out=zsrc_t.ap()[768 + oc * 128:768 + (oc + 1) * 128, tt * 512:(tt + 1) * 512], in_=sg[:, :]), reads=[sgb], writes=[zsrc_cb[3 + oc // 2]])

        def phase_B(l):
            with ExitStack() as stB:
                cB = Ctx(nc, stB, p)
                NCH = T // 128
                prm_sb = cB.sb([128, NPRM], F32); prmb = p.buf()
                vaug = cB.sb([128, NCH, 128], BF16); vb = p.buf()
                vones = cB.sb([128, NCH, 128], BF16); vonesb = p.buf()
                g_sb = cB.sb([128, NCH, 2], F32); gsbb = p.buf()
                p.dma("sp", "prm", lambda e, l=l: e.dma_start(out=prm_sb[:, :], in_=prm[l]), writes=[prmb])
                p.op("dve", lambda e: e.memset(vones[:, :, 64:128], 1.0), writes=[vonesb])
                with ExitStack() as st:
                    c = Ctx(nc, st, p)
                    stm = [c.sb([128, 1024], BF16) for _ in range(2)]; stmb = p.bufs(2)
                    for r in range(4):
                        gather("tm%d" % (r % 2), stm[r % 2][:, :], zall1024, IDX_TM + r, 128, [zall_cb[0]], [stmb[r % 2]])
                        p.op("act", lambda e, r=r: e.copy(out=vones[:, r * 16:(r + 1) * 16, 0:64], in_=stm[r % 2].rearrange("p (c e) -> p c e", e=64)),
                             reads=[stmb[r % 2]], writes=[vonesb])
                        gather("av", vaug.rearrange("p c e -> p (c e)")[:, r * 2048:(r + 1) * 2048], zall2048, IDX_AV + r, 128, zall_cb[1:3], [vb])
                        gather("gt", g_sb.rearrange("p c t -> p (c t)")[:, r * 32:(r + 1) * 32], zallf32, IDX_GT + r, 128, [zall_cb[11]], [gsbb])
                ag_chunk(zsrc_t, zall_t, RZ, 3, zsrc_cb, zall_cb)
                p.barrier(exclude=("cc",))
                for r0 in (0, 128):
                    p.dma("sp", "hz", lambda e, r0=r0: e.dma_start(out=bass.AP(msrc_t, 1024 * 2048 + r0 * 8, [[8, 128], [1, 2]]), in_=zero_bf[:, 0:2]),
                          reads=[zerobb], writes=[msrc_cb[4]])
                with ExitStack() as st:
                    c = Ctx(nc, st, p)
                    PW = 2048
                    W = 16 + PW
                    xa = [c.sb([64, W], BF16) for _ in range(2)]; xab = p.bufs(2)
                    s_a = c.sb([64, W], F32); sab = p.buf()
                    s_b = c.sb([64, W], F32); sbb = p.buf()
                    acc = c.sb([64, W], F32); accb = p.buf()
                    d_bf = c.sb([64, PW], BF16); dbb = p.buf()
                    wp_bf = c.sb([64, 64], BF16); wpb = p.buf()
                    stg = [c.sb([64, 512], BF16) for _ in range(2)]; stgb = p.bufs(2)
                    pps = [c.ps() for _ in range(2)]; ppsb = p.bufs(2)
                    p.dma("pool", "wp", lambda e, l=l: e.dma_start(out=wp_bf[:, :], in_=poolw[l]), writes=[wpb])
                    p.op("dve", lambda e: e.memset(xa[0][:, 0:16], 0.0), writes=[xab[0]])
                    i_s = 0
                    for pc in range(T // PW):
                        X, Xb = xa[pc % 2], xab[pc % 2]
                        if pc > 0:
                            Xp, Xpb = xa[(pc - 1) % 2], xab[(pc - 1) % 2]
                            p.op("act", lambda e, X=X, Xp=Xp: e.copy(out=X[:, 0:16], in_=Xp[:, PW:PW + 16]), reads=[Xpb], writes=[Xb])
                        gather("px%d" % (pc % 2), X[0:64, 16:16 + PW], zall2048, IDX_POOL + pc, 64, [zall_cb[3]], [Xb])
                        if pc < 2:
                            ag_chunk(zsrc_t, zall_t, RZ, 4 + pc, zsrc_cb, zall_cb)
                        p.op("dve", lambda e, X=X: e.tensor_tensor(out=s_a[:, 1:W], in0=X[:, 1:W], in1=X[:, 0:W - 1], op=ALU.add), reads=[Xb], writes=[sab])
                        p.op("dve", lambda e: e.tensor_scalar(out=acc[:, 16:W], in0=s_a[:, 16:W], scalar1=prm_sb[0:64, 1:2], scalar2=None, op0=ALU.mult),
                             reads=[sab, prmb], writes=[accb])
                        src, srcb, dst, dstb = s_a, sab, s_b, sbb
                        lo = 1
                        for k, sh in ((1, 2), (2, 4), (3, 8)):
                            lo2 = lo + sh
                            p.op("dve", lambda e, src=src, dst=dst, lo2=lo2, sh=sh: e.tensor_tensor(
                                out=dst[:, lo2:W], in0=src[:, lo2:W], in1=src[:, lo2 - sh:W - sh], op=ALU.add), reads=[srcb], writes=[dstb])
                            p.op("dve", lambda e, dst=dst, k=k: e.scalar_tensor_tensor(
                                out=acc[:, 16:W], in0=dst[:, 16:W], scalar=prm_sb[0:64, 1 + k:2 + k], in1=acc[:, 16:W], op0=ALU.mult, op1=ALU.add),
                                reads=[dstb, prmb, accb], writes=[accb])
                            src, srcb, dst, dstb = dst, dstb, src, srcb
                            lo = lo2
                        if pc == 0:
                            p.op("dve", lambda e: e.tensor_tensor(out=acc[:, 16:32], in0=acc[:, 16:32], in1=prm_sb[0:64, 19:35], op=ALU.mult),
                                 reads=[accb, prmb], writes=[accb])
                        p.op("dve", lambda e, X=X: e.tensor_tensor(out=d_bf[:, :], in0=acc[:, 16:W], in1=X[:, 16:W], op=ALU.subtract), reads=[accb, Xb], writes=[dbb])
                        for q in range(PW // 512):
                            ps, psb = pps[i_s % 2], ppsb[i_s % 2]
                            sg, sgb = stg[i_s % 2], stgb[i_s % 2]
                            ch = "po%d" % (i_s % 2); i_s += 1
                            p.mm([lambda e, ps=ps, q=q: e.matmul(ps[0:64, :], lhsT=wp_bf[:, :], rhs=d_bf[:, q * 512:(q + 1) * 512], start=True, stop=True)],
                                 reads=[wpb, dbb], writes=[psb])
                            p.op("act", lambda e, ps=ps, sg=sg: e.activation(out=sg[:, :], in_=ps[0:64, :], func=AF.Copy, scale=prm_sb[0:64, 0:1]),
                                 reads=[psb, prmb], writes=[sgb])
                            out_piece(ch, sg[:, :], sgb, 0, 64, pc * PW + q * 512)
                p.barrier(exclude=("cc",))
                with ExitStack() as st:
                    c = Ctx(nc, st, p)
                    qT_bf = c.sb([64, T], BF16); qTb = p.bufs(4, "qT")
                    kT_bf = c.sb([64, T], BF16); kTb = p.bufs(4, "kT")
                    iv = c.sb([128, NCH], F32); ivb = p.buf()
                    lf = c.sb([128, NCH], F32); lfb_ = p.buf()
                    bias_s = c.sb([128, NCH], F32); biasb = p.buf()
                    wk = c.sb([128, NCH], F32); wkb = p.buf()
                    dec = c.sb([128, NCH], F32); decb = p.buf()
                    St = c.sb([64, 128], F32); Stb = p.buf()
                    St_bf2 = [c.sb([64, 128], BF16) for _ in range(2)]; Stbfb2 = p.bufs(2)
                    PW = 2048
                    xin = [c.sb([64, 3 + PW], BF16) for _ in range(2)]; xinb = p.bufs(2)
                    cacc = c.sb([64, PW], F32); caccb = p.buf()
                    n_x = 0
                    ag_chunk(zsrc_t, zall_t, RZ, 6, zsrc_cb, zall_cb)
                    for (icol, zcb, dst, dstbufs, c0col, scl) in ((IDX_MQ, [zall_cb[4]], qT_bf, qTb, 5, 1.0), (IDX_MK, [zall_cb[5]], kT_bf, kTb, 9, 0.125)):
                        for pc in range(T // PW):
                            X, Xb = xin[n_x % 2], xinb[n_x % 2]
                            if pc == 0:
                                p.op("dve", lambda e, X=X: e.memset(X[:, 0:3], 0.0), writes=[Xb])
                            else:
                                Xp, Xpb = xin[(n_x - 1) % 2], xinb[(n_x - 1) % 2]
                                p.op("act", lambda e, X=X, Xp=Xp: e.copy(out=X[:, 0:3], in_=Xp[:, PW:PW + 3]), reads=[Xpb], writes=[Xb])
                            gather("mx%d" % (n_x % 2), X[0:64, 3:3 + PW], zall2048, icol + pc, 64, zcb, [Xb])
                            n_x += 1
                            p.op("dve", lambda e, X=X, c0col=c0col: e.tensor_scalar(out=cacc[:, :], in0=X[:, 0:PW], scalar1=prm_sb[0:64, c0col:c0col + 1],
                                                                                     scalar2=None, op0=ALU.mult), reads=[Xb, prmb], writes=[caccb])
                            for j in (1, 2, 3):
                                p.op("dve", lambda e, X=X, c0col=c0col, j=j: e.scalar_tensor_tensor(
                                    out=cacc[:, :], in0=X[:, j:j + PW], scalar=prm_sb[0:64, c0col + j:c0col + j + 1], in1=cacc[:, :],
                                    op0=ALU.mult, op1=ALU.add), reads=[Xb, prmb, caccb], writes=[caccb])
                            p.op("act", lambda e: e.activation(out=cacc[:, :], in_=cacc[:, :], func=AF.Silu), reads=[caccb], writes=[caccb])
                            p.op("dve", lambda e, dst=dst, pc=pc, scl=scl: e.tensor_scalar(out=dst[:, pc * PW:(pc + 1) * PW], in0=cacc[:, :], scalar1=scl,
                                                                                           scalar2=None, op0=ALU.mult), reads=[caccb], writes=[dstbufs[pc]])
                    p.op("dve", lambda e: e.tensor_scalar(out=iv[:, :], in0=g_sb[:, :, 0], scalar1=prm_sb[:, 13:14], scalar2=None, op0=ALU.add),
                         reads=[gsbb, prmb], writes=[ivb])
                    p.op("dve", lambda e: e.tensor_scalar(out=lf[:, :], in0=g_sb[:, :, 1], scalar1=prm_sb[:, 14:15], scalar2=None, op0=ALU.add),
                         reads=[gsbb, prmb], writes=[lfb_])
                    p.op("act", lambda e: e.activation(out=lf[:, :], in_=lf[:, :], func=AF.Exp, scale=-1.0), reads=[lfb_], writes=[lfb_])
                    p.op("act", lambda e: e.activation(out=lf[:, :], in_=lf[:, :], func=AF.Ln, bias=1.0), reads=[lfb_], writes=[lfb_])
                    p.op("dve", lambda e: e.tensor_scalar(out=lf[:, :], in0=lf[:, :], scalar1=-1.0, scalar2=None, op0=ALU.mult), reads=[lfb_], writes=[lfb_])
                    psA = c.ps(); psAb = p.buf()
                    psB = c.ps(); psBb = p.buf()
                    p.mm([lambda e: e.matmul(psA[:, 0:NCH], lhsT=triU[:, :], rhs=lf[:, :], start=True, stop=True)], reads=[triUb, lfb_], writes=[psAb])
                    p.mm([lambda e: e.matmul(psB[:, 0:NCH], lhsT=ones_f[:, :], rhs=lf[:, :], start=True, stop=True)], reads=[onesfb, lfb_], writes=[psBb])
                    p.op("dve", lambda e: e.tensor_tensor(out=bias_s[:, :], in0=iv[:, :], in1=psA[:, 0:NCH], op=ALU.subtract), reads=[ivb, psAb], writes=[biasb])
                    p.op("dve", lambda e: e.tensor_tensor(out=wk[:, :], in0=bias_s[:, :], in1=psB[:, 0:NCH], op=ALU.add), reads=[biasb, psBb], writes=[wkb])
                    p.op("act", lambda e: e.activation(out=wk[:, :], in_=wk[:, :], func=AF.Exp), reads=[wkb], writes=[wkb])
                    p.op("act", lambda e: e.activation(out=dec[:, :], in_=psB[:, 0:NCH], func=AF.Exp), reads=[psBb], writes=[decb])
                    p.op("dve", lambda e: e.memset(St[:, :], 0.0), writes=[Stb])
                    p.op("dve", lambda e: e.memset(St_bf2[0][:, :], 0.0), writes=[Stbfb2[0]])
                    lfrep = [c.sb([128, 128], F32) for _ in range(2)]; lfrepb = p.bufs(2)
                    DT = [c.sb([128, 128], F32) for _ in range(2)]; DTb = p.bufs(2)
                    WT = [c.sb([128, 128], BF16) for _ in range(2)]; WTb = p.bufs(2)
                    eG = [c.sb([64, 128], F32) for _ in range(2)]; eGb = p.bufs(2)
                    qs = [c.sb([64, 128], BF16) for _ in range(2)]; qsb = p.bufs(2)
                    ksc = [c.sb([128, 64], BF16) for _ in range(2)]; kscb = p.bufs(2)
                    dn = [c.sb([64, 128], F32) for _ in range(2)]; dnb = p.bufs(2)
                    hTm = [c.sb([64, 512], F32) for _ in range(2)]; hTmb = p.bufs(2)
                    sqm = c.sb([64, 512], BF16); sqmb = p.buf()
                    rstdm = c.sb([64, 512], F32); rstdmb = p.buf()
                    mo_sb = [c.sb([64, 512], BF16) for _ in range(2)]; mob = p.bufs(2)
                    mo_f = [c.sb([64, 512], F32) for _ in range(2)]; mofb = p.bufs(2)
                    ho = [c.sb([64, 512], F32) for _ in range(2)]; hob = p.bufs(2)
                    ho_bf = [c.sb([64, 512], BF16) for _ in range(2)]; hobfb = p.bufs(2)
                    psG = [c.ps() for _ in range(2)]; psGb = p.bufs(2)
                    psN = [c.ps() for _ in range(2)]; psNb = p.bufs(2)
                    psK = [c.ps() for _ in range(2)]; psKb = p.bufs(2)

                    def stage1(ch):
                        q = ch % 2
                        t0 = ch * 128
                        pcq = ch // 16
                        p.op("dve", lambda e: e.tensor_scalar(out=lfrep[q][:, :], in0=ones_f[:, :], scalar1=lf[:, ch:ch + 1], scalar2=None, op0=ALU.mult),
                             reads=[onesfb, lfb_], writes=[lfrepb[q]])
                        p.mm([lambda e: e.matmul(psG[q][:, 0:128], lhsT=lfrep[q][:, :], rhs=triU[:, :], start=True, stop=True),
                              lambda e: e.matmul(psG[q][:, 128:256], lhsT=kT_bf[:, t0:t0 + 128], rhs=qT_bf[:, t0:t0 + 128], start=True, stop=True),
                              lambda e: e.matmul(psK[q][:, 0:64], lhsT=kT_bf[:, t0:t0 + 128], rhs=ident_bf[0:64, 0:64], start=True, stop=True)],
                             reads=[lfrepb[q], triUb, kTb[pcq], qTb[pcq], identb], writes=[psGb[q], psKb[q]])
                        p.op("act", lambda e: e.activation(out=DT[q][:, :], in_=psG[q][:, 0:128], func=AF.Exp, bias=bias_s[:, ch:ch + 1]),
                             reads=[psGb[q], biasb], writes=[DTb[q]])
                        p.op("act", lambda e: e.activation(out=eG[q][:, :], in_=psG[q][0:64, 0:128], func=AF.Exp), reads=[psGb[q]], writes=[eGb[q]])
                        p.op("act", lambda e: e.activation(out=ksc[q][:, :], in_=psK[q][:, 0:64], func=AF.Copy, scale=wk[:, ch:ch + 1]),
                             reads=[psKb[q], wkb], writes=[kscb[q]])
                        p.op("dve", lambda e: e.tensor_tensor(out=DT[q][:, :], in0=DT[q][:, :], in1=triU[:, :], op=ALU.mult), reads=[DTb[q], triUb], writes=[DTb[q]])
                        p.op("dve", lambda e: e.tensor_tensor(out=qs[q][:, :], in0=eG[q][:, :], in1=qT_bf[:, t0:t0 + 128], op=ALU.mult),
                             reads=[eGb[q], qTb[pcq]], writes=[qsb[q]])
                        p.op("dve", lambda e: e.tensor_tensor(out=WT[q][:, :], in0=DT[q][:, :], in1=psG[q][:, 128:256], op=ALU.mult), reads=[DTb[q], psGb[q]], writes=[WTb[q]])
                        p.mm([lambda e: e.matmul(psK[q][0:64, 64:192], lhsT=ksc[q][:, :], rhs=vones[:, ch, :], start=True, stop=True)],
                             reads=[kscb[q], vonesb], writes=[psKb[q]])

                    def stage2(ch):
                        q = ch % 2
                        St_bf, Stbfb = St_bf2[q], Stbfb2[q]
                        St_nx, Stnxb = St_bf2[1 - q], Stbfb2[1 - q]
                        p.mm([lambda e: e.matmul(psN[q][0:64, 0:128], lhsT=vones[:, ch, 0:64], rhs=WT[q][:, :], start=True, stop=False),
                              lambda e: e.matmul(psN[q][0:64, 0:128], lhsT=St_bf[:, 0:64], rhs=qs[q][:, :], start=False, stop=True),
                              lambda e: e.matmul(psN[q][0:64, 128:256], lhsT=vones[:, ch, 64:128], rhs=WT[q][:, :], start=True, stop=False),
                              lambda e: e.matmul(psN[q][0:64, 128:256], lhsT=St_bf[:, 64:128], rhs=qs[q][:, :], start=False, stop=True)],
                             reads=[vonesb, WTb[q], Stbfb, qsb[q]], writes=[psNb[q]])
                        p.op("dve", lambda e: e.scalar_tensor_tensor(out=St[:, :], in0=St[:, :], scalar=dec[0:64, ch:ch + 1], in1=psK[q][0:64, 64:192],
                                                                     op0=ALU.mult, op1=ALU.add), reads=[Stb, decb, psKb[q]], writes=[Stb])
                        p.op("act", lambda e: e.copy(out=St_nx[:, :], in_=St[:, :]), reads=[Stb], writes=[Stnxb])
                        hq = (ch // 4) % 2
                        hc = (ch % 4) * 128
                        p.op("act", lambda e: e.activation(out=dn[q][:, :], in_=psN[q][0:64, 128:256], func=AF.Abs), reads=[psNb[q]], writes=[dnb[q]])
                        p.op("dve", lambda e: e.tensor_scalar(out=dn[q][:, :], in0=dn[q][:, :], scalar1=1.0, scalar2=None, op0=ALU.max), reads=[dnb[q]], writes=[dnb[q]])
                        p.op("dve", lambda e: e.reciprocal(out=dn[q][:, :], in_=dn[q][:, :]), reads=[dnb[q]], writes=[dnb[q]])
                        p.op("dve", lambda e: e.tensor_tensor(out=hTm[hq][:, hc:hc + 128], in0=psN[q][0:64, 0:128], in1=dn[q][:, :], op=ALU.mult),
                             reads=[psNb[q], dnb[q]], writes=[hTmb[hq]])
                        if ch % 4 == 3:
                            pci = ch // 4
                            pt0 = pci * 512
                            H_, Hb_ = hTm[hq], hTmb[hq]
                            gather("mo%d" % hq, mo_sb[hq][0:64, :], zall512, IDX_MO + pci, 64, [zall_cb[6]], [mob[hq]])
                            p.op("act", lambda e: e.activation(out=sqm[:, :], in_=H_[:, :], func=AF.Square), reads=[Hb_], writes=[sqmb])
                            p.mm([lambda e: e.matmul(psA[0:64, :], lhsT=ones_bf[0:64, 0:64], rhs=sqm[:, :], start=True, stop=True)], reads=[onesbb, sqmb], writes=[psAb])
                            p.op("act", lambda e: e.activation(out=rstdm[:, :], in_=psA[0:64, :], func=AF.Ln, bias=EPS, scale=1.0 / 64), reads=[psAb], writes=[rstdmb])
                            p.op("act", lambda e: e.activation(out=rstdm[:, :], in_=rstdm[:, :], func=AF.Exp, scale=-0.5), reads=[rstdmb], writes=[rstdmb])
                            p.op("act", lambda e: e.activation(out=mo_f[hq][:, :], in_=mo_sb[hq][:, :], func=AF.Sigmoid), reads=[mob[hq]], writes=[mofb[hq]])
                            p.op("dve", lambda e: e.scalar_tensor_tensor(out=ho[hq][:, :], in0=H_[:, :], scalar=prm_sb[0:64, 15:16], in1=rstdm[:, :],
                                                                         op0=ALU.mult, op1=ALU.mult), reads=[Hb_, prmb, rstdmb], writes=[hob[hq]])
                            p.op("dve", lambda e: e.tensor_tensor(out=ho_bf[hq][:, :], in0=ho[hq][:, :], in1=mo_f[hq][:, :], op=ALU.mult),
                                 reads=[hob[hq], mofb[hq]], writes=[hobfb[hq]])
                            out_piece("ho%d" % hq, ho_bf[hq][:, :], hobfb[hq], 64, 64, pt0)
                            if pci < 4:
                                ag_chunk(zsrc_t, zall_t, RZ, 7 + pci, zsrc_cb, zall_cb)
                            elif pci == 4:
                                ag_chunk(msrc_t, mall_t, RM, 0, msrc_cb, mall_cb)

                    stage1(0)
                    for ch in range(NCH):
                        if ch + 1 < NCH:
                            stage1(ch + 1)
                        stage2(ch)
                p.barrier(exclude=("cc",))
                fused_moba(nc, p, gather, out_piece, prm_sb, prmb, ident_bf, identb, ones_bf, onesbb, blk, blkb, caus, causb, vaug, vb,
                           ind_d, zall512, zall_cb, lambda i: ag_chunk(msrc_t, mall_t, RM, i, msrc_cb, mall_cb),
                           None)

        for l in range(nlayers):
            phase_A(l)
            p.barrier(exclude=("cc",))
            if upto < 2:
                continue
            phase_B(l)
            p.barrier(exclude=("cc",))
            if upto < 3:
                continue
            fused_C(nc, p, l, gather, x_sb, xb, onesD_bf, onesDb, w_out, g2, w_up, convw, w_down, zall16, zall_cb, mall512, mall2, mall_cb)
            p.barrier(exclude=("cc",))
        for k in range(8):
            p.dma("sp", "out", lambda e, k=k: e.dma_start(out=xoT[k * 128:(k + 1) * 128, :], in_=x_sb[:, k, 2:TQH]), reads=[xb[k]], writes=[outb])
        p.finish("sp", [outb])
        p.emit()
    return nc


def fused_moba(nc, p, gather, out_piece, prm_sb, prmb, ident_bf, identb, ones_bf, onesbb, blk, blkb, caus, causb, vaug, vb, ind_d, zall512, zall_cb, ag2, agz24):
    NP = T // 512
    with ExitStack() as st:
        c = Ctx(nc, st, p)
        Kaug = [c.sb([128, T], BF16) for _ in range(2)]; Kb = [p.bufs(NP, "K%d" % h) for h in range(2)]
        Qaug = [c.sb([128, T], BF16) for _ in range(2)]; Qb = [p.bufs(NP, "Q%d" % h) for h in range(2)]
        kms = c.sb([128, 64], F32); kmsb = p.buf()
        AUX = ((64, 96), (0, 32))
        DAT = ((0, 64), (64, 128))
        for h in range(2):
            p.op("dve", lambda e, h=h: e.memset(Kaug[h][:, :], 0.0), writes=Kb[h])
            p.op("pool", lambda e, h=h: e.memset(Qaug[h][:, :], 0.0), writes=Qb[h])
        for h in range(2):
            a0, a1 = AUX[h]
            for q in range(T // 2048):
                p.dma("pool", "ind%d" % h, lambda e, h=h, a0=a0, a1=a1, q=q: e.dma_start(out=Kaug[h][a0:a1, q * 2048:(q + 1) * 2048],
                                                                                       in_=ind_d[:, q * 2048:(q + 1) * 2048]), writes=Kb[h])
        p.op("dve", lambda e: e.memset(kms[:, :], 0.0), writes=[kmsb])
        with ExitStack() as st2:
            c2 = Ctx(nc, st2, p)
            xin = [c2.sb([128, 512], BF16) for _ in range(2)]; xinb = p.bufs(2)
            sq = c2.sb([128, 512], BF16); sqb = p.buf()
            rstd = c2.sb([128, 512], F32); rstdb = p.buf()
            xn = [c2.sb([128, 512], F32) for _ in range(2)]; xnb = p.bufs(2)
            gsb = [c2.sb([128, 64], F32) for _ in range(2)]; gsbb = p.bufs(2)
            top8 = [c2.sb([128, 16], F32) for _ in range(2)]; top8b = p.bufs(2)
            nm = [c2.sb([128, 4, 128], BF16) for _ in range(2)]; nmb = p.bufs(2)
            psM = c2.ps(); psMb = p.buf()
            psGt = [c2.ps() for _ in range(2)]; psGtb = p.bufs(2)
            psT = [c2.ps() for _ in range(2)]; psTb = p.bufs(2)
            for q in range(2):
                p.op("pool", lambda e, q=q: e.memset(nm[q][:, :, :], 0.0), writes=[nmb[q]])
            xnq = [c2.sb([128, 512], F32) for _ in range(2)]; xnqb = p.bufs(2)
            cnt = {"in": 0, "g": 0}

            def normstage(pc):
                t0 = pc * 512
                for which in range(2):
                    icol = IDX_AK if which == 0 else IDX_AQ
                    zcb = zall_cb[9:11] if which == 0 else zall_cb[7:9]
                    gcol = 17 if which == 0 else 16
                    X, Xb = xin[cnt["in"] % 2], xinb[cnt["in"] % 2]
                    gather("ax%d" % (cnt["in"] % 2), X[:, :], zall512, icol + pc, 128, zcb, [Xb])
                    cnt["in"] += 1
                    p.op("act", lambda e, X=X: e.activation(out=sq[:, :], in_=X[:, :], func=AF.Square), reads=[Xb], writes=[sqb])
                    p.mm([lambda e: e.matmul(psM[:, :], lhsT=blk[:, :], rhs=sq[:, :], start=True, stop=True)], reads=[blkb, sqb], writes=[psMb])
                    p.op("act", lambda e: e.activation(out=rstd[:, :], in_=psM[:, :], func=AF.Ln, bias=EPS, scale=1.0 / 64), reads=[psMb], writes=[rstdb])
                    p.op("act", lambda e: e.activation(out=rstd[:, :], in_=rstd[:, :], func=AF.Exp, scale=-0.5), reads=[rstdb], writes=[rstdb])
                    if which == 0:
                        N_, Nb_ = xn[0], xnb[0]
                    else:
                        N_, Nb_ = xnq[pc % 2], xnqb[pc % 2]
                    p.op("dve", lambda e, X=X, N_=N_, gcol=gcol: e.scalar_tensor_tensor(out=N_[:, :], in0=X[:, :], scalar=prm_sb[:, gcol:gcol + 1], in1=rstd[:, :],
                                                                 op0=ALU.mult, op1=ALU.mult), reads=[Xb, prmb, rstdb], writes=[Nb_])
                    dst = Kaug if which == 0 else Qaug
                    dstb = Kb if which == 0 else Qb
                    for h in range(2):
                        d0, d1 = DAT[h]
                        p.op("act", lambda e, h=h, d0=d0, d1=d1, dst=dst, N_=N_: e.copy(out=dst[h][d0:d1, t0:t0 + 512], in_=N_[d0:d1, :]),
                             reads=[Nb_], writes=[dstb[h][pc]])
                    if which == 0:
                        for h in range(2):
                            p.op("dve", lambda e, h=h, N_=N_: e.tensor_reduce(
                                out=kms[h * 64:(h + 1) * 64, h * 32 + 2 * pc:h * 32 + 2 * pc + 2],
                                in_=N_[h * 64:(h + 1) * 64, :].rearrange("q (b k) -> q b k", k=256), axis=AX.X, op=ALU.add), reads=[Nb_], writes=[kmsb])
                if pc == 1:
                    ag2(1)

            def gatestage(pc):
                t0 = pc * 512
                QN, QNb = xnq[pc % 2], xnqb[pc % 2]
                nq = pc % 2
                for qb in range(4):
                    own = 2 * pc + qb // 2
                    gq = cnt["g"] % 2; cnt["g"] += 1
                    p.mm([lambda e, gq=gq, qb=qb: e.matmul(psGt[gq][:, 0:64], lhsT=QN[:, qb * 128:(qb + 1) * 128], rhs=kms[:, :], start=True, stop=True)],
                         reads=[QNb, kmsb], writes=[psGtb[gq]])
                    p.op("dve", lambda e, gq=gq: e.memset(gsb[gq][:, :], -1e30), writes=[gsbb[gq]])
                    if own > 0:
                        p.op("dve", lambda e, gq=gq, own=own: e.tensor_copy(out=gsb[gq].rearrange("q (h n) -> q h n", h=2)[:, :, 0:own],
                                                                            in_=psGt[gq][:, 0:64].rearrange("q (h n) -> q h n", h=2)[:, :, 0:own]),
                             reads=[psGtb[gq]], writes=[gsbb[gq]])
                    for h in range(2):
                        p.op("dve", lambda e, gq=gq, h=h: e.max(out=top8[gq][:, h * 8:(h + 1) * 8], in_=gsb[gq][:, h * 32:(h + 1) * 32]),
                             reads=[gsbb[gq]], writes=[top8b[gq]])
                    for h in range(2):
                        p.op("dve", lambda e, gq=gq, h=h: e.tensor_scalar(out=gsb[gq][:, h * 32:(h + 1) * 32], in0=gsb[gq][:, h * 32:(h + 1) * 32],
                                                                          scalar1=top8[gq][:, h * 8 + 2:h * 8 + 3], scalar2=None, op0=ALU.is_ge),
                             reads=[gsbb[gq], top8b[gq]], writes=[gsbb[gq]])
                    for h in range(2):
                        cc0 = 64 if h == 0 else 96
                        p.op("dve", lambda e, gq=gq, h=h, cc0=cc0, qb=qb: e.tensor_scalar(out=nm[nq][:, qb, cc0:cc0 + 32], in0=gsb[gq][:, h * 32:(h + 1) * 32],
                                                                                      scalar1=-1.0, scalar2=-NEG, op0=ALU.add, op1=ALU.mult),
                             reads=[gsbb[gq]], writes=[nmb[nq]])
                        p.op("dve", lambda e, cc0=cc0, qb=qb, own=own: e.memset(nm[nq][:, qb, cc0 + own:cc0 + own + 1], 0.0), writes=[nmb[nq]])
                        if own < 31:
                            p.op("dve", lambda e, cc0=cc0, qb=qb, own=own: e.memset(nm[nq][:, qb, cc0 + own + 1:cc0 + 32], NEG), writes=[nmb[nq]])

            def transstage(pc):
                t0 = pc * 512
                nq = pc % 2
                tq = pc % 2
                p.mm([lambda e, qb=qb: e.matmul(psT[tq][0:96, qb * 128:(qb + 1) * 128], lhsT=nm[nq][:, qb, 0:96], rhs=ident_bf[:, :], start=True, stop=True)
                      for qb in range(4)], reads=[nmb[nq], identb], writes=[psTb[tq]])
                p.op("act", lambda e: e.copy(out=Qaug[0][64:96, t0:t0 + 512], in_=psT[tq][64:96, :]), reads=[psTb[tq]], writes=[Qb[0][pc]])
                p.mm([lambda e, qb=qb: e.matmul(psT[tq][0:32, qb * 128:(qb + 1) * 128], lhsT=nm[nq][:, qb, 96:128], rhs=ident_bf[:, :], start=True, stop=True)
                      for qb in range(4)], reads=[nmb[nq], identb], writes=[psTb[tq]])
                p.op("act", lambda e: e.copy(out=Qaug[1][0:32, t0:t0 + 512], in_=psT[tq][0:32, :]), reads=[psTb[tq]], writes=[Qb[1][pc]])

            normstage(0)
            for pc in range(NP):
                if pc + 1 < NP:
                    normstage(pc + 1)
                if pc > 0:
                    transstage(pc - 1)
                gatestage(pc)
            transstage(NP - 1)
        p.barrier(exclude=("cc",))
        with ExitStack() as st3:
            c3 = Ctx(nc, st3, p)
            NPT = 3
            NPS = 2
            PT = [c3.sb([128, 1024], BF16) for _ in range(NPT)]; PTb = p.bufs(NPT)
            psS = [c3.ps([128, 1024]) for _ in range(NPS)]; psSb = p.bufs(NPS)
            psO = [c3.ps() for _ in range(2)]; psOb = p.bufs(2)
            psR = [c3.ps() for _ in range(2)]; psRb = p.bufs(2)
            rr = [c3.sb([64, 512], F32) for _ in range(2)]; rrb = p.bufs(2)
            ao = [c3.sb([64, 512], BF16) for _ in range(2)]; aob = p.bufs(2)
            tiles = []
            for pc in range(NP):
                for h in range(2):
                    for kt in range(0, 4 * (pc + 1), 2):
                        tiles.append((pc, h, kt))

            def stage1(i):
                pc, h, kt0 = tiles[i]
                rows = 96 if h == 0 else 128
                t0 = pc * 512
                sq_ = i % NPS
                pq = i % NPT
                p.mm([lambda e, u=u: e.matmul(psS[sq_][:, u * 512:(u + 1) * 512], lhsT=Kaug[h][0:rows, (kt0 + u) * 128:(kt0 + u + 1) * 128],
                                              rhs=Qaug[h][0:rows, t0:t0 + 512], start=True, stop=True) for u in range(2)],
                     reads=[Kb[h][kt0 // 4], Qb[h][pc]], writes=[psSb[sq_]])
                p.op("act", lambda e: e.activation(out=PT[pq][:, :], in_=psS[sq_][:, :], func=AF.Exp, scale=0.125),
                     reads=[psSb[sq_]], writes=[PTb[pq]])
                if kt0 >= 4 * pc:
                    r = kt0 - 4 * pc
                    p.op("dve", lambda e: e.tensor_tensor(out=PT[pq][:, :], in0=PT[pq][:, :], in1=caus[:, r * 512:(r + 2) * 512], op=ALU.mult),
                         reads=[PTb[pq], causb], writes=[PTb[pq]])

            def stage2(i):
                pc, h, kt0 = tiles[i]
                nkt = 4 * (pc + 1)
                t0 = pc * 512
                pq = i % NPT
                oq = (2 * pc + h) % 2
                p.mm([lambda e, u=u: e.matmul(psO[oq][0:64, :], lhsT=vaug[:, kt0 + u, h * 64:(h + 1) * 64], rhs=PT[pq][:, u * 512:(u + 1) * 512],
                                              start=(kt0 + u == 0), stop=(kt0 + u == nkt - 1)) for u in range(2)],
                     reads=[vb, PTb[pq]], writes=[psOb[oq]])
                p.mm([lambda e, u=u: e.matmul(psR[oq][0:64, :], lhsT=ones_bf[:, 0:64], rhs=PT[pq][:, u * 512:(u + 1) * 512],
                                              start=(kt0 + u == 0), stop=(kt0 + u == nkt - 1)) for u in range(2)],
                     reads=[onesbb, PTb[pq]], writes=[psRb[oq]])
                if kt0 + 2 == nkt:
                    p.op("dve", lambda e: e.reciprocal(out=rr[oq][:, :], in_=psR[oq][0:64, :]), reads=[psRb[oq]], writes=[rrb[oq]])
                    p.op("dve", lambda e: e.tensor_tensor(out=ao[oq][:, :], in0=psO[oq][0:64, :], in1=rr[oq][:, :], op=ALU.mult),
                         reads=[psOb[oq], rrb[oq]], writes=[aob[oq]])
                    out_piece("ao%d" % oq, ao[oq][:, :], aob[oq], 128 + h * 64, 64, t0)
                    if h == 1 and pc == 7:
                        ag2(2)

            LOOK = 1
            n = len(tiles)
            for i in range(min(LOOK, n)):
                stage1(i)
            for i in range(n):
                if i + LOOK < n:
                    stage1(i + LOOK)
                stage2(i)
            ag2(3)
            ag2(4)


def fused_C(nc, p, l, gather, x_sb, xb, ones_bf, onesb, w_out, g2, w_up, convw, w_down, zall16, zall_cb, mall512, mall2, mall_cb):
    with ExitStack() as st:
        c = Ctx(nc, st, p)
        wo_sb = c.sb([128, 8, D], BF16); wob = p.buf()
        wd_sb = c.sb([128, 22, D], BF16); wdb = p.buf()
        g_sb = c.sb([128, 8], F32); gb = p.buf()
        cw_sb = c.sb([128, 44, 3], F32); cwb = p.buf()
        xt = c.sb([128, 16], F32); xtb = p.buf()
        mh = c.sb([128, 8, 2], BF16); mhb = p.buf()
        mixb = [c.sb([128, 8, 512], BF16) for _ in range(2)]; mixbb = p.bufs(2)
        NWU = 3
        wu = [c.sb([128, 8, 256], BF16) for _ in range(NWU)]; wub = p.bufs(NWU)
        act = c.sb([128, 22, 512], BF16); actb = p.bufs(22, "act")
        hT = c.sb([128, 8, 512], BF16); hb = p.bufs(8, "h")
        rstd = c.sb([128, 512], F32); rstdb = p.buf()
        carry = c.sb([128, 22, 2, 2], F32); carryb = p.bufs(22, "cy")
        u2 = [c.sb([128, 2, 514], F32) for _ in range(2)]; u2b = p.bufs(2)
        tg = [c.sb([128, 512], F32) for _ in range(2)]; tgb = p.bufs(2)
        tv = [c.sb([128, 512], F32) for _ in range(2)]; tvb = p.bufs(2)
        psn = c.ps(); psnb = p.buf()
        psgv = [c.ps([128, 1024]) for _ in range(2)]; psgvb = p.bufs(2)
        pso = [c.ps() for _ in range(3)]; psob = p.bufs(3)
        p.dma("sp", "g", lambda e: e.dma_start(out=g_sb[:, :], in_=g2[l]), writes=[gb])
        p.dma("sp", "cw", lambda e: e.dma_start(out=cw_sb[:, :, :], in_=convw[l]), writes=[cwb])
        for k in range(8):
            p.dma("pool", "wo", lambda e, k=k: e.dma_start(out=wo_sb[:, k, :], in_=w_out[l, k * 128:(k + 1) * 128, :]), writes=[wob])
        p.op("dve", lambda e: e.memset(carry[:, :, :], 0.0), writes=list(carryb))
        gather("xt", xt[:, :], zall16, IDX_XT, 128, [zall_cb[11]], [xtb])
        p.op("act", lambda e: e.copy(out=x_sb[:, :, 0:2], in_=xt.rearrange("q (k t) -> q k t", t=2)), reads=[xtb], writes=list(xb))
        wd_loaded = False
        i_o = 0
        i_u = 0
        for ti, (col0, n) in enumerate(C_TILES):
            mb, mbb = mixb[ti % 2], mixbb[ti % 2]
            if ti == 0:
                for k in range(8):
                    gather("mh", mh[:, k, :], mall2, IDX_MH + k, 128, [mall_cb[4]], [mhb])
                p.op("act", lambda e, mb=mb: e.copy(out=mb[:, :, 0:2], in_=mh[:, :, :]), reads=[mhb], writes=[mbb])
            else:
                for k in range(8):
                    gather("mix%d" % (ti % 2), mb[:, k, :], mall512, IDX_MX + k * 4 + (ti - 1), 128, ([mall_cb[0]] if k < 2 else [mall_cb[1]] if k < 4 else mall_cb[2:4]), [mbb])
            for oc in range(8):
                ps, psb = pso[i_o % 3], psob[i_o % 3]; i_o += 1
                p.mm([lambda e, k=k, oc=oc, ps=ps, mb=mb, n=n: e.matmul(ps[:, :n], lhsT=wo_sb[:, k, oc * 128:(oc + 1) * 128], rhs=mb[:, k, :n],
                                                                       start=(k == 0), stop=(k == 7)) for k in range(8)],
                     reads=[wob, mbb], writes=[psb])
                p.op("dve", lambda e, oc=oc, ps=ps, col0=col0, n=n: e.tensor_tensor(
                    out=x_sb[:, oc, col0:col0 + n], in0=x_sb[:, oc, col0:col0 + n], in1=ps[:, :n], op=ALU.add),
                    reads=[psb, xb[oc]], writes=[xb[oc]])
            rmsnorm_fm(p, c, x_sb, None, col0, n, g_sb, gb, ones_bf, onesb, hT, hb, act, actb[:8], psn, psnb, rstd, rstdb, xbs=xb)
            if not wd_loaded:
                for fc in range(22):
                    p.dma("pool", "wd", lambda e, fc=fc: e.dma_start(out=wd_sb[:, fc, :], in_=w_down[l, fc * 128:(fc + 1) * 128, :]), writes=[wdb])
                wd_loaded = True
            for fc in range(22):
                q = i_u % 2
                w = i_u % NWU
                i_u += 1
                p.dma("pool", "wu%d" % w, lambda e, w=w, fc=fc: e.dma_start(out=wu[w][:, :, :], in_=w_up[l, fc]), writes=[wub[w]])
                p.mm([lambda e, k=k, q=q, w=w, n=n: e.matmul(psgv[q][:, 0:n], lhsT=wu[w][:, k, 0:128], rhs=hT[:, k, :n], start=(k == 0), stop=(k == 7))
                      for k in range(8)] +
                     [lambda e, k=k, q=q, w=w, n=n: e.matmul(psgv[q][:, 512:512 + n], lhsT=wu[w][:, k, 128:256], rhs=hT[:, k, :n], start=(k == 0), stop=(k == 7))
                      for k in range(8)], reads=[wub[w]] + list(hb), writes=[psgvb[q]])
                U, Ub = u2[q], u2b[q]
                p.op("act", lambda e, U=U, fc=fc: e.copy(out=U[:, :, 0:2], in_=carry[:, fc, :, :]), reads=[carryb[fc]], writes=[Ub])
                p.op("act", lambda e, U=U, q=q, n=n: e.copy(out=U[:, :, 2:2 + n], in_=psgv[q].rearrange("p (h w) -> p h w", h=2)[:, :, 0:n]),
                     reads=[psgvb[q]], writes=[Ub])
                p.op("act", lambda e, U=U, fc=fc, n=n: e.copy(out=carry[:, fc, :, :], in_=U[:, :, n:n + 2]), reads=[Ub], writes=[carryb[fc]])
                for (hh, t_, tb_, ch) in ((0, tg[q], tgb[q], fc), (1, tv[q], tvb[q], 22 + fc)):
                    p.op("dve", lambda e, U=U, t_=t_, hh=hh, ch=ch, n=n: e.tensor_scalar(
                        out=t_[:, :n], in0=U[:, hh, 0:n], scalar1=cw_sb[:, ch, 0:1], scalar2=None, op0=ALU.mult), reads=[Ub, cwb], writes=[tb_])
                for jj in (1, 2):
                    for (hh, t_, tb_, ch) in ((0, tg[q], tgb[q], fc), (1, tv[q], tvb[q], 22 + fc)):
                        p.op("dve", lambda e, U=U, t_=t_, hh=hh, ch=ch, n=n, jj=jj: e.scalar_tensor_tensor(
                            out=t_[:, :n], in0=U[:, hh, jj:jj + n], scalar=cw_sb[:, ch, jj:jj + 1], in1=t_[:, :n], op0=ALU.mult, op1=ALU.add),
                            reads=[Ub, cwb, tb_], writes=[tb_])
                p.op("act", lambda e, q=q, n=n: e.activation(out=tg[q][:, :n], in_=tg[q][:, :n], func=AF.Silu), reads=[tgb[q]], writes=[tgb[q]])
                p.op("dve", lambda e, q=q, fc=fc, n=n: e.tensor_tensor(out=act[:, fc, :n], in0=tg[q][:, :n], in1=tv[q][:, :n], op=ALU.mult),
                     reads=[tgb[q], tvb[q]], writes=[actb[fc]])
            for oc in range(8):
                ps, psb = pso[i_o % 3], psob[i_o % 3]; i_o += 1
                p.mm([lambda e, fc=fc, oc=oc, ps=ps, n=n: e.matmul(ps[:, :n], lhsT=wd_sb[:, fc, oc * 128:(oc + 1) * 128], rhs=act[:, fc, :n],
                                                                   start=(fc == 0), stop=(fc == 21)) for fc in range(22)],
                     reads=[wdb] + list(actb), writes=[psb])
                p.op("dve", lambda e, oc=oc, ps=ps, col0=col0, n=n: e.tensor_tensor(
                    out=x_sb[:, oc, col0:col0 + n], in0=x_sb[:, oc, col0:col0 + n], in1=ps[:, :n], op=ALU.add),
                    reads=[psb, xb[oc]], writes=[xb[oc]])


def fused_w_in_cols():
    fm, _ = w_in_cols()
    tm = list(range(768, 1024)) + list(range(2312, 2824))
    for g in range(4):
        tm += [1280 + g, 1284 + g]
    return np.array(fm), np.array(tm)


def ag_row(r, rho, nrows):
    rho = np.asarray(rho, dtype=np.int64)
    ch = rho // CHR
    n = np.minimum(CHR, nrows - ch * CHR)
    return 4 * ch * CHR + r * n + (rho - ch * CHR)


def fused_idx(cidx):
    r_ = cidx % 4
    g = j = r_
    pp = np.arange(128, dtype=np.int64)
    idx = np.zeros((128, NIDX), np.int64)
    gp = g * 128 + pp
    for r in range(4):
        idx[:, IDX_TM + r] = ag_row(r, gp // 2, RZ) * 2 + gp % 2
        idx[:, IDX_AV + r] = ag_row(r, 256 + gp, RZ)
        idx[:, IDX_GT + r] = (r * RF + gp // 64) * 64 + gp % 64
        idx[:, IDX_POOL + r] = ag_row(r, 768 + g * 64 + (pp % 64), RZ)
        idx[:, IDX_MQ + r] = ag_row(r, 768 + 256 + g * 64 + (pp % 64), RZ)
        idx[:, IDX_MK + r] = ag_row(r, 768 + 512 + g * 64 + (pp % 64), RZ)
    for pc in range(16):
        r, f = pc // 4, pc % 4
        idx[:, IDX_MO + pc] = ag_row(r, 768 + 768 + g * 64 + (pp % 64), RZ) * 4 + f
        idx[:, IDX_AQ + pc] = ag_row(r, 768 + 1024 + g * 128 + pp, RZ) * 4 + f
        idx[:, IDX_AK + pc] = ag_row(r, 768 + 1536 + g * 128 + pp, RZ) * 4 + f
    if j > 0:
        idx[:, IDX_XT] = ((j - 1) * RF + 8) * 128 + pp
    else:
        idx[:, IDX_XT] = (0 * RF + 9) * 128 + pp
    for k in range(8):
        f = k * 128 + pp
        gq = np.where(f < 256, f // 64, np.where(f < 512, (f - 256) // 64, (f - 512) // 128))
        rr = np.where(f < 256, f % 64, np.where(f < 512, 64 + (f - 256) % 64, 128 + (f - 512) % 128))
        srow = np.where(rr < 128, rr * 4 + j, 512 + j * 128 + (rr - 128))
        row = np.array([ag_row(int(a), int(b), RM) for a, b in zip(gq, srow)])
        hrow = np.array([ag_row(int(a), 1024, RM) for a in gq])
        for q in range(4):
            idx[:, IDX_MX + k * 4 + q] = row * 4 + q
        idx[:, IDX_MH + k] = hrow * 1024 + rr * 4 + j
    return idx.astype(np.uint32)


_FUSED = {}


def kernel(**inputs):
    inp = {k: np.ascontiguousarray(np.asarray(v), dtype=np.float32) for k, v in inputs.items()}
    x = inp["x"]
    fm, tm = fused_w_in_cols()
    wfm = np.ascontiguousarray(inp["w_in"][:, :, fm])
    wtm = np.ascontiguousarray(inp["w_in"][:, :, tm])
    g1 = np.ascontiguousarray(inp["ln1_g"].reshape(4, 8, 128).transpose(0, 2, 1))
    g2 = np.ascontiguousarray(inp["ln2_g"].reshape(4, 8, 128).transpose(0, 2, 1))
    cw = np.ascontiguousarray(inp["ffn_conv"].reshape(4, 3, 44, 128).transpose(0, 3, 2, 1))
    triU, ident, caus, ind = static_consts()
    w_up_p = np.ascontiguousarray(inp["w_up"].reshape(4, 8, 128, 2, 22, 128).transpose(0, 4, 2, 1, 3, 5).reshape(4, 22, 128, 8, 256))
    maps = []
    for cidx in range(NCORE):
        b, g = cidx // 4, cidx % 4
        j = g
        w = POOL_WINDOWS[g]
        prm = np.zeros((4, 128, NPRM), np.float32)
        for l in range(4):
            prm[l, 0:64, 0] = inp["pool_scale"][l, g * 64:(g + 1) * 64]
            prm[l, :, 1 + g] = 1.0 / w
            prm[l, 0:64, 5:9] = inp["m_conv"][l][:, g * 64:(g + 1) * 64].T
            prm[l, 0:64, 9:13] = inp["m_conv"][l][:, 256 + g * 64:256 + (g + 1) * 64].T
            prm[l, :, 13] = inp["m_b_i"][l, g]
            prm[l, :, 14] = inp["m_b_f"][l, g]
            prm[l, 0:64, 15] = inp["m_norm_g"][l, g * 64:(g + 1) * 64]
            prm[l, :, 16] = np.tile(inp["a_q_g"][l], 2)
            prm[l, :, 17] = np.tile(inp["a_k_g"][l], 2)
            prm[l, :, 19:35] = (w / np.minimum(np.arange(16) + 1, w)).astype(np.float32)[None, :]
        maps.append({
            "xT": np.ascontiguousarray(x[b, j * TQ:(j + 1) * TQ, :].T),
            "wfm": wfm, "wtm": wtm, "g1": g1, "w_out": inp["w_out"], "g2": g2, "w_up": w_up_p, "convw": cw, "w_down": inp["w_down"],
            "prm": prm, "poolw": np.ascontiguousarray(inp["pool_w"][:, g]),
            "triU": triU, "ident": ident, "caus": caus, "ind": ind, "idx": fused_idx(cidx),
        })
    if "nc" not in _FUSED:
        _FUSED["nc"] = build_fused()
    res = run_bass_kernel_spmd(_FUSED["nc"], maps, core_ids=list(range(NCORE))).results
    out = np.empty_like(x)
    for cidx in range(NCORE):
        b, j = cidx // 4, cidx % 4
        out[b, j * TQ:(j + 1) * TQ, :] = res[cidx]["xoT"].T
    return out
```

```python
from contextlib import ExitStack
import numpy as np
import concourse.bass as bass
import concourse.mybir as mybir
from concourse.bass_utils import run_bass_kernel_spmd

F32 = mybir.dt.float32
BF16 = mybir.dt.bfloat16
AF = mybir.ActivationFunctionType
ALU = mybir.AluOpType
AX = mybir.AxisListType

D = 1024
T = 8192
NCORE = 8
TQ = 2048
DFF = 2816
EPS = 1e-6
NEG = -10000.0


class Buf:
    __slots__ = ("name", "w", "r")

    def __init__(self, name):
        self.name = name
        self.w = None
        self.r = {}


class Prog:
    ENG = ("pe", "act", "dve", "pool", "sp")

    def __init__(self, nc, same_engine_sync=True):
        self.nc = nc
        self.lists = {e: [] for e in self.ENG}
        self.cnt = {}
        self.seen = {e: {} for e in self.ENG}
        self.same = same_engine_sync
        self.nbuf = 0

    def buf(self, name=None):
        self.nbuf += 1
        return Buf(name or f"b{self.nbuf}")

    def bufs(self, n, name="b"):
        return [self.buf(f"{name}{i}") for i in range(n)]

    def _dep(self, eng, tok):
        if tok is None:
            return
        k, v = tok
        if k == eng and (eng == "pe" or not self.same):
            return
        if self.seen[eng].get(k, 0) >= v:
            return
        self.seen[eng][k] = v
        self.lists[eng].append(("wait", k, v))

    def _issue(self, eng, key, inc, fn, reads, writes):
        for b in reads:
            self._dep(eng, b.w)
        for b in writes:
            self._dep(eng, b.w)
            for k, v in b.r.items():
                self._dep(eng, (k, v))
        self.cnt[key] = self.cnt.get(key, 0) + inc
        tok = (key, self.cnt[key])
        self.lists[eng].append(("op", fn, key, inc))
        for b in reads:
            if b.r.get(key, 0) < tok[1]:
                b.r[key] = tok[1]
        for b in writes:
            b.w = tok
            b.r = {}
        return tok

    def op(self, eng, fn, reads=(), writes=()):
        return self._issue(eng, eng, 1, fn, reads, writes)

    def dma(self, eng, ch, fn, reads=(), writes=()):
        return self._issue(eng, "d_" + ch, 16, fn, reads, writes)

    def cc(self, fn, reads=(), writes=()):
        return self._issue("pool", "cc", 1, fn, reads, writes)

    def mm(self, fns, reads=(), writes=()):
        for b in reads:
            self._dep("pe", b.w)
        for b in writes:
            self._dep("pe", b.w)
            for k, v in b.r.items():
                self._dep("pe", (k, v))
        for fn in fns[:-1]:
            self.lists["pe"].append(("op", fn, None, 0))
        self.cnt["pe"] = self.cnt.get("pe", 0) + 1
        tok = ("pe", self.cnt["pe"])
        self.lists["pe"].append(("op", fns[-1], "pe", 1))
        for b in reads:
            if b.r.get("pe", 0) < tok[1]:
                b.r["pe"] = tok[1]
        for b in writes:
            b.w = tok
            b.r = {}
        return tok

    def finish(self, eng, bufs):
        for b in bufs:
            self._dep(eng, b.w)
        self.finish_all(eng)

    def barrier(self, exclude=()):
        for e in self.ENG:
            for k, v in list(self.cnt.items()):
                if k not in exclude:
                    self._dep(e, (k, v))

    def wait_dma_all(self, eng):
        for k, v in list(self.cnt.items()):
            if k.startswith("d_"):
                self._dep(eng, (k, v))

    def finish_all(self, eng):
        for k, v in list(self.cnt.items()):
            if k.startswith("d_"):
                self._dep(eng, (k, v))

    def emit(self):
        nc = self.nc
        keys = sorted(self.cnt.keys())
        with ExitStack() as st:
            sems = {k: st.enter_context(nc.semaphore("s_" + k)) for k in keys}
            block = st.enter_context(nc.Block())

            def run(e, lst):
                for it in lst:
                    if it[0] == "wait":
                        e.wait_ge(sems[it[1]], it[2])
                    else:
                        ins = it[1](e)
                        if it[3]:
                            if it[2].startswith("cc"):
                                ins.then_inc(sems[it[2]])
                            else:
                                ins.then_inc(sems[it[2]], it[3])

            lists = self.lists

            @block.tensor
            def _(e):
                run(e, lists["pe"])

            @block.scalar
            def _(e):
                run(e, lists["act"])

            @block.vector
            def _(e):
                run(e, lists["dve"])

            @block.gpsimd
            def _(e):
                run(e, lists["pool"])

            @block.sync
            def _(e):
                run(e, lists["sp"])


class Ctx:
    N = [0]

    def __init__(self, nc, st, p):
        self.nc, self.st, self.p = nc, st, p

    def sb(self, shape, dt, name=None):
        Ctx.N[0] += 1
        return self.st.enter_context(self.nc.sbuf_tensor(name or f"sb{Ctx.N[0]}", list(shape), dt))

    def ps(self, shape=(128, 512), dt=F32, name=None):
        Ctx.N[0] += 1
        return self.st.enter_context(self.nc.psum_tensor(name or f"ps{Ctx.N[0]}", list(shape), dt))

    def dram_in(self, name, shape, dt=F32):
        return self.nc.dram_tensor(name, list(shape), dt, kind="ExternalInput").ap()

    def dram_out(self, name, shape, dt=F32):
        return self.nc.dram_tensor(name, list(shape), dt, kind="ExternalOutput").ap()


def rmsnorm_fm(p, c, x_sb, xb, col0, n, g_sb, gb, ones_bf, onesb, hT, hb, sq, sqb, ps, psb, rstd, rstdb, xbs=None):
    if xbs is None:
        xbs = [xb] * 8
    for k in range(8):
        p.op("act", lambda e, k=k: e.activation(out=sq[:, k, :n], in_=x_sb[:, k, col0:col0 + n], func=AF.Square),
             reads=[xbs[k]], writes=[sqb[k]])
    p.mm([lambda e, k=k: e.matmul(ps[:, :n], lhsT=ones_bf[:, :], rhs=sq[:, k, :n], start=(k == 0), stop=(k == 7))
          for k in range(8)], reads=[onesb] + list(sqb), writes=[psb])
    p.op("act", lambda e: e.activation(out=rstd[:, :n], in_=ps[:, :n], func=AF.Ln, bias=EPS), reads=[psb], writes=[rstdb])
    p.op("act", lambda e: e.activation(out=rstd[:, :n], in_=rstd[:, :n], func=AF.Exp, scale=-0.5), reads=[rstdb], writes=[rstdb])
    for k in range(8):
        p.op("dve", lambda e, k=k: e.scalar_tensor_tensor(out=hT[:, k, :n], in0=x_sb[:, k, col0:col0 + n],
                                                          scalar=g_sb[:, k:k + 1], in1=rstd[:, :n],
                                                          op0=ALU.mult, op1=ALU.mult),
             reads=[xbs[k], gb, rstdb], writes=[hb[k]])


NFM = 2048
NTM = 776


def build_A():
    nc = bass.Bass("TRN2", target_bir_lowering=False)
    p = Prog(nc)
    with ExitStack() as st:
        c = Ctx(nc, st, p)
        xT = c.dram_in("xT", [D, TQ])
        wfm = c.dram_in("wfm", [D, NFM])
        wtm = c.dram_in("wtm", [D, NTM])
        g1 = c.dram_in("g1", [128, 8])
        zfm = c.dram_out("zfm", [NFM, TQ])
        ztm = c.dram_out("ztm", [TQ, NTM])

        x_sb = c.sb([128, 8, TQ], F32); xb = p.buf()
        wfm_sb = c.sb([128, 8, NFM], BF16); wfmb = p.buf()
        wtm_sb = c.sb([128, 8, NTM], BF16); wtmb = p.buf()
        g_sb = c.sb([128, 8], F32); gb = p.buf()
        ones_bf = c.sb([128, 128], BF16); onesb = p.buf()
        sq = c.sb([128, 8, 512], BF16); sqb = p.bufs(8)
        hT = c.sb([128, 8, 512], BF16); hb = p.bufs(8)
        rstd = c.sb([128, 512], F32); rstdb = p.buf()
        NST = 4
        stg = [c.sb([128, 512], F32) for _ in range(NST)]; stgb = p.bufs(NST)
        psn = c.ps(); psnb = p.buf()
        NPS = 4
        pss = [c.ps() for _ in range(NPS)]; pssb = p.bufs(NPS)
        zfmb = p.buf(); ztmb = p.buf()

        p.dma("sp", "x", lambda e: e.dma_start(out=x_sb[:, :, :], in_=xT.rearrange("(k q) t -> q k t", q=128)), writes=[xb])
        p.dma("sp", "g", lambda e: e.dma_start(out=g_sb[:, :], in_=g1), writes=[gb])
        for k in range(8):
            p.dma("pool", "wfm", lambda e, k=k: e.dma_start(out=wfm_sb[:, k, :], in_=wfm[k * 128:(k + 1) * 128, :]), writes=[wfmb])
        for k in range(8):
            p.dma("pool", "wtm", lambda e, k=k: e.dma_start(out=wtm_sb[:, k, :], in_=wtm[k * 128:(k + 1) * 128, :]), writes=[wtmb])
        p.op("dve", lambda e: e.memset(ones_bf[:, :], 1.0 / D), writes=[onesb])

        i_ps = 0
        i_st = 0
        for tt in range(TQ // 512):
            col0 = tt * 512
            rmsnorm_fm(p, c, x_sb, xb, col0, 512, g_sb, gb, ones_bf, onesb, hT, hb, sq, sqb, psn, psnb, rstd, rstdb)
            for oc in range(NFM // 128):
                ps, psb = pss[i_ps % NPS], pssb[i_ps % NPS]; i_ps += 1
                p.mm([lambda e, k=k, oc=oc, ps=ps: e.matmul(ps[:, :], lhsT=wfm_sb[:, k, oc * 128:(oc + 1) * 128], rhs=hT[:, k, :],
                                                             start=(k == 0), stop=(k == 7)) for k in range(8)],
                     reads=[wfmb] + list(hb), writes=[psb])
                sg, sgb = stg[i_st % NST], stgb[i_st % NST]; i_st += 1
                p.op("act", lambda e, ps=ps, sg=sg: e.copy(out=sg[:, :], in_=ps[:, :]), reads=[psb], writes=[sgb])
                p.dma("sp", "o%d" % (i_st % NST), lambda e, sg=sg, oc=oc, col0=col0: e.dma_start(
                    out=zfm[oc * 128:(oc + 1) * 128, col0:col0 + 512], in_=sg[:, :]), reads=[sgb], writes=[zfmb])
            for s in range(4):
                for (c0, n) in ((0, 512), (512, NTM - 512)):
                    ps, psb = pss[i_ps % NPS], pssb[i_ps % NPS]; i_ps += 1
                    p.mm([lambda e, k=k, s=s, c0=c0, n=n, ps=ps: e.matmul(ps[:, :n], lhsT=hT[:, k, s * 128:(s + 1) * 128],
                                                                         rhs=wtm_sb[:, k, c0:c0 + n], start=(k == 0), stop=(k == 7))
                          for k in range(8)], reads=[wtmb] + list(hb), writes=[psb])
                    sg, sgb = stg[i_st % NST], stgb[i_st % NST]; i_st += 1
                    p.op("act", lambda e, ps=ps, sg=sg, n=n: e.copy(out=sg[:, :n], in_=ps[:, :n]), reads=[psb], writes=[sgb])
                    r0 = col0 + s * 128
                    p.dma("sp", "o%d" % (i_st % NST), lambda e, sg=sg, r0=r0, c0=c0, n=n: e.dma_start(
                        out=ztm[r0:r0 + 128, c0:c0 + n], in_=sg[:, :n]), reads=[sgb], writes=[ztmb])
        p.finish("sp", [zfmb, ztmb])
        p.emit()
    return nc


def build_moba(nc, p, prm_sb, prmb, ident_bf, identb, ones_bf, onesbb, aqT, akT, av, caus_d, ind_d, mixT, outb):
    import os
    STAGE = int(os.environ.get('MOBA_STAGE', '9'))
    NP = T // 512
    with ExitStack() as st:
        c = Ctx(nc, st, p)
        Kaug = [c.sb([128, T], BF16, name="Kaug%d" % _) for _ in range(2)]; Kb = [p.bufs(NP, "K%d" % h) for h in range(2)]
        Qaug = [c.sb([128, T], BF16, name="Qaug%d" % _) for _ in range(2)]; Qb = [p.bufs(NP, "Q%d" % h) for h in range(2)]
        vaug = c.sb([128, T // 128, 128], BF16, name="vaug"); vb = p.buf()
        caus = c.sb([128, 2048], BF16, name="caus_sb"); causb = p.buf()
        kms = c.sb([128, 64], F32, name="kms_sb"); kmsb = p.buf()
        AUX = ((64, 96), (0, 32))
        DAT = ((0, 64), (64, 128))
        for h in range(2):
            p.op("dve", lambda e, h=h: e.memset(Kaug[h][:, :], 0.0), writes=Kb[h])
            p.op("pool", lambda e, h=h: e.memset(Qaug[h][:, :], 0.0), writes=Qb[h])
        for h in range(2):
            a0, a1 = AUX[h]
            for q in range(T // 2048):
                p.dma("pool", "ind%d" % h, lambda e, h=h, a0=a0, a1=a1, q=q: e.dma_start(out=Kaug[h][a0:a1, q * 2048:(q + 1) * 2048],
                                                                               in_=ind_d[:, q * 2048:(q + 1) * 2048]), writes=Kb[h])
        p.dma("pool", "caus", lambda e: e.dma_start(out=caus[:, :], in_=caus_d), writes=[causb])
        for q in range(T // 2048):
            p.dma("pool", "vaug", lambda e, q=q: e.dma_start(out=vaug[:, q * 16:(q + 1) * 16, :],
                                                           in_=av.rearrange("(c q) e -> q c e", q=128)[:, q * 16:(q + 1) * 16, :]), writes=[vb])
        p.op("dve", lambda e: e.memset(kms[:, :], 0.0), writes=[kmsb])
        with ExitStack() as st2:
            c2 = Ctx(nc, st2, p)
            blk = c2.sb([128, 128], BF16); blkb = p.buf()
            p.op("dve", lambda e: e.memset(blk[:, :], 0.0), writes=[blkb])
            p.op("dve", lambda e: e.memset(blk[0:64, 0:64], 1.0), writes=[blkb])
            p.op("dve", lambda e: e.memset(blk[64:128, 64:128], 1.0), writes=[blkb])
            xin = [c2.sb([128, 512], F32) for _ in range(2)]; xinb = p.bufs(2)
            sq = c2.sb([128, 512], BF16); sqb = p.buf()
            rstd = c2.sb([128, 512], F32); rstdb = p.buf()
            xn = [c2.sb([128, 512], F32) for _ in range(2)]; xnb = p.bufs(2)
            gsb = [c2.sb([128, 64], F32) for _ in range(2)]; gsbb = p.bufs(2)
            top8 = [c2.sb([128, 16], F32) for _ in range(2)]; top8b = p.bufs(2)
            nm = [c2.sb([128, 4, 128], BF16) for _ in range(2)]; nmb = p.bufs(2)
            psM = c2.ps(); psMb = p.buf()
            psGt = [c2.ps() for _ in range(2)]; psGtb = p.bufs(2)
            psT = [c2.ps() for _ in range(2)]; psTb = p.bufs(2)
            for q in range(2):
                p.op("pool", lambda e, q=q: e.memset(nm[q][:, :, :], 0.0), writes=[nmb[q]])
            n_in = 0
            n_g = 0
            for pc in range(NP if STAGE >= 1 else 0):
                t0 = pc * 512
                for which in range(2):
                    src = akT if which == 0 else aqT
                    gcol = 17 if which == 0 else 16
                    X, Xb = xin[n_in % 2], xinb[n_in % 2]
                    p.dma("sp", "ax%d" % (n_in % 2), lambda e, X=X, src=src, t0=t0: e.dma_start(out=X[:, :], in_=src[:, t0:t0 + 512]), writes=[Xb])
                    n_in += 1
                    p.op("act", lambda e, X=X: e.activation(out=sq[:, :], in_=X[:, :], func=AF.Square), reads=[Xb], writes=[sqb])
                    p.mm([lambda e: e.matmul(psM[:, :], lhsT=blk[:, :], rhs=sq[:, :], start=True, stop=True)], reads=[blkb, sqb], writes=[psMb])
                    p.op("act", lambda e: e.activation(out=rstd[:, :], in_=psM[:, :], func=AF.Ln, bias=EPS, scale=1.0 / 64), reads=[psMb], writes=[rstdb])
                    p.op("act", lambda e: e.activation(out=rstd[:, :], in_=rstd[:, :], func=AF.Exp, scale=-0.5), reads=[rstdb], writes=[rstdb])
                    N_, Nb_ = xn[which], xnb[which]
                    p.op("dve", lambda e, X=X, N_=N_, gcol=gcol: e.scalar_tensor_tensor(out=N_[:, :], in0=X[:, :], scalar=prm_sb[:, gcol:gcol + 1], in1=rstd[:, :],
                                                                                         op0=ALU.mult, op1=ALU.mult), reads=[Xb, prmb, rstdb], writes=[Nb_])
                    dst = Kaug if which == 0 else Qaug
                    dstb = Kb if which == 0 else Qb
                    for h in range(2):
                        d0, d1 = DAT[h]
                        p.op("act", lambda e, h=h, d0=d0, d1=d1, dst=dst, N_=N_, t0=t0: e.copy(out=dst[h][d0:d1, t0:t0 + 512], in_=N_[d0:d1, :]),
                             reads=[Nb_], writes=[dstb[h][pc]])
                    if which == 0:
                        for h in range(2):
                            p.op("dve", lambda e, N_=N_, pc=pc, h=h: e.tensor_reduce(
                                out=kms[h * 64:(h + 1) * 64, h * 32 + 2 * pc:h * 32 + 2 * pc + 2],
                                in_=N_[h * 64:(h + 1) * 64, :].rearrange("q (b k) -> q b k", k=256), axis=AX.X, op=ALU.add), reads=[Nb_], writes=[kmsb])
                if STAGE < 2:
                    continue
                QN, QNb = xn[1], xnb[1]
                nq = pc % 2
                for qb in range(4):
                    own = 2 * pc + qb // 2
                    gq = n_g % 2; n_g += 1
                    p.mm([lambda e, gq=gq, qb=qb: e.matmul(psGt[gq][:, 0:64], lhsT=QN[:, qb * 128:(qb + 1) * 128], rhs=kms[:, :], start=True, stop=True)],
                         reads=[QNb, kmsb], writes=[psGtb[gq]])
                    p.op("dve", lambda e, gq=gq: e.memset(gsb[gq][:, :], -1e30), writes=[gsbb[gq]])
                    if own > 0:
                        p.op("dve", lambda e, gq=gq, own=own: e.tensor_copy(out=gsb[gq].rearrange("q (h n) -> q h n", h=2)[:, :, 0:own],
                                                                            in_=psGt[gq][:, 0:64].rearrange("q (h n) -> q h n", h=2)[:, :, 0:own]),
                             reads=[psGtb[gq]], writes=[gsbb[gq]])
                    for h in range(2):
                        p.op("dve", lambda e, gq=gq, h=h: e.max(out=top8[gq][:, h * 8:(h + 1) * 8], in_=gsb[gq][:, h * 32:(h + 1) * 32]),
                             reads=[gsbb[gq]], writes=[top8b[gq]])
                        c0 = 64 if h == 0 else 96
                        p.op("dve", lambda e, gq=gq, h=h: e.tensor_scalar(out=gsb[gq][:, h * 32:(h + 1) * 32], in0=gsb[gq][:, h * 32:(h + 1) * 32],
                                                                          scalar1=top8[gq][:, h * 8 + 2:h * 8 + 3], scalar2=None, op0=ALU.is_ge),
                             reads=[gsbb[gq], top8b[gq]], writes=[gsbb[gq]])
                        p.op("dve", lambda e, gq=gq, h=h, c0=c0, qb=qb, nq=nq: e.tensor_scalar(out=nm[nq][:, qb, c0:c0 + 32], in0=gsb[gq][:, h * 32:(h + 1) * 32],
                                                                                       scalar1=-1.0, scalar2=-NEG, op0=ALU.add, op1=ALU.mult),
                             reads=[gsbb[gq]], writes=[nmb[nq]])
                        p.op("dve", lambda e, h=h, c0=c0, qb=qb, nq=nq, own=own: e.memset(nm[nq][:, qb, c0 + own:c0 + own + 1], 0.0), writes=[nmb[nq]])
                        if own < 31:
                            p.op("dve", lambda e, h=h, c0=c0, qb=qb, nq=nq, own=own: e.memset(nm[nq][:, qb, c0 + own + 1:c0 + 32], NEG), writes=[nmb[nq]])
                if STAGE < 3:
                    continue
                tq = pc % 2
                p.mm([lambda e, tq=tq, qb=qb, nq=nq: e.matmul(psT[tq][0:96, qb * 128:(qb + 1) * 128], lhsT=nm[nq][:, qb, 0:96], rhs=ident_bf[:, :], start=True, stop=True)
                      for qb in range(4)], reads=[nmb[nq], identb], writes=[psTb[tq]])
                p.op("act", lambda e, tq=tq, t0=t0: e.copy(out=Qaug[0][64:96, t0:t0 + 512], in_=psT[tq][64:96, :]), reads=[psTb[tq]], writes=[Qb[0][pc]])
                p.mm([lambda e, tq=tq, qb=qb, nq=nq: e.matmul(psT[tq][0:32, qb * 128:(qb + 1) * 128], lhsT=nm[nq][:, qb, 96:128], rhs=ident_bf[:, :], start=True, stop=True)
                      for qb in range(4)], reads=[nmb[nq], identb], writes=[psTb[tq]])
                p.op("act", lambda e, tq=tq, t0=t0: e.copy(out=Qaug[1][0:32, t0:t0 + 512], in_=psT[tq][0:32, :]), reads=[psTb[tq]], writes=[Qb[1][pc]])
        p.barrier()
        with ExitStack() as st3:
            c3 = Ctx(nc, st3, p)
            NPT = 3
            PT = [c3.sb([128, 512], BF16) for _ in range(NPT)]; PTb = p.bufs(NPT)
            psS = [c3.ps() for _ in range(2)]; psSb = p.bufs(2)
            psO = [c3.ps() for _ in range(2)]; psOb = p.bufs(2)
            psR = [c3.ps() for _ in range(2)]; psRb = p.bufs(2)
            rr = [c3.sb([64, 512], F32) for _ in range(2)]; rrb = p.bufs(2)
            ao = [c3.sb([64, 512], F32) for _ in range(2)]; aob = p.bufs(2)
            n_s = 0
            n_o = 0
            for pc in range(NP if STAGE >= 4 else 0):
                t0 = pc * 512
                nkt = 4 * (pc + 1)
                for h in range(2):
                    rows = 96 if h == 0 else 128
                    oq = n_o % 2; n_o += 1
                    for kt in range(nkt):
                        sq_ = n_s % 2
                        pq = n_s % NPT
                        n_s += 1
                        p.mm([lambda e, sq_=sq_, h=h, rows=rows, kt=kt, t0=t0: e.matmul(psS[sq_][:, :], lhsT=Kaug[h][0:rows, kt * 128:(kt + 1) * 128],
                                                                                         rhs=Qaug[h][0:rows, t0:t0 + 512], start=True, stop=True)],
                             reads=[Kb[h][kt // 4], Qb[h][pc]], writes=[psSb[sq_]])
                        p.op("act", lambda e, sq_=sq_, pq=pq: e.activation(out=PT[pq][:, :], in_=psS[sq_][:, :], func=AF.Exp, scale=0.125),
                             reads=[psSb[sq_]], writes=[PTb[pq]])
                        if kt >= 4 * pc:
                            r = kt - 4 * pc
                            p.op("dve", lambda e, pq=pq, r=r: e.tensor_tensor(out=PT[pq][:, :], in0=PT[pq][:, :], in1=caus[:, r * 512:(r + 1) * 512], op=ALU.mult),
                                 reads=[PTb[pq], causb], writes=[PTb[pq]])
                        p.mm([lambda e, oq=oq, h=h, kt=kt, pq=pq, nkt=nkt: e.matmul(psO[oq][0:64, :], lhsT=vaug[:, kt, h * 64:(h + 1) * 64], rhs=PT[pq][:, :],
                                                                                     start=(kt == 0), stop=(kt == nkt - 1))],
                             reads=[vb, PTb[pq]], writes=[psOb[oq]])
                        p.mm([lambda e, oq=oq, kt=kt, pq=pq, nkt=nkt: e.matmul(psR[oq][0:64, :], lhsT=ones_bf[:, 0:64], rhs=PT[pq][:, :],
                                                                                start=(kt == 0), stop=(kt == nkt - 1))],
                             reads=[onesbb, PTb[pq]], writes=[psRb[oq]])
                    p.op("dve", lambda e, oq=oq: e.reciprocal(out=rr[oq][:, :], in_=psR[oq][0:64, :]), reads=[psRb[oq]], writes=[rrb[oq]])
                    p.op("dve", lambda e, oq=oq: e.tensor_tensor(out=ao[oq][:, :], in0=psO[oq][0:64, :], in1=rr[oq][:, :], op=ALU.mult),
                         reads=[psOb[oq], rrb[oq]], writes=[aob[oq]])
                    p.dma("sp", "ao%d" % oq, lambda e, oq=oq, h=h, t0=t0: e.dma_start(out=mixT[128 + h * 64:192 + h * 64, t0:t0 + 512], in_=ao[oq][:, :]),
                          reads=[aob[oq]], writes=[outb])


def w_in_cols():
    fm = list(range(0, 256)) + list(range(256, 768)) + list(range(1024, 1280)) + list(range(1288, 1288 + 1024))
    tm = list(range(768, 1024)) + list(range(1288 + 1024, 1288 + 1536)) + list(range(1280, 1288))
    return np.array(fm), np.array(tm)


def split_w_in(w_in):
    fm, tm = w_in_cols()
    return np.ascontiguousarray(w_in[:, fm]), np.ascontiguousarray(w_in[:, tm])


TQH = TQ + 2
C_TILES = [(0, 2)] + [(2 + 512 * i, 512) for i in range(4)]


def build_C():
    nc = bass.Bass("TRN2", target_bir_lowering=False)
    p = Prog(nc)
    with ExitStack() as st:
        c = Ctx(nc, st, p)
        xT = c.dram_in("xT", [D, TQH])
        mixT = c.dram_in("mixT", [D, TQH])
        w_out = c.dram_in("w_out", [D, D])
        g2 = c.dram_in("g2", [128, 8])
        w_up = c.dram_in("w_up", [D, 2 * DFF])
        convw = c.dram_in("convw", [128, 44, 3])
        w_down = c.dram_in("w_down", [DFF, D])
        xoT = c.dram_out("xoT", [D, TQ])

        x_sb = c.sb([128, 8, TQH], F32); xb = p.bufs(8, "x")
        wo_sb = c.sb([128, 8, D], BF16); wob = p.buf()
        wd_sb = c.sb([128, 22, D], BF16); wdb = p.buf()
        g_sb = c.sb([128, 8], F32); gb = p.buf()
        cw_sb = c.sb([128, 44, 3], F32); cwb = p.buf()
        ones_bf = c.sb([128, 128], BF16); onesb = p.buf()
        mixb = [c.sb([128, 8, 512], BF16) for _ in range(2)]; mixbb = p.bufs(2)
        wu = [c.sb([128, 8, 256], BF16) for _ in range(2)]; wub = p.bufs(2)
        act = c.sb([128, 22, 512], BF16); actb = p.bufs(22, "act")
        hT = c.sb([128, 8, 512], BF16); hb = p.bufs(8, "h")
        rstd = c.sb([128, 512], F32); rstdb = p.buf()
        carry = c.sb([128, 44, 2], F32); carryb = p.bufs(44, "cy")
        ug = [c.sb([128, 514], F32) for _ in range(2)]; ugb = p.bufs(2)
        uv = [c.sb([128, 514], F32) for _ in range(2)]; uvb = p.bufs(2)
        tg = [c.sb([128, 512], F32) for _ in range(2)]; tgb = p.bufs(2)
        tv = [c.sb([128, 512], F32) for _ in range(2)]; tvb = p.bufs(2)
        psn = c.ps(); psnb = p.buf()
        psg = [c.ps() for _ in range(2)]; psgb = p.bufs(2)
        psv = [c.ps() for _ in range(2)]; psvb = p.bufs(2)
        pso = [c.ps() for _ in range(2)]; psob = p.bufs(2)
        outb = p.buf()

        for k in range(8):
            p.dma("sp", "x%d" % k, lambda e, k=k: e.dma_start(out=x_sb[:, k, :], in_=xT[k * 128:(k + 1) * 128, :]), writes=[xb[k]])
        p.dma("sp", "g", lambda e: e.dma_start(out=g_sb[:, :], in_=g2), writes=[gb])
        p.dma("sp", "cw", lambda e: e.dma_start(out=cw_sb[:, :, :], in_=convw), writes=[cwb])
        for k in range(8):
            p.dma("pool", "wo", lambda e, k=k: e.dma_start(out=wo_sb[:, k, :], in_=w_out[k * 128:(k + 1) * 128, :]), writes=[wob])
        p.op("dve", lambda e: e.memset(ones_bf[:, :], 1.0 / D), writes=[onesb])
        p.op("dve", lambda e: e.memset(carry[:, :, :], 0.0), writes=list(carryb))
        wd_loaded = False

        i_o = 0
        i_u = 0
        for ti, (col0, n) in enumerate(C_TILES):
            mb, mbb = mixb[ti % 2], mixbb[ti % 2]
            p.dma("pool", "mix%d" % (ti % 2), lambda e, mb=mb, col0=col0, n=n: e.dma_start(
                out=mb[:, :, :n], in_=mixT.rearrange("(k q) t -> q k t", q=128)[:, :, col0:col0 + n]), writes=[mbb])
            for oc in range(8):
                ps, psb = pso[i_o % 2], psob[i_o % 2]; i_o += 1
                p.mm([lambda e, k=k, oc=oc, ps=ps, mb=mb, n=n: e.matmul(ps[:, :n], lhsT=wo_sb[:, k, oc * 128:(oc + 1) * 128], rhs=mb[:, k, :n],
                                                                       start=(k == 0), stop=(k == 7)) for k in range(8)],
                     reads=[wob, mbb], writes=[psb])
                p.op("dve", lambda e, oc=oc, ps=ps, col0=col0, n=n: e.tensor_tensor(
                    out=x_sb[:, oc, col0:col0 + n], in0=x_sb[:, oc, col0:col0 + n], in1=ps[:, :n], op=ALU.add),
                    reads=[psb, xb[oc]], writes=[xb[oc]])
            rmsnorm_fm(p, c, x_sb, None, col0, n, g_sb, gb, ones_bf, onesb, hT, hb, act, actb[:8], psn, psnb, rstd, rstdb, xbs=xb)
            if not wd_loaded:
                for fc in range(22):
                    p.dma("pool", "wd", lambda e, fc=fc: e.dma_start(out=wd_sb[:, fc, :], in_=w_down[fc * 128:(fc + 1) * 128, :]), writes=[wdb])
                wd_loaded = True
            for fc in range(22):
                q = i_u % 2; i_u += 1
                p.dma("pool", "wu%d" % q, lambda e, q=q, fc=fc: e.dma_start(
                    out=wu[q][:, :, 0:128], in_=w_up.rearrange("(k q) c -> q k c", q=128)[:, :, fc * 128:(fc + 1) * 128]), writes=[wub[q]])
                p.dma("pool", "wu%d" % q, lambda e, q=q, fc=fc: e.dma_start(
                    out=wu[q][:, :, 128:256], in_=w_up.rearrange("(k q) c -> q k c", q=128)[:, :, DFF + fc * 128:DFF + (fc + 1) * 128]), writes=[wub[q]])
                p.mm([lambda e, k=k, q=q, n=n: e.matmul(psg[q][:, :n], lhsT=wu[q][:, k, 0:128], rhs=hT[:, k, :n], start=(k == 0), stop=(k == 7))
                      for k in range(8)], reads=[wub[q]] + list(hb), writes=[psgb[q]])
                p.mm([lambda e, k=k, q=q, n=n: e.matmul(psv[q][:, :n], lhsT=wu[q][:, k, 128:256], rhs=hT[:, k, :n], start=(k == 0), stop=(k == 7))
                      for k in range(8)], reads=[wub[q]] + list(hb), writes=[psvb[q]])
                for (ps_, psb_, u_, ub_, t_, tb_, ch, eng) in ((psg[q], psgb[q], ug[q], ugb[q], tg[q], tgb[q], fc, "dve"),
                                                              (psv[q], psvb[q], uv[q], uvb[q], tv[q], tvb[q], 22 + fc, "dve")):
                    p.op("act", lambda e, u_=u_, ch=ch: e.copy(out=u_[:, 0:2], in_=carry[:, ch, :]), reads=[carryb[ch]], writes=[ub_])
                    p.op("act", lambda e, u_=u_, ps_=ps_, n=n: e.copy(out=u_[:, 2:2 + n], in_=ps_[:, :n]), reads=[psb_], writes=[ub_])
                    p.op("act", lambda e, u_=u_, ch=ch, n=n: e.copy(out=carry[:, ch, :], in_=u_[:, n:n + 2]), reads=[ub_], writes=[carryb[ch]])
                    p.op(eng, lambda e, u_=u_, t_=t_, ch=ch, n=n: e.tensor_scalar(
                        out=t_[:, :n], in0=u_[:, 0:n], scalar1=cw_sb[:, ch, 0:1], scalar2=None, op0=ALU.mult), reads=[ub_, cwb], writes=[tb_])
                    for j in (1, 2):
                        p.op(eng, lambda e, u_=u_, t_=t_, ch=ch, n=n, j=j: e.scalar_tensor_tensor(
                            out=t_[:, :n], in0=u_[:, j:j + n], scalar=cw_sb[:, ch, j:j + 1], in1=t_[:, :n], op0=ALU.mult, op1=ALU.add),
                            reads=[ub_, cwb, tb_], writes=[tb_])
                p.op("act", lambda e, q=q, n=n: e.activation(out=tg[q][:, :n], in_=tg[q][:, :n], func=AF.Silu), reads=[tgb[q]], writes=[tgb[q]])
                p.op("dve", lambda e, q=q, fc=fc, n=n: e.tensor_tensor(out=act[:, fc, :n], in0=tg[q][:, :n], in1=tv[q][:, :n], op=ALU.mult),
                     reads=[tgb[q], tvb[q]], writes=[actb[fc]])
            for oc in range(8):
                ps, psb = pso[i_o % 2], psob[i_o % 2]; i_o += 1
                p.mm([lambda e, fc=fc, oc=oc, ps=ps, n=n: e.matmul(ps[:, :n], lhsT=wd_sb[:, fc, oc * 128:(oc + 1) * 128], rhs=act[:, fc, :n],
                                                                   start=(fc == 0), stop=(fc == 21)) for fc in range(22)],
                     reads=[wdb] + list(actb), writes=[psb])
                p.op("dve", lambda e, oc=oc, ps=ps, col0=col0, n=n: e.tensor_tensor(
                    out=x_sb[:, oc, col0:col0 + n], in0=x_sb[:, oc, col0:col0 + n], in1=ps[:, :n], op=ALU.add),
                    reads=[psb, xb[oc]], writes=[xb[oc]])
            if col0 >= 2:
                for k in range(8):
                    p.dma("sp", "out", lambda e, k=k, col0=col0, n=n: e.dma_start(
                        out=xoT[k * 128:(k + 1) * 128, col0 - 2:col0 - 2 + n], in_=x_sb[:, k, col0:col0 + n]), reads=[xb[k]], writes=[outb])
        p.finish("sp", [outb])
        p.emit()
    return nc


def halo_T(a, b, j):
    out = np.zeros((a.shape[2], TQH), np.float32)
    out[:, 2:] = a[b, j * TQ:(j + 1) * TQ, :].T
    if j > 0:
        out[:, :2] = a[b, j * TQ - 2:j * TQ, :].T
    return out


def host_C_inputs(x, mix, w_out, g2, w_up, ffn_conv, w_down):
    g2l = np.ascontiguousarray(g2.reshape(8, 128).T)
    cw = np.ascontiguousarray(ffn_conv.reshape(3, 44, 128).transpose(2, 1, 0))
    maps = []
    for c in range(NCORE):
        b, j = c // 4, c % 4
        maps.append({"xT": halo_T(x, b, j), "mixT": halo_T(mix, b, j), "w_out": w_out, "g2": g2l,
                     "w_up": w_up, "convw": cw, "w_down": w_down})
    return maps


NPRM = 40
POOL_WINDOWS = (2, 4, 8, 16)


def build_B(do_pool=True, do_mlstm=True, do_moba=True):
    nc = bass.Bass("TRN2", target_bir_lowering=False)
    p = Prog(nc)
    with ExitStack() as st0:
        c0 = Ctx(nc, st0, p)
        pT = c0.dram_in("pT", [64, T])
        mqT = c0.dram_in("mqT", [64, T])
        mkT = c0.dram_in("mkT", [64, T])
        moT = c0.dram_in("moT", [64, T])
        aqT = c0.dram_in("aqT", [128, T])
        akT = c0.dram_in("akT", [128, T])
        mv = c0.dram_in("mv", [T, 64])
        av = c0.dram_in("av", [T, 128])
        gates = c0.dram_in("gates", [T, 2])
        prm = c0.dram_in("prm", [128, NPRM])
        poolw = c0.dram_in("poolw", [64, 64])
        triU_d = c0.dram_in("triU", [128, 128])
        ident_d = c0.dram_in("ident", [128, 128])
        caus_d = c0.dram_in("caus", [128, 2048])
        ind_d = c0.dram_in("ind", [32, T])
        mixT = c0.dram_out("mixT", [256, T])
        outb = p.buf()

        prm_sb = c0.sb([128, NPRM], F32); prmb = p.buf()
        triU = c0.sb([128, 128], F32); triUb = p.buf()
        ident_bf = c0.sb([128, 128], BF16); identb = p.buf()
        ones_f = c0.sb([128, 128], F32); onesfb = p.buf()
        ones_bf = c0.sb([128, 128], BF16); onesbb = p.buf()
        p.dma("sp", "prm", lambda e: e.dma_start(out=prm_sb[:, :], in_=prm), writes=[prmb])
        p.dma("sp", "triU", lambda e: e.dma_start(out=triU[:, :], in_=triU_d), writes=[triUb])
        p.dma("pool", "ident", lambda e: e.dma_start(out=ident_bf[:, :], in_=ident_d), writes=[identb])
        p.op("dve", lambda e: e.memset(ones_f[:, :], 1.0), writes=[onesfb])
        p.op("dve", lambda e: e.memset(ones_bf[:, :], 1.0), writes=[onesbb])

        if do_pool:
            with ExitStack() as st:
                c = Ctx(nc, st, p)
                PW = 2048
                xa = [c.sb([64, 16 + PW], F32) for _ in range(2)]; xab = p.bufs(2)
                s_a = c.sb([64, 16 + PW], F32); sab = p.buf()
                s_b = c.sb([64, 16 + PW], F32); sbb = p.buf()
                acc = c.sb([64, 16 + PW], F32); accb = p.buf()
                d_bf = c.sb([64, PW], BF16); dbb = p.buf()
                wp_bf = c.sb([64, 64], BF16); wpb = p.buf()
                stg = [c.sb([64, 512], F32) for _ in range(2)]; stgb = p.bufs(2)
                pps = [c.ps() for _ in range(2)]; ppsb = p.bufs(2)
                p.dma("pool", "wp", lambda e: e.dma_start(out=wp_bf[:, :], in_=poolw), writes=[wpb])
                p.op("dve", lambda e: e.memset(xa[0][:, 0:16], 0.0), writes=[xab[0]])
                i_s = 0
                for pc in range(T // PW):
                    X, Xb = xa[pc % 2], xab[pc % 2]
                    if pc > 0:
                        Xp, Xpb = xa[(pc - 1) % 2], xab[(pc - 1) % 2]
                        p.op("act", lambda e, X=X, Xp=Xp: e.copy(out=X[:, 0:16], in_=Xp[:, PW:PW + 16]), reads=[Xpb], writes=[Xb])
                    p.dma("sp", "px%d" % (pc % 2), lambda e, X=X, pc=pc: e.dma_start(out=X[:, 16:16 + PW], in_=pT[:, pc * PW:(pc + 1) * PW]), writes=[Xb])
                    W = 16 + PW
                    p.op("dve", lambda e, X=X: e.tensor_tensor(out=s_a[:, 1:W], in0=X[:, 1:W], in1=X[:, 0:W - 1], op=ALU.add), reads=[Xb], writes=[sab])
                    p.op("dve", lambda e: e.tensor_scalar(out=acc[:, 16:W], in0=s_a[:, 16:W], scalar1=prm_sb[0:64, 1:2], scalar2=None, op0=ALU.mult),
                         reads=[sab, prmb], writes=[accb])
                    src, srcb, dst, dstb = s_a, sab, s_b, sbb
                    lo = 1
                    for k, sh in ((1, 2), (2, 4), (3, 8)):
                        lo2 = lo + sh
                        p.op("dve", lambda e, src=src, dst=dst, lo2=lo2, sh=sh: e.tensor_tensor(
                            out=dst[:, lo2:W], in0=src[:, lo2:W], in1=src[:, lo2 - sh:W - sh], op=ALU.add), reads=[srcb], writes=[dstb])
                        p.op("dve", lambda e, dst=dst, k=k: e.scalar_tensor_tensor(
                            out=acc[:, 16:W], in0=dst[:, 16:W], scalar=prm_sb[0:64, 1 + k:2 + k], in1=acc[:, 16:W], op0=ALU.mult, op1=ALU.add),
                            reads=[dstb, prmb, accb], writes=[accb])
                        src, srcb, dst, dstb = dst, dstb, src, srcb
                        lo = lo2
                    if pc == 0:
                        p.op("dve", lambda e: e.tensor_tensor(out=acc[:, 16:32], in0=acc[:, 16:32], in1=prm_sb[0:64, 19:35], op=ALU.mult),
                             reads=[accb, prmb], writes=[accb])
                    p.op("dve", lambda e, X=X: e.tensor_tensor(out=d_bf[:, :], in0=acc[:, 16:W], in1=X[:, 16:W], op=ALU.subtract), reads=[accb, Xb], writes=[dbb])
                    for q in range(PW // 512):
                        ps, psb = pps[i_s % 2], ppsb[i_s % 2]
                        sg, sgb = stg[i_s % 2], stgb[i_s % 2]
                        ch = "po%d" % (i_s % 2); i_s += 1
                        p.mm([lambda e, ps=ps, q=q: e.matmul(ps[0:64, :], lhsT=wp_bf[:, :], rhs=d_bf[:, q * 512:(q + 1) * 512], start=True, stop=True)],
                             reads=[wpb, dbb], writes=[psb])
                        p.op("act", lambda e, ps=ps, sg=sg: e.activation(out=sg[:, :], in_=ps[0:64, :], func=AF.Copy, scale=prm_sb[0:64, 0:1]),
                             reads=[psb, prmb], writes=[sgb])
                        t0 = pc * PW + q * 512
                        p.dma("sp", ch, lambda e, sg=sg, t0=t0: e.dma_start(out=mixT[0:64, t0:t0 + 512], in_=sg[:, :]), reads=[sgb], writes=[outb])

        p.barrier()
        if do_mlstm:
            with ExitStack() as st:
                c = Ctx(nc, st, p)
                NCH = T // 128
                qT_bf = c.sb([64, T], BF16); qTb = p.bufs(4, "qT")
                kT_bf = c.sb([64, T], BF16); kTb = p.bufs(4, "kT")
                vones = c.sb([128, NCH, 128], BF16); vonesb = p.buf()
                g_sb = c.sb([128, NCH, 2], F32); gsbb = p.buf()
                iv = c.sb([128, NCH], F32); ivb = p.buf()
                lf = c.sb([128, NCH], F32); lfb_ = p.buf()
                bias_s = c.sb([128, NCH], F32); biasb = p.buf()
                wk = c.sb([128, NCH], F32); wkb = p.buf()
                dec = c.sb([128, NCH], F32); decb = p.buf()
                St = c.sb([64, 128], F32); Stb = p.buf()
                St_bf = c.sb([64, 128], BF16); Stbfb = p.buf()
                PW = 2048
                xin = [c.sb([64, 3 + PW], F32) for _ in range(2)]; xinb = p.bufs(2)
                cacc = c.sb([64, PW], F32); caccb = p.buf()
                n_x = 0
                for (src_d, dst, dstbufs, c0col, scl) in ((mqT, qT_bf, qTb, 5, 1.0), (mkT, kT_bf, kTb, 9, 0.125)):
                    for pc in range(T // PW):
                        X, Xb = xin[n_x % 2], xinb[n_x % 2]
                        if pc == 0:
                            p.op("dve", lambda e, X=X: e.memset(X[:, 0:3], 0.0), writes=[Xb])
                        else:
                            Xp, Xpb = xin[(n_x - 1) % 2], xinb[(n_x - 1) % 2]
                            p.op("act", lambda e, X=X, Xp=Xp: e.copy(out=X[:, 0:3], in_=Xp[:, PW:PW + 3]), reads=[Xpb], writes=[Xb])
                        p.dma("sp", "mx%d" % (n_x % 2), lambda e, X=X, pc=pc, src_d=src_d: e.dma_start(
                            out=X[:, 3:3 + PW], in_=src_d[:, pc * PW:(pc + 1) * PW]), writes=[Xb])
                        n_x += 1
                        p.op("dve", lambda e, X=X, c0col=c0col: e.tensor_scalar(out=cacc[:, :], in0=X[:, 0:PW], scalar1=prm_sb[0:64, c0col:c0col + 1],
                                                                                 scalar2=None, op0=ALU.mult), reads=[Xb, prmb], writes=[caccb])
                        for j in (1, 2, 3):
                            p.op("dve", lambda e, X=X, c0col=c0col, j=j: e.scalar_tensor_tensor(
                                out=cacc[:, :], in0=X[:, j:j + PW], scalar=prm_sb[0:64, c0col + j:c0col + j + 1], in1=cacc[:, :],
                                op0=ALU.mult, op1=ALU.add), reads=[Xb, prmb, caccb], writes=[caccb])
                        p.op("act", lambda e: e.activation(out=cacc[:, :], in_=cacc[:, :], func=AF.Silu), reads=[caccb], writes=[caccb])
                        p.op("dve", lambda e, dst=dst, pc=pc, scl=scl: e.tensor_scalar(out=dst[:, pc * PW:(pc + 1) * PW], in0=cacc[:, :], scalar1=scl,
                                                                                       scalar2=None, op0=ALU.mult), reads=[caccb], writes=[dstbufs[pc]])
                p.op("dve", lambda e: e.memset(vones[:, :, 64:128], 1.0), writes=[vonesb])
                for q in range(4):
                    p.dma("pool", "vones", lambda e, q=q: e.dma_start(out=vones[:, q * 16:(q + 1) * 16, 0:64],
                                                                   in_=mv.rearrange("(c q) e -> q c e", q=128)[:, q * 16:(q + 1) * 16, :]), writes=[vonesb])
                for q in range(4):
                    p.dma("sp", "gates", lambda e, q=q: e.dma_start(out=g_sb[:, q * 16:(q + 1) * 16, :],
                                                                in_=gates.rearrange("(c q) two -> q c two", q=128)[:, q * 16:(q + 1) * 16, :]), writes=[gsbb])
                p.op("dve", lambda e: e.tensor_scalar(out=iv[:, :], in0=g_sb[:, :, 0], scalar1=prm_sb[:, 13:14], scalar2=None, op0=ALU.add),
                     reads=[gsbb, prmb], writes=[ivb])
                p.op("dve", lambda e: e.tensor_scalar(out=lf[:, :], in0=g_sb[:, :, 1], scalar1=prm_sb[:, 14:15], scalar2=None, op0=ALU.add),
                     reads=[gsbb, prmb], writes=[lfb_])
                p.op("act", lambda e: e.activation(out=lf[:, :], in_=lf[:, :], func=AF.Exp, scale=-1.0), reads=[lfb_], writes=[lfb_])
                p.op("act", lambda e: e.activation(out=lf[:, :], in_=lf[:, :], func=AF.Ln, bias=1.0), reads=[lfb_], writes=[lfb_])
                p.op("dve", lambda e: e.tensor_scalar(out=lf[:, :], in0=lf[:, :], scalar1=-1.0, scalar2=None, op0=ALU.mult), reads=[lfb_], writes=[lfb_])
                psA = c.ps(); psAb = p.buf()
                psB = c.ps(); psBb = p.buf()
                p.mm([lambda e: e.matmul(psA[:, 0:NCH], lhsT=triU[:, :], rhs=lf[:, :], start=True, stop=True)], reads=[triUb, lfb_], writes=[psAb])
                p.mm([lambda e: e.matmul(psB[:, 0:NCH], lhsT=ones_f[:, :], rhs=lf[:, :], start=True, stop=True)], reads=[onesfb, lfb_], writes=[psBb])
                p.op("dve", lambda e: e.tensor_tensor(out=bias_s[:, :], in0=iv[:, :], in1=psA[:, 0:NCH], op=ALU.subtract), reads=[ivb, psAb], writes=[biasb])
                p.op("dve", lambda e: e.tensor_tensor(out=wk[:, :], in0=bias_s[:, :], in1=psB[:, 0:NCH], op=ALU.add), reads=[biasb, psBb], writes=[wkb])
                p.op("act", lambda e: e.activation(out=wk[:, :], in_=wk[:, :], func=AF.Exp), reads=[wkb], writes=[wkb])
                p.op("act", lambda e: e.activation(out=dec[:, :], in_=psB[:, 0:NCH], func=AF.Exp), reads=[psBb], writes=[decb])
                p.op("dve", lambda e: e.memset(St[:, :], 0.0), writes=[Stb])
                p.op("dve", lambda e: e.memset(St_bf[:, :], 0.0), writes=[Stbfb])

                lfrep = [c.sb([128, 128], F32) for _ in range(2)]; lfrepb = p.bufs(2)
                DT = [c.sb([128, 128], F32) for _ in range(2)]; DTb = p.bufs(2)
                WT = [c.sb([128, 128], BF16) for _ in range(2)]; WTb = p.bufs(2)
                eG = [c.sb([64, 128], F32) for _ in range(2)]; eGb = p.bufs(2)
                qs = [c.sb([64, 128], BF16) for _ in range(2)]; qsb = p.bufs(2)
                ksc = [c.sb([128, 64], BF16) for _ in range(2)]; kscb = p.bufs(2)
                dn = [c.sb([64, 128], F32) for _ in range(2)]; dnb = p.bufs(2)
                hT = [c.sb([64, 512], F32) for _ in range(2)]; hTb = p.bufs(2)
                sq = c.sb([64, 512], BF16); sqb = p.buf()
                rstd = c.sb([64, 512], F32); rstdb = p.buf()
                mo_sb = [c.sb([64, 512], F32) for _ in range(2)]; mob = p.bufs(2)
                ho = [c.sb([64, 512], F32) for _ in range(2)]; hob = p.bufs(2)
                psG = [c.ps() for _ in range(2)]; psGb = p.bufs(2)
                psN = [c.ps() for _ in range(2)]; psNb = p.bufs(2)
                for ch in range(NCH):
                    q = ch % 2
                    t0 = ch * 128
                    pcq = ch // 16
                    p.op("dve", lambda e, q=q, ch=ch: e.tensor_scalar(out=lfrep[q][:, :], in0=ones_f[:, :], scalar1=lf[:, ch:ch + 1], scalar2=None, op0=ALU.mult),
                         reads=[onesfb, lfb_], writes=[lfrepb[q]])
                    p.mm([lambda e, q=q: e.matmul(psG[q][:, 0:128], lhsT=lfrep[q][:, :], rhs=triU[:, :], start=True, stop=True)],
                         reads=[lfrepb[q], triUb], writes=[psGb[q]])
                    p.mm([lambda e, q=q, t0=t0: e.matmul(psG[q][:, 128:256], lhsT=kT_bf[:, t0:t0 + 128], rhs=qT_bf[:, t0:t0 + 128], start=True, stop=True)],
                         reads=[kTb[pcq], qTb[pcq]], writes=[psGb[q]])
                    p.op("act", lambda e, q=q, ch=ch: e.activation(out=DT[q][:, :], in_=psG[q][:, 0:128], func=AF.Exp, bias=bias_s[:, ch:ch + 1]),
                         reads=[psGb[q], biasb], writes=[DTb[q]])
                    p.op("act", lambda e, q=q: e.activation(out=eG[q][:, :], in_=psG[q][0:64, 0:128], func=AF.Exp), reads=[psGb[q]], writes=[eGb[q]])
                    p.op("dve", lambda e, q=q: e.tensor_tensor(out=DT[q][:, :], in0=DT[q][:, :], in1=triU[:, :], op=ALU.mult), reads=[DTb[q], triUb], writes=[DTb[q]])
                    p.op("dve", lambda e, q=q: e.tensor_tensor(out=WT[q][:, :], in0=DT[q][:, :], in1=psG[q][:, 128:256], op=ALU.mult), reads=[DTb[q], psGb[q]], writes=[WTb[q]])
                    p.op("dve", lambda e, q=q, t0=t0: e.tensor_tensor(out=qs[q][:, :], in0=eG[q][:, :], in1=qT_bf[:, t0:t0 + 128], op=ALU.mult),
                         reads=[eGb[q], qTb[pcq]], writes=[qsb[q]])
                    p.mm([lambda e, q=q, ch=ch: e.matmul(psN[q][0:64, 0:128], lhsT=vones[:, ch, 0:64], rhs=WT[q][:, :], start=True, stop=False),
                          lambda e, q=q: e.matmul(psN[q][0:64, 0:128], lhsT=St_bf[:, 0:64], rhs=qs[q][:, :], start=False, stop=True),
                          lambda e, q=q, ch=ch: e.matmul(psN[q][0:64, 128:256], lhsT=vones[:, ch, 64:128], rhs=WT[q][:, :], start=True, stop=False),
                          lambda e, q=q: e.matmul(psN[q][0:64, 128:256], lhsT=St_bf[:, 64:128], rhs=qs[q][:, :], start=False, stop=True),
                          lambda e, q=q, t0=t0: e.matmul(psN[q][:, 256:320], lhsT=kT_bf[:, t0:t0 + 128], rhs=ident_bf[0:64, 0:64], start=True, stop=True)],
                         reads=[vonesb, WTb[q], Stbfb, qsb[q], kTb[pcq], identb], writes=[psNb[q]])
                    hq = (ch // 4) % 2
                    hc = (ch % 4) * 128
                    p.op("act", lambda e, q=q: e.activation(out=dn[q][:, :], in_=psN[q][0:64, 128:256], func=AF.Abs), reads=[psNb[q]], writes=[dnb[q]])
                    p.op("dve", lambda e, q=q: e.tensor_scalar(out=dn[q][:, :], in0=dn[q][:, :], scalar1=1.0, scalar2=None, op0=ALU.max), reads=[dnb[q]], writes=[dnb[q]])
                    p.op("dve", lambda e, q=q: e.reciprocal(out=dn[q][:, :], in_=dn[q][:, :]), reads=[dnb[q]], writes=[dnb[q]])
                    p.op("dve", lambda e, q=q, hq=hq, hc=hc: e.tensor_tensor(out=hT[hq][:, hc:hc + 128], in0=psN[q][0:64, 0:128], in1=dn[q][:, :], op=ALU.mult),
                         reads=[psNb[q], dnb[q]], writes=[hTb[hq]])
                    p.op("act", lambda e, q=q, ch=ch: e.activation(out=ksc[q][:, :], in_=psN[q][:, 256:320], func=AF.Copy, scale=wk[:, ch:ch + 1]),
                         reads=[psNb[q], wkb], writes=[kscb[q]])
                    p.mm([lambda e, q=q, ch=ch: e.matmul(psN[q][0:64, 320:448], lhsT=ksc[q][:, :], rhs=vones[:, ch, :], start=True, stop=True)],
                         reads=[kscb[q], vonesb], writes=[psNb[q]])
                    p.op("dve", lambda e, q=q, ch=ch: e.scalar_tensor_tensor(out=St[:, :], in0=St[:, :], scalar=dec[0:64, ch:ch + 1], in1=psN[q][0:64, 320:448],
                                                                             op0=ALU.mult, op1=ALU.add), reads=[Stb, decb, psNb[q]], writes=[Stb])
                    p.op("act", lambda e: e.copy(out=St_bf[:, :], in_=St[:, :]), reads=[Stb], writes=[Stbfb])
                    if ch % 4 == 3:
                        pt0 = (ch // 4) * 512
                        H_, Hb_ = hT[hq], hTb[hq]
                        p.dma("sp", "mo%d" % hq, lambda e, hq=hq, pt0=pt0: e.dma_start(out=mo_sb[hq][:, :], in_=moT[:, pt0:pt0 + 512]), writes=[mob[hq]])
                        p.op("act", lambda e, H_=H_: e.activation(out=sq[:, :], in_=H_[:, :], func=AF.Square), reads=[Hb_], writes=[sqb])
                        p.mm([lambda e: e.matmul(psA[0:64, :], lhsT=ones_bf[0:64, 0:64], rhs=sq[:, :], start=True, stop=True)], reads=[onesbb, sqb], writes=[psAb])
                        p.op("act", lambda e: e.activation(out=rstd[:, :], in_=psA[0:64, :], func=AF.Ln, bias=EPS, scale=1.0 / 64), reads=[psAb], writes=[rstdb])
                        p.op("act", lambda e: e.activation(out=rstd[:, :], in_=rstd[:, :], func=AF.Exp, scale=-0.5), reads=[rstdb], writes=[rstdb])
                        p.op("act", lambda e, hq=hq: e.activation(out=mo_sb[hq][:, :], in_=mo_sb[hq][:, :], func=AF.Sigmoid), reads=[mob[hq]], writes=[mob[hq]])
                        p.op("dve", lambda e, H_=H_, hq=hq: e.scalar_tensor_tensor(out=ho[hq][:, :], in0=H_[:, :], scalar=prm_sb[0:64, 15:16], in1=rstd[:, :],
                                                                                   op0=ALU.mult, op1=ALU.mult), reads=[Hb_, prmb, rstdb], writes=[hob[hq]])
                        p.op("dve", lambda e, hq=hq: e.tensor_tensor(out=ho[hq][:, :], in0=ho[hq][:, :], in1=mo_sb[hq][:, :], op=ALU.mult),
                             reads=[hob[hq], mob[hq]], writes=[hob[hq]])
                        p.dma("sp", "ho%d" % hq, lambda e, hq=hq, pt0=pt0: e.dma_start(out=mixT[64:128, pt0:pt0 + 512], in_=ho[hq][:, :]), reads=[hob[hq]], writes=[outb])

        p.barrier()
        if do_moba:
            build_moba(nc, p, prm_sb, prmb, ident_bf, identb, ones_bf, onesbb, aqT, akT, av, caus_d, ind_d, mixT, outb)
        p.finish_all("sp")
        p.emit()
    return nc


def static_consts():
    triU = np.triu(np.ones((128, 128), np.float32))
    ident = np.eye(128, dtype=np.float32)
    k = np.arange(128)[:, None]
    q = np.arange(512)[None, :]
    caus = np.concatenate([((r * 128 + k) <= q).astype(np.float32) for r in range(4)], axis=1)
    ind = (np.arange(T)[None, :] // 256 == np.arange(32)[:, None]).astype(np.float32)
    return triU, ident, np.ascontiguousarray(caus), np.ascontiguousarray(ind)


def host_B_inputs(zfm_b, ztm_b, prm_l):
    triU, ident, caus, ind = static_consts()
    maps = []
    for cidx in range(NCORE):
        b, g = cidx // 4, cidx % 4
        zf, zt = zfm_b[b], ztm_b[b]
        w = POOL_WINDOWS[g]
        prm = np.zeros((128, NPRM), np.float32)
        prm[0:64, 0] = prm_l["pool_scale"][g * 64:(g + 1) * 64]
        prm[:, 1 + g] = 1.0 / w
        prm[0:64, 5:9] = prm_l["m_conv"][:, g * 64:(g + 1) * 64].T
        prm[0:64, 9:13] = prm_l["m_conv"][:, 256 + g * 64:256 + (g + 1) * 64].T
        prm[:, 13] = prm_l["m_b_i"][g]
        prm[:, 14] = prm_l["m_b_f"][g]
        prm[0:64, 15] = prm_l["m_norm_g"][g * 64:(g + 1) * 64]
        prm[:, 16] = np.tile(prm_l["a_q_g"], 2)
        prm[:, 17] = np.tile(prm_l["a_k_g"], 2)
        prm[:, 19:35] = (w / np.minimum(np.arange(16) + 1, w)).astype(np.float32)[None, :]
        maps.append({
            "pT": np.ascontiguousarray(zf[g * 64:(g + 1) * 64]),
            "mqT": np.ascontiguousarray(zf[256 + g * 64:256 + (g + 1) * 64]),
            "mkT": np.ascontiguousarray(zf[512 + g * 64:512 + (g + 1) * 64]),
            "moT": np.ascontiguousarray(zf[768 + g * 64:768 + (g + 1) * 64]),
            "aqT": np.ascontiguousarray(zf[1024 + g * 128:1024 + (g + 1) * 128]),
            "akT": np.ascontiguousarray(zf[1536 + g * 128:1536 + (g + 1) * 128]),
            "mv": np.ascontiguousarray(zt[:, g * 64:(g + 1) * 64]),
            "av": np.ascontiguousarray(zt[:, 256 + g * 128:256 + (g + 1) * 128]),
            "gates": np.ascontiguousarray(zt[:, [768 + g, 772 + g]]),
            "prm": prm, "poolw": np.ascontiguousarray(prm_l["pool_w"][g]),
            "triU": triU, "ident": ident, "caus": caus, "ind": ind,
        })
    return maps


_PROGS = {}


def _prog(name):
    if name not in _PROGS:
        _PROGS[name] = {"A": build_A, "B": build_B, "C": build_C}[name]()
    return _PROGS[name]


def kernel_unfused(**inputs):
    inp = {k: np.ascontiguousarray(np.asarray(v), dtype=np.float32) for k, v in inputs.items()}
    x = inp["x"].copy()
    cores = list(range(NCORE))
    for l in range(4):
        wfm, wtm = split_w_in(inp["w_in"][l])
        g1l = np.ascontiguousarray(inp["ln1_g"][l].reshape(8, 128).T)
        maps = []
        for c in cores:
            b, j = c // 4, c % 4
            maps.append({"xT": np.ascontiguousarray(x[b, j * TQ:(j + 1) * TQ, :].T), "wfm": wfm, "wtm": wtm, "g1": g1l})
        res = run_bass_kernel_spmd(_prog("A"), maps, core_ids=cores).results
        zfm_b = [np.concatenate([res[b * 4 + j]["zfm"] for j in range(4)], axis=1) for b in range(2)]
        ztm_b = [np.concatenate([res[b * 4 + j]["ztm"] for j in range(4)], axis=0) for b in range(2)]
        prm_l = {k: inp[k][l] for k in ("pool_scale", "m_conv", "m_b_i", "m_b_f", "m_norm_g", "a_q_g", "a_k_g", "pool_w")}
        res = run_bass_kernel_spmd(_prog("B"), host_B_inputs(zfm_b, ztm_b, prm_l), core_ids=cores).results
        mix = np.empty((2, T, D), np.float32)
        for c in cores:
            b, g = c // 4, c % 4
            m = res[c]["mixT"]
            mix[b, :, g * 64:(g + 1) * 64] = m[0:64].T
            mix[b, :, 256 + g * 64:256 + (g + 1) * 64] = m[64:128].T
            mix[b, :, 512 + g * 128:512 + (g + 1) * 128] = m[128:256].T
        res = run_bass_kernel_spmd(_prog("C"), host_C_inputs(x, mix, inp["w_out"][l], inp["ln2_g"][l], inp["w_up"][l],
                                                             inp["ffn_conv"][l], inp["w_down"][l]), core_ids=cores).results
        xn = np.empty_like(x)
        for c in cores:
            b, j = c // 4, c % 4
            xn[b, j * TQ:(j + 1) * TQ, :] = res[c]["xoT"].T
        x = xn
    return x


U32 = mybir.dt.uint32
RZ = 2816
RF = 10
RM = 1025
CHR = 256
GROUPS = [[0, 1, 2, 3], [4, 5, 6, 7]]
IDX_POOL, IDX_MQ, IDX_MK = 0, 4, 8
IDX_MO, IDX_AQ, IDX_AK = 12, 28, 44
IDX_TM = 60
IDX_XT = 64
IDX_MX = 65
IDX_MH = 97
IDX_AV = 105
IDX_GT = 109
NIDX = 113


def build_fused(nlayers=4, upto=3):
    nc = bass.Bass("TRN2", target_bir_lowering=False)
    p = Prog(nc)
    with ExitStack() as st0:
        c0 = Ctx(nc, st0, p)
        xT = c0.dram_in("xT", [D, TQ])
        wfm = c0.dram_in("wfm", [4, D, NFM])
        wtm = c0.dram_in("wtm", [4, D, NTM])
        g1 = c0.dram_in("g1", [4, 128, 8])
        w_out = c0.dram_in("w_out", [4, D, D])
        g2 = c0.dram_in("g2", [4, 128, 8])
        w_up = c0.dram_in("w_up", [4, 22, 128, 8, 256])
        convw = c0.dram_in("convw", [4, 128, 44, 3])
        w_down = c0.dram_in("w_down", [4, DFF, D])
        prm = c0.dram_in("prm", [4, 128, NPRM])
        poolw = c0.dram_in("poolw", [4, 64, 64])
        triU_d = c0.dram_in("triU", [128, 128])
        ident_d = c0.dram_in("ident", [128, 128])
        caus_d = c0.dram_in("caus", [128, 2048])
        ind_d = c0.dram_in("ind", [32, T])
        idx_d = c0.dram_in("idx", [128, NIDX], U32)
        xoT = c0.dram_out("xoT", [D, TQ])
        zsrc_t = nc.dram_tensor("zsrc", [RZ, 2048], BF16, kind="Internal")
        zall_t = nc.dram_tensor("zall", [4 * RZ, 2048], BF16, kind="Internal")
        zsrcf_t = nc.dram_tensor("zsrcf", [RF, 2048], F32, kind="Internal")
        zallf_t = nc.dram_tensor("zallf", [4 * RF, 2048], F32, kind="Internal")
        msrc_t = nc.dram_tensor("msrc", [RM, 2048], BF16, kind="Internal")
        mall_t = nc.dram_tensor("mall", [4 * RM, 2048], BF16, kind="Internal")
        zsrcb, zallb, msrcb, mallb, outb = p.buf(), p.buf(), p.buf(), p.buf(), p.buf()
        zall2048 = zall_t.ap()
        zall512 = zall_t.ap().rearrange("r (f w) -> (r f) w", w=512)
        zall1024 = zall_t.ap().rearrange("r (f w) -> (r f) w", w=1024)
        zallf32 = zallf_t.ap().rearrange("r (f w) -> (r f) w", w=32)
        zall16 = zallf_t.ap().rearrange("r (f w) -> (r f) w", w=16)
        mall512 = mall_t.ap().rearrange("r (f w) -> (r f) w", w=512)
        mall2 = mall_t.ap().rearrange("r (f w) -> (r f) w", w=2)

        x_sb = c0.sb([128, 8, TQH], F32); xb = p.bufs(8, "x")
        idx_sb = c0.sb([128, NIDX], U32); idxb = p.buf()
        onesD_bf = c0.sb([128, 128], BF16); onesDb = p.buf()
        ones_f = c0.sb([128, 128], F32); onesfb = p.buf()
        ones_bf = c0.sb([128, 128], BF16); onesbb = p.buf()
        zero_sb = c0.sb([128, 16], F32); zerob = p.buf()
        zero_bf = c0.sb([128, 16], BF16); zerobb = p.buf()
        triU = c0.sb([128, 128], F32); triUb = p.buf()
        ident_bf = c0.sb([128, 128], BF16); identb = p.buf()
        caus = c0.sb([128, 2048], BF16); causb = p.buf()
        blk = c0.sb([128, 128], BF16); blkb = p.buf()
        for k in range(8):
            p.dma("sp", "x%d" % k, lambda e, k=k: e.dma_start(out=x_sb[:, k, 2:TQH], in_=xT[k * 128:(k + 1) * 128, :]), writes=[xb[k]])
        p.dma("sp", "idx", lambda e: e.dma_start(out=idx_sb[:, :], in_=idx_d), writes=[idxb])
        p.dma("sp", "triU", lambda e: e.dma_start(out=triU[:, :], in_=triU_d), writes=[triUb])
        p.dma("pool", "ident", lambda e: e.dma_start(out=ident_bf[:, :], in_=ident_d), writes=[identb])
        p.dma("pool", "caus", lambda e: e.dma_start(out=caus[:, :], in_=caus_d), writes=[causb])
        p.op("dve", lambda e: e.memset(onesD_bf[:, :], 1.0 / D), writes=[onesDb])
        p.op("dve", lambda e: e.memset(ones_f[:, :], 1.0), writes=[onesfb])
        p.op("dve", lambda e: e.memset(ones_bf[:, :], 1.0), writes=[onesbb])
        p.op("dve", lambda e: e.memset(zero_sb[:, :], 0.0), writes=[zerob])
        p.op("dve", lambda e: e.memset(zero_bf[:, :], 0.0), writes=[zerobb])
        p.op("dve", lambda e: e.memset(blk[:, :], 0.0), writes=[blkb])
        p.op("dve", lambda e: e.memset(blk[0:64, 0:64], 1.0), writes=[blkb])
        p.op("dve", lambda e: e.memset(blk[64:128, 64:128], 1.0), writes=[blkb])
        for k in range(8):
            p.op("dve", lambda e, k=k: e.memset(x_sb[:, k, 0:2], 0.0), writes=[xb[k]])

        zsrc_cb = p.bufs(12, "zs")
        zall_cb = p.bufs(12, "za")
        msrc_cb = p.bufs(5, "ms")
        mall_cb = p.bufs(5, "ma")

        def ag_chunk(src_t, dst_t, nrows, i, scb, dcb):
            r0 = i * CHR
            n = min(CHR, nrows - r0)
            p.wait_dma_all("pool")
            p.cc(lambda e: e.collective_compute("AllGather", ALU.bypass, replica_groups=GROUPS,
                                                ins=[src_t.ap()[r0:r0 + n, :]], outs=[dst_t.ap()[4 * r0:4 * r0 + 4 * n, :]]),
                 reads=[scb[i]], writes=[dcb[i]])

        def ag_f():
            p.wait_dma_all("pool")
            p.cc(lambda e: e.collective_compute("AllGather", ALU.bypass, replica_groups=GROUPS, ins=[zsrcf_t.ap()], outs=[zallf_t.ap()]),
                 reads=[zsrc_cb[11]], writes=[zall_cb[11]])

        def gather(ch, out_ap, view, col, nparts, srcbs, writes):
            p.dma("pool", ch, lambda e: e.indirect_dma_start(
                out=out_ap, out_offset=None, in_=view,
                in_offset=bass.IndirectOffsetOnAxis(ap=idx_sb[0:nparts, col:col + 1], axis=0)), reads=list(srcbs) + [idxb], writes=writes)

        def out_piece(ch, sg_ap, sgb, r0, nrows, t0):
            j, cc0 = t0 // 2048, t0 % 2048
            if r0 < 128:
                dst = bass.AP(msrc_t, (r0 * 4 + j) * 2048 + cc0, [[4 * 2048, nrows], [1, 512]])
                wb = [msrc_cb[0]] if r0 < 64 else [msrc_cb[1]]
            else:
                dst = bass.AP(msrc_t, (512 + j * 128 + (r0 - 128)) * 2048 + cc0, [[2048, nrows], [1, 512]])
                wb = [msrc_cb[2 + j // 2]]
            p.dma("sp", ch, lambda e: e.dma_start(out=dst, in_=sg_ap), reads=[sgb], writes=wb)
            if cc0 + 512 == 2048 and j < 3:
                hd = bass.AP(msrc_t, 1024 * 2048 + r0 * 8 + (j + 1) * 2, [[8, nrows], [1, 2]])
                p.dma("sp", ch, lambda e: e.dma_start(out=hd, in_=sg_ap[:, 510:512]), reads=[sgb], writes=[msrc_cb[4]])

        def phase_A(l):
            with ExitStack() as st:
                c = Ctx(nc, st, p)
                wfm_sb = c.sb([128, 8, NFM], BF16); wfmb = p.buf()
                wtm_sb = c.sb([128, 8, NTM], BF16); wtmb = p.buf()
                g_sb = c.sb([128, 8], F32); gb = p.buf()
                sq = c.sb([128, 8, 512], BF16); sqb = p.bufs(8)
                hTs = [c.sb([128, 8, 512], BF16) for _ in range(4)]; hbs = [p.bufs(8) for _ in range(4)]
                rstd = c.sb([128, 512], F32); rstdb = p.buf()
                NST = 12
                stg = [c.sb([128, 512], BF16) for _ in range(NST)]; stgb = p.bufs(NST)
                stgt = [c.sb([128, 768], BF16) for _ in range(2)]; stgtb = p.bufs(2)
                stgg = [c.sb([128, 8], F32) for _ in range(2)]; stggb = p.bufs(2)
                psn = c.ps(); psnb = p.buf()
                NPS = 7
                pss = [c.ps() for _ in range(NPS)]; pssb = p.bufs(NPS)
                p.dma("sp", "g", lambda e: e.dma_start(out=g_sb[:, :], in_=g1[l]), writes=[gb])
                for k in range(8):
                    p.dma("pool", "wtm", lambda e, k=k: e.dma_start(out=wtm_sb[:, k, :], in_=wtm[l, k * 128:(k + 1) * 128, :]), writes=[wtmb])
                for k in range(8):
                    p.dma("pool", "wfm", lambda e, k=k: e.dma_start(out=wfm_sb[:, k, :], in_=wfm[l, k * 128:(k + 1) * 128, :]), writes=[wfmb])
                p.dma("sp", "xt", lambda e: e.dma_start(out=bass.AP(zsrcf_t, 8 * 2048, [[16, 128], [2, 8], [1, 2]]), in_=x_sb[:, :, TQH - 2:TQH]),
                      reads=list(xb), writes=[zsrc_cb[11]])
                p.dma("sp", "xz", lambda e: e.dma_start(out=bass.AP(zsrcf_t, 9 * 2048, [[16, 128], [1, 16]]), in_=zero_sb[:, :]), reads=[zerob], writes=[zsrc_cb[11]])
                for tt in range(4):
                    rmsnorm_fm(p, c, x_sb, None, 2 + tt * 512, 512, g_sb, gb, onesD_bf, onesDb, hTs[tt], hbs[tt], sq, sqb, psn, psnb, rstd, rstdb, xbs=xb)
                i_ps = 0
                i_st = 0
                i_tt = 0
                for tt in range(4):
                    hT, hb = hTs[tt], hbs[tt]
                    for s in range(4):
                        q = i_tt % 2; i_tt += 1
                        sgt, sgtb = stgt[q], stgtb[q]
                        sgg, sggb = stgg[q], stggb[q]
                        ps, psb = pss[i_ps % NPS], pssb[i_ps % NPS]; i_ps += 1
                        p.mm([lambda e, k=k, s=s, ps=ps, hT=hT: e.matmul(ps[:, :512], lhsT=hT[:, k, s * 128:(s + 1) * 128],
                                                                        rhs=wtm_sb[:, k, 0:512], start=(k == 0), stop=(k == 7))
                              for k in range(8)], reads=[wtmb] + list(hb), writes=[psb])
                        p.op("act", lambda e, ps=ps, sgt=sgt: e.copy(out=sgt[:, 0:512], in_=ps[:, :512]), reads=[psb], writes=[sgtb])
                        ps, psb = pss[i_ps % NPS], pssb[i_ps % NPS]; i_ps += 1
                        p.mm([lambda e, k=k, s=s, ps=ps, hT=hT: e.matmul(ps[:, :NTM - 512], lhsT=hT[:, k, s * 128:(s + 1) * 128],
                                                                        rhs=wtm_sb[:, k, 512:NTM], start=(k == 0), stop=(k == 7))
                              for k in range(8)], reads=[wtmb] + list(hb), writes=[psb])
                        p.op("act", lambda e, ps=ps, sgt=sgt: e.copy(out=sgt[:, 512:768], in_=ps[:, 0:256]), reads=[psb], writes=[sgtb])
                        p.op("act", lambda e, ps=ps, sgg=sgg: e.copy(out=sgg[:, 0:8], in_=ps[:, 256:264]), reads=[psb], writes=[sggb])
                        cidx = tt * 4 + s
                        dmv = bass.AP(zsrc_t, cidx * 64, [[16 * 64, 128], [128 * 16 * 64, 4], [1, 64]])
                        p.dma("sp", "ot%d" % q, lambda e, dmv=dmv, sgt=sgt: e.dma_start(
                            out=dmv, in_=sgt[:, 0:256].rearrange("p (g e) -> p g e", g=4)), reads=[sgtb], writes=[zsrc_cb[0]])
                        dav = bass.AP(zsrc_t, 256 * 2048 + cidx * 128, [[16 * 128, 128], [128 * 16 * 128, 4], [1, 128]])
                        p.dma("sp", "ot%d" % q, lambda e, dav=dav, sgt=sgt: e.dma_start(
                            out=dav, in_=sgt[:, 256:768].rearrange("p (g e) -> p g e", g=4)), reads=[sgtb], writes=[zsrc_cb[1], zsrc_cb[2]])
                        dgt = bass.AP(zsrcf_t, cidx * 2, [[32, 128], [128 * 32, 4], [1, 2]])
                        p.dma("sp", "og%d" % q, lambda e, dgt=dgt, sgg=sgg: e.dma_start(
                            out=dgt, in_=sgg[:, 0:8].rearrange("p (g e) -> p g e", g=4)), reads=[sggb], writes=[zsrc_cb[11]])
                ag_f()
                for i in range(3):
                    ag_chunk(zsrc_t, zall_t, RZ, i, zsrc_cb, zall_cb)
                for oc in range(NFM // 128):
                    for tt in range(4):
                        hT, hb = hTs[tt], hbs[tt]
                        ps, psb = pss[i_ps % NPS], pssb[i_ps % NPS]; i_ps += 1
                        p.mm([lambda e, k=k, oc=oc, ps=ps, hT=hT: e.matmul(ps[:, :], lhsT=wfm_sb[:, k, oc * 128:(oc + 1) * 128], rhs=hT[:, k, :],
                                                                            start=(k == 0), stop=(k == 7)) for k in range(8)],
                             reads=[wfmb] + list(hb), writes=[psb])
                        q = i_st % NST; i_st += 1
                        sg, sgb = stg[q], stgb[q]
                        p.op("act", lambda e, ps=ps, sg=sg: e.copy(out=sg[:, :], in_=ps[:, :]), reads=[psb], writes=[sgb])
                        p.dma("sp", "o%d" % q, lambda e, sg=sg, oc=oc, tt=tt: e.dma_start(
                            out=zsrc_t.ap()[768 + oc * 128:768 + (oc + 1) * 128, tt * 512:(tt + 1) * 512], in_=sg[:, :]), reads=[sgb], writes=[zsrc_cb[3 + oc // 2]])

        def phase_B(l):
            with ExitStack() as stB:
                cB = Ctx(nc, stB, p)
                NCH = T // 128
                prm_sb = cB.sb([128, NPRM], F32); prmb = p.buf()
                vaug = cB.sb([128, NCH, 128], BF16); vb = p.buf()
                vones = cB.sb([128, NCH, 128], BF16); vonesb = p.buf()
                g_sb = cB.sb([128, NCH, 2], F32); gsbb = p.buf()
                p.dma("sp", "prm", lambda e, l=l: e.dma_start(out=prm_sb[:, :], in_=prm[l]), writes=[prmb])
                p.op("dve", lambda e: e.memset(vones[:, :, 64:128], 1.0), writes=[vonesb])
                with ExitStack() as st:
                    c = Ctx(nc, st, p)
                    stm = [c.sb([128, 1024], BF16) for _ in range(2)]; stmb = p.bufs(2)
                    for r in range(4):
                        gather("tm%d" % (r % 2), stm[r % 2][:, :], zall1024, IDX_TM + r, 128, [zall_cb[0]], [stmb[r % 2]])
                        p.op("act", lambda e, r=r: e.copy(out=vones[:, r * 16:(r + 1) * 16, 0:64], in_=stm[r % 2].rearrange("p (c e) -> p c e", e=64)),
                             reads=[stmb[r % 2]], writes=[vonesb])
                        gather("av", vaug.rearrange("p c e -> p (c e)")[:, r * 2048:(r + 1) * 2048], zall2048, IDX_AV + r, 128, zall_cb[1:3], [vb])
                        gather("gt", g_sb.rearrange("p c t -> p (c t)")[:, r * 32:(r + 1) * 32], zallf32, IDX_GT + r, 128, [zall_cb[11]], [gsbb])
                ag_chunk(zsrc_t, zall_t, RZ, 3, zsrc_cb, zall_cb)
                p.barrier(exclude=("cc",))
                for r0 in (0, 128):
                    p.dma("sp", "hz", lambda e, r0=r0: e.dma_start(out=bass.AP(msrc_t, 1024 * 2048 + r0 * 8, [[8, 128], [1, 2]]), in_=zero_bf[:, 0:2]),
                          reads=[zerobb], writes=[msrc_cb[4]])
                with ExitStack() as st:
                    c = Ctx(nc, st, p)
                    PW = 2048
                    W = 16 + PW
                    xa = [c.sb([64, W], BF16) for _ in range(2)]; xab = p.bufs(2)
                    s_a = c.sb([64, W], F32); sab = p.buf()
                    s_b = c.sb([64, W], F32); sbb = p.buf()
                    acc = c.sb([64, W], F32); accb = p.buf()
                    d_bf = c.sb([64, PW], BF16); dbb = p.buf()
                    wp_bf = c.sb([64, 64], BF16); wpb = p.buf()
                    stg = [c.sb([64, 512], BF16) for _ in range(2)]; stgb = p.bufs(2)
                    pps = [c.ps() for _ in range(2)]; ppsb = p.bufs(2)
                    p.dma("pool", "wp", lambda e, l=l: e.dma_start(out=wp_bf[:, :], in_=poolw[l]), writes=[wpb])
                    p.op("dve", lambda e: e.memset(xa[0][:, 0:16], 0.0), writes=[xab[0]])
                    i_s = 0
                    for pc in range(T // PW):
                        X, Xb = xa[pc % 2], xab[pc % 2]
                        if pc > 0:
                            Xp, Xpb = xa[(pc - 1) % 2], xab[(pc - 1) % 2]
                            p.op("act", lambda e, X=X, Xp=Xp: e.copy(out=X[:, 0:16], in_=Xp[:, PW:PW + 16]), reads=[Xpb], writes=[Xb])
                        gather("px%d" % (pc % 2), X[0:64, 16:16 + PW], zall2048, IDX_POOL + pc, 64, [zall_cb[3]], [Xb])
                        if pc < 2:
                            ag_chunk(zsrc_t, zall_t, RZ, 4 + pc, zsrc_cb, zall_cb)
                        p.op("dve", lambda e, X=X: e.tensor_tensor(out=s_a[:, 1:W], in0=X[:, 1:W], in1=X[:, 0:W - 1], op=ALU.add), reads=[Xb], writes=[sab])
                        p.op("dve", lambda e: e.tensor_scalar(out=acc[:, 16:W], in0=s_a[:, 16:W], scalar1=prm_sb[0:64, 1:2], scalar2=None, op0=ALU.mult),
                             reads=[sab, prmb], writes=[accb])
                        src, srcb, dst, dstb = s_a, sab, s_b, sbb
                        lo = 1
                        for k, sh in ((1, 2), (2, 4), (3, 8)):
                            lo2 = lo + sh
                            p.op("dve", lambda e, src=src, dst=dst, lo2=lo2, sh=sh: e.tensor_tensor(
                                out=dst[:, lo2:W], in0=src[:, lo2:W], in1=src[:, lo2 - sh:W - sh], op=ALU.add), reads=[srcb], writes=[dstb])
                            p.op("dve", lambda e, dst=dst, k=k: e.scalar_tensor_tensor(
                                out=acc[:, 16:W], in0=dst[:, 16:W], scalar=prm_sb[0:64, 1 + k:2 + k], in1=acc[:, 16:W], op0=ALU.mult, op1=ALU.add),
                                reads=[dstb, prmb, accb], writes=[accb])
                            src, srcb, dst, dstb = dst, dstb, src, srcb
                            lo = lo2
                        if pc == 0:
                            p.op("dve", lambda e: e.tensor_tensor(out=acc[:, 16:32], in0=acc[:, 16:32], in1=prm_sb[0:64, 19:35], op=ALU.mult),
                                 reads=[accb, prmb], writes=[accb])
                        p.op("dve", lambda e, X=X: e.tensor_tensor(out=d_bf[:, :], in0=acc[:, 16:W], in1=X[:, 16:W], op=ALU.subtract), reads=[accb, Xb], writes=[dbb])
                        for q in range(PW // 512):
                            ps, psb = pps[i_s % 2], ppsb[i_s % 2]
                            sg, sgb = stg[i_s % 2], stgb[i_s % 2]
                            ch = "po%d" % (i_s % 2); i_s += 1
                            p.mm([lambda e, ps=ps, q=q: e.matmul(ps[0:64, :], lhsT=wp_bf[:, :], rhs=d_bf[:, q * 512:(q + 1) * 512], start=True, stop=True)],
                                 reads=[wpb, dbb], writes=[psb])
                            p.op("act", lambda e, ps=ps, sg=sg: e.activation(out=sg[:, :], in_=ps[0:64, :], func=AF.Copy, scale=prm_sb[0:64, 0:1]),
                                 reads=[psb, prmb], writes=[sgb])
                            out_piece(ch, sg[:, :], sgb, 0, 64, pc * PW + q * 512)
                p.barrier(exclude=("cc",))
                with ExitStack() as st:
                    c = Ctx(nc, st, p)
                    qT_bf = c.sb([64, T], BF16); qTb = p.bufs(4, "qT")
                    kT_bf = c.sb([64, T], BF16); kTb = p.bufs(4, "kT")
                    iv = c.sb([128, NCH], F32); ivb = p.buf()
                    lf = c.sb([128, NCH], F32); lfb_ = p.buf()
                    bias_s = c.sb([128, NCH], F32); biasb = p.buf()
                    wk = c.sb([128, NCH], F32); wkb = p.buf()
                    dec = c.sb([128, NCH], F32); decb = p.buf()
                    St = c.sb([64, 128], F32); Stb = p.buf()
                    St_bf2 = [c.sb([64, 128], BF16) for _ in range(2)]; Stbfb2 = p.bufs(2)
                    PW = 2048
                    xin = [c.sb([64, 3 + PW], BF16) for _ in range(2)]; xinb = p.bufs(2)
                    cacc = c.sb([64, PW], F32); caccb = p.buf()
                    n_x = 0
                    ag_chunk(zsrc_t, zall_t, RZ, 6, zsrc_cb, zall_cb)
                    for (icol, zcb, dst, dstbufs, c0col, scl) in ((IDX_MQ, [zall_cb[4]], qT_bf, qTb, 5, 1.0), (IDX_MK, [zall_cb[5]], kT_bf, kTb, 9, 0.125)):
                        for pc in range(T // PW):
                            X, Xb = xin[n_x % 2], xinb[n_x % 2]
                            if pc == 0:
                                p.op("dve", lambda e, X=X: e.memset(X[:, 0:3], 0.0), writes=[Xb])
                            else:
                                Xp, Xpb = xin[(n_x - 1) % 2], xinb[(n_x - 1) % 2]
                                p.op("act", lambda e, X=X, Xp=Xp: e.copy(out=X[:, 0:3], in_=Xp[:, PW:PW + 3]), reads=[Xpb], writes=[Xb])
                            gather("mx%d" % (n_x % 2), X[0:64, 3:3 + PW], zall2048, icol + pc, 64, zcb, [Xb])
                            n_x += 1
                            p.op("dve", lambda e, X=X, c0col=c0col: e.tensor_scalar(out=cacc[:, :], in0=X[:, 0:PW], scalar1=prm_sb[0:64, c0col:c0col + 1],
                                                                                     scalar2=None, op0=ALU.mult), reads=[Xb, prmb], writes=[caccb])
                            for j in (1, 2, 3):
                                p.op("dve", lambda e, X=X, c0col=c0col, j=j: e.scalar_tensor_tensor(
                                    out=cacc[:, :], in0=X[:, j:j + PW], scalar=prm_sb[0:64, c0col + j:c0col + j + 1], in1=cacc[:, :],
                                    op0=ALU.mult, op1=ALU.add), reads=[Xb, prmb, caccb], writes=[caccb])
                            p.op("act", lambda e: e.activation(out=cacc[:, :], in_=cacc[:, :], func=AF.Silu), reads=[caccb], writes=[caccb])
                            p.op("dve", lambda e, dst=dst, pc=pc, scl=scl: e.tensor_scalar(out=dst[:, pc * PW:(pc + 1) * PW], in0=cacc[:, :], scalar1=scl,
                                                                                           scalar2=None, op0=ALU.mult), reads=[caccb], writes=[dstbufs[pc]])
                    p.op("dve", lambda e: e.tensor_scalar(out=iv[:, :], in0=g_sb[:, :, 0], scalar1=prm_sb[:, 13:14], scalar2=None, op0=ALU.add),
                         reads=[gsbb, prmb], writes=[ivb])
                    p.op("dve", lambda e: e.tensor_scalar(out=lf[:, :], in0=g_sb[:, :, 1], scalar1=prm_sb[:, 14:15], scalar2=None, op0=ALU.add),
                         reads=[gsbb, prmb], writes=[lfb_])
                    p.op("act", lambda e: e.activation(out=lf[:, :], in_=lf[:, :], func=AF.Exp, scale=-1.0), reads=[lfb_], writes=[lfb_])
                    p.op("act", lambda e: e.activation(out=lf[:, :], in_=lf[:, :], func=AF.Ln, bias=1.0), reads=[lfb_], writes=[lfb_])
                    p.op("dve", lambda e: e.tensor_scalar(out=lf[:, :], in0=lf[:, :], scalar1=-1.0, scalar2=None, op0=ALU.mult), reads=[lfb_], writes=[lfb_])
                    psA = c.ps(); psAb = p.buf()
                    psB = c.ps(); psBb = p.buf()
                    p.mm([lambda e: e.matmul(psA[:, 0:NCH], lhsT=triU[:, :], rhs=lf[:, :], start=True, stop=True)], reads=[triUb, lfb_], writes=[psAb])
                    p.mm([lambda e: e.matmul(psB[:, 0:NCH], lhsT=ones_f[:, :], rhs=lf[:, :], start=True, stop=True)], reads=[onesfb, lfb_], writes=[psBb])
                    p.op("dve", lambda e: e.tensor_tensor(out=bias_s[:, :], in0=iv[:, :], in1=psA[:, 0:NCH], op=ALU.subtract), reads=[ivb, psAb], writes=[biasb])
                    p.op("dve", lambda e: e.tensor_tensor(out=wk[:, :], in0=bias_s[:, :], in1=psB[:, 0:NCH], op=ALU.add), reads=[biasb, psBb], writes=[wkb])
                    p.op("act", lambda e: e.activation(out=wk[:, :], in_=wk[:, :], func=AF.Exp), reads=[wkb], writes=[wkb])
                    p.op("act", lambda e: e.activation(out=dec[:, :], in_=psB[:, 0:NCH], func=AF.Exp), reads=[psBb], writes=[decb])
                    p.op("dve", lambda e: e.memset(St[:, :], 0.0), writes=[Stb])
                    p.op("dve", lambda e: e.memset(St_bf2[0][:, :], 0.0), writes=[Stbfb2[0]])
                    lfrep = [c.sb([128, 128], F32) for _ in range(2)]; lfrepb = p.bufs(2)
                    DT = [c.sb([128, 128], F32) for _ in range(2)]; DTb = p.bufs(2)
                    WT = [c.sb([128, 128], BF16) for _ in range(2)]; WTb = p.bufs(2)
                    eG = [c.sb([64, 128], F32) for _ in range(2)]; eGb = p.bufs(2)
                    qs = [c.sb([64, 128], BF16) for _ in range(2)]; qsb = p.bufs(2)
                    ksc = [c.sb([128, 64], BF16) for _ in range(2)]; kscb = p.bufs(2)
                    dn = [c.sb([64, 128], F32) for _ in range(2)]; dnb = p.bufs(2)
                    hTm = [c.sb([64, 512], F32) for _ in range(2)]; hTmb = p.bufs(2)
                    sqm = c.sb([64, 512], BF16); sqmb = p.buf()
                    rstdm = c.sb([64, 512], F32); rstdmb = p.buf()
                    mo_sb = [c.sb([64, 512], BF16) for _ in range(2)]; mob = p.bufs(2)
                    mo_f = [c.sb([64, 512], F32) for _ in range(2)]; mofb = p.bufs(2)
                    ho = [c.sb([64, 512], F32) for _ in range(2)]; hob = p.bufs(2)
                    ho_bf = [c.sb([64, 512], BF16) for _ in range(2)]; hobfb = p.bufs(2)
                    psG = [c.ps() for _ in range(2)]; psGb = p.bufs(2)
                    psN = [c.ps() for _ in range(2)]; psNb = p.bufs(2)
                    psK = [c.ps() for _ in range(2)]; psKb = p.bufs(2)

                    def stage1(ch):
                        q = ch % 2
                        t0 = ch * 128
                        pcq = ch // 16
                        p.op("dve", lambda e: e.tensor_scalar(out=lfrep[q][:, :], in0=ones_f[:, :], scalar1=lf[:, ch:ch + 1], scalar2=None, op0=ALU.mult),
                             reads=[onesfb, lfb_], writes=[lfrepb[q]])
                        p.mm([lambda e: e.matmul(psG[q][:, 0:128], lhsT=lfrep[q][:, :], rhs=triU[:, :], start=True, stop=True),
                              lambda e: e.matmul(psG[q][:, 128:256], lhsT=kT_bf[:, t0:t0 + 128], rhs=qT_bf[:, t0:t0 + 128], start=True, stop=True),
                              lambda e: e.matmul(psK[q][:, 0:64], lhsT=kT_bf[:, t0:t0 + 128], rhs=ident_bf[0:64, 0:64], start=True, stop=True)],
                             reads=[lfrepb[q], triUb, kTb[pcq], qTb[pcq], identb], writes=[psGb[q], psKb[q]])
                        p.op("act", lambda e: e.activation(out=DT[q][:, :], in_=psG[q][:, 0:128], func=AF.Exp, bias=bias_s[:, ch:ch + 1]),
                             reads=[psGb[q], biasb], writes=[DTb[q]])
                        p.op("act", lambda e: e.activation(out=eG[q][:, :], in_=psG[q][0:64, 0:128], func=AF.Exp), reads=[psGb[q]], writes=[eGb[q]])
                        p.op("act", lambda e: e.activation(out=ksc[q][:, :], in_=psK[q][:, 0:64], func=AF.Copy, scale=wk[:, ch:ch + 1]),
                             reads=[psKb[q], wkb], writes=[kscb[q]])
                        p.op("dve", lambda e: e.tensor_tensor(out=DT[q][:, :], in0=DT[q][:, :], in1=triU[:, :], op=ALU.mult), reads=[DTb[q], triUb], writes=[DTb[q]])
                        p.op("dve", lambda e: e.tensor_tensor(out=qs[q][:, :], in0=eG[q][:, :], in1=qT_bf[:, t0:t0 + 128], op=ALU.mult),
                             reads=[eGb[q], qTb[pcq]], writes=[qsb[q]])
                        p.op("dve", lambda e: e.tensor_tensor(out=WT[q][:, :], in0=DT[q][:, :], in1=psG[q][:, 128:256], op=ALU.mult), reads=[DTb[q], psGb[q]], writes=[WTb[q]])
                        p.mm([lambda e: e.matmul(psK[q][0:64, 64:192], lhsT=ksc[q][:, :], rhs=vones[:, ch, :], start=True, stop=True)],
                             reads=[kscb[q], vonesb], writes=[psKb[q]])

                    def stage2(ch):
                        q = ch % 2
                        St_bf, Stbfb = St_bf2[q], Stbfb2[q]
                        St_nx, Stnxb = St_bf2[1 - q], Stbfb2[1 - q]
                        p.mm([lambda e: e.matmul(psN[q][0:64, 0:128], lhsT=vones[:, ch, 0:64], rhs=WT[q][:, :], start=True, stop=False),
                              lambda e: e.matmul(psN[q][0:64, 0:128], lhsT=St_bf[:, 0:64], rhs=qs[q][:, :], start=False, stop=True),
                              lambda e: e.matmul(psN[q][0:64, 128:256], lhsT=vones[:, ch, 64:128], rhs=WT[q][:, :], start=True, stop=False),
                              lambda e: e.matmul(psN[q][0:64, 128:256], lhsT=St_bf[:, 64:128], rhs=qs[q][:, :], start=False, stop=True)],
                             reads=[vonesb, WTb[q], Stbfb, qsb[q]], writes=[psNb[q]])
                        p.op("dve", lambda e: e.scalar_tensor_tensor(out=St[:, :], in0=St[:, :], scalar=dec[0:64, ch:ch + 1], in1=psK[q][0:64, 64:192],
                                                                     op0=ALU.mult, op1=ALU.add), reads=[Stb, decb, psKb[q]], writes=[Stb])
                        p.op("act", lambda e: e.copy(out=St_nx[:, :], in_=St[:, :]), reads=[Stb], writes=[Stnxb])
                        hq = (ch // 4) % 2
                        hc = (ch % 4) * 128
                        p.op("act", lambda e: e.activation(out=dn[q][:, :], in_=psN[q][0:64, 128:256], func=AF.Abs), reads=[psNb[q]], writes=[dnb[q]])
                        p.op("dve", lambda e: e.tensor_scalar(out=dn[q][:, :], in0=dn[q][:, :], scalar1=1.0, scalar2=None, op0=ALU.max), reads=[dnb[q]], writes=[dnb[q]])
                        p.op("dve", lambda e: e.reciprocal(out=dn[q][:, :], in_=dn[q][:, :]), reads=[dnb[q]], writes=[dnb[q]])
                        p.op("dve", lambda e: e.tensor_tensor(out=hTm[hq][:, hc:hc + 128], in0=psN[q][0:64, 0:128], in1=dn[q][:, :], op=ALU.mult),
                             reads=[psNb[q], dnb[q]], writes=[hTmb[hq]])
                        if ch % 4 == 3:
                            pci = ch // 4
                            pt0 = pci * 512
                            H_, Hb_ = hTm[hq], hTmb[hq]
                            gather("mo%d" % hq, mo_sb[hq][0:64, :], zall512, IDX_MO + pci, 64, [zall_cb[6]], [mob[hq]])
                            p.op("act", lambda e: e.activation(out=sqm[:, :], in_=H_[:, :], func=AF.Square), reads=[Hb_], writes=[sqmb])
                            p.mm([lambda e: e.matmul(psA[0:64, :], lhsT=ones_bf[0:64, 0:64], rhs=sqm[:, :], start=True, stop=True)], reads=[onesbb, sqmb], writes=[psAb])
                            p.op("act", lambda e: e.activation(out=rstdm[:, :], in_=psA[0:64, :], func=AF.Ln, bias=EPS, scale=1.0 / 64), reads=[psAb], writes=[rstdmb])
                            p.op("act", lambda e: e.activation(out=rstdm[:, :], in_=rstdm[:, :], func=AF.Exp, scale=-0.5), reads=[rstdmb], writes=[rstdmb])
                            p.op("act", lambda e: e.activation(out=mo_f[hq][:, :], in_=mo_sb[hq][:, :], func=AF.Sigmoid), reads=[mob[hq]], writes=[mofb[hq]])
                            p.op("dve", lambda e: e.scalar_tensor_tensor(out=ho[hq][:, :], in0=H_[:, :], scalar=prm_sb[0:64, 15:16], in1=rstdm[:, :],
                                                                         op0=ALU.mult, op1=ALU.mult), reads=[Hb_, prmb, rstdmb], writes=[hob[hq]])
                            p.op("dve", lambda e: e.tensor_tensor(out=ho_bf[hq][:, :], in0=ho[hq][:, :], in1=mo_f[hq][:, :], op=ALU.mult),
                                 reads=[hob[hq], mofb[hq]], writes=[hobfb[hq]])
                            out_piece("ho%d" % hq, ho_bf[hq][:, :], hobfb[hq], 64, 64, pt0)
                            if pci < 4:
                                ag_chunk(zsrc_t, zall_t, RZ, 7 + pci, zsrc_cb, zall_cb)
                            elif pci == 4:
                                ag_chunk(msrc_t, mall_t, RM, 0, msrc_cb, mall_cb)

                    stage1(0)
                    for ch in range(NCH):
                        if ch + 1 < NCH:
                            stage1(ch + 1)
                        stage2(ch)
                p.barrier(exclude=("cc",))
                fused_moba(nc, p, gather, out_piece, prm_sb, prmb, ident_bf, identb, ones_bf, onesbb, blk, blkb, caus, causb, vaug, vb,
                           ind_d, zall512, zall_cb, lambda i: ag_chunk(msrc_t, mall_t, RM, i, msrc_cb, mall_cb),
                           None)

        for l in range(nlayers):
            phase_A(l)
            p.barrier(exclude=("cc",))
            if upto < 2:
                continue
            phase_B(l)
            p.barrier(exclude=("cc",))
            if upto < 3:
                continue
            fused_C(nc, p, l, gather, x_sb, xb, onesD_bf, onesDb, w_out, g2, w_up, convw, w_down, zall16, zall_cb, mall512, mall2, mall_cb)
            p.barrier(exclude=("cc",))
        for k in range(8):
            p.dma("sp", "out", lambda e, k=k: e.dma_start(out=xoT[k * 128:(k + 1) * 128, :], in_=x_sb[:, k, 2:TQH]), reads=[xb[k]], writes=[outb])
        p.finish("sp", [outb])
        p.emit()
    return nc


def fused_moba(nc, p, gather, out_piece, prm_sb, prmb, ident_bf, identb, ones_bf, onesbb, blk, blkb, caus, causb, vaug, vb, ind_d, zall512, zall_cb, ag2, agz24):
    NP = T // 512
    with ExitStack() as st:
        c = Ctx(nc, st, p)
        Kaug = [c.sb([128, T], BF16) for _ in range(2)]; Kb = [p.bufs(NP, "K%d" % h) for h in range(2)]
        Qaug = [c.sb([128, T], BF16) for _ in range(2)]; Qb = [p.bufs(NP, "Q%d" % h) for h in range(2)]
        kms = c.sb([128, 64], F32); kmsb = p.buf()
        AUX = ((64, 96), (0, 32))
        DAT = ((0, 64), (64, 128))
        for h in range(2):
            p.op("dve", lambda e, h=h: e.memset(Kaug[h][:, :], 0.0), writes=Kb[h])
            p.op("pool", lambda e, h=h: e.memset(Qaug[h][:, :], 0.0), writes=Qb[h])
        for h in range(2):
            a0, a1 = AUX[h]
            for q in range(T // 2048):
                p.dma("pool", "ind%d" % h, lambda e, h=h, a0=a0, a1=a1, q=q: e.dma_start(out=Kaug[h][a0:a1, q * 2048:(q + 1) * 2048],
                                                                                       in_=ind_d[:, q * 2048:(q + 1) * 2048]), writes=Kb[h])
        p.op("dve", lambda e: e.memset(kms[:, :], 0.0), writes=[kmsb])
        with ExitStack() as st2:
            c2 = Ctx(nc, st2, p)
            xin = [c2.sb([128, 512], BF16) for _ in range(2)]; xinb = p.bufs(2)
            sq = c2.sb([128, 512], BF16); sqb = p.buf()
            rstd = c2.sb([128, 512], F32); rstdb = p.buf()
            xn = [c2.sb([128, 512], F32) for _ in range(2)]; xnb = p.bufs(2)
            gsb = [c2.sb([128, 64], F32) for _ in range(2)]; gsbb = p.bufs(2)
            top8 = [c2.sb([128, 16], F32) for _ in range(2)]; top8b = p.bufs(2)
            nm = [c2.sb([128, 4, 128], BF16) for _ in range(2)]; nmb = p.bufs(2)
            psM = c2.ps(); psMb = p.buf()
            psGt = [c2.ps() for _ in range(2)]; psGtb = p.bufs(2)
            psT = [c2.ps() for _ in range(2)]; psTb = p.bufs(2)
            for q in range(2):
                p.op("pool", lambda e, q=q: e.memset(nm[q][:, :, :], 0.0), writes=[nmb[q]])
            xnq = [c2.sb([128, 512], F32) for _ in range(2)]; xnqb = p.bufs(2)
            cnt = {"in": 0, "g": 0}

            def normstage(pc):
                t0 = pc * 512
                for which in range(2):
                    icol = IDX_AK if which == 0 else IDX_AQ
                    zcb = zall_cb[9:11] if which == 0 else zall_cb[7:9]
                    gcol = 17 if which == 0 else 16
                    X, Xb = xin[cnt["in"] % 2], xinb[cnt["in"] % 2]
                    gather("ax%d" % (cnt["in"] % 2), X[:, :], zall512, icol + pc, 128, zcb, [Xb])
                    cnt["in"] += 1
                    p.op("act", lambda e, X=X: e.activation(out=sq[:, :], in_=X[:, :], func=AF.Square), reads=[Xb], writes=[sqb])
                    p.mm([lambda e: e.matmul(psM[:, :], lhsT=blk[:, :], rhs=sq[:, :], start=True, stop=True)], reads=[blkb, sqb], writes=[psMb])
                    p.op("act", lambda e: e.activation(out=rstd[:, :], in_=psM[:, :], func=AF.Ln, bias=EPS, scale=1.0 / 64), reads=[psMb], writes=[rstdb])
                    p.op("act", lambda e: e.activation(out=rstd[:, :], in_=rstd[:, :], func=AF.Exp, scale=-0.5), reads=[rstdb], writes=[rstdb])
                    if which == 0:
                        N_, Nb_ = xn[0], xnb[0]
                    else:
                        N_, Nb_ = xnq[pc % 2], xnqb[pc % 2]
                    p.op("dve", lambda e, X=X, N_=N_, gcol=gcol: e.scalar_tensor_tensor(out=N_[:, :], in0=X[:, :], scalar=prm_sb[:, gcol:gcol + 1], in1=rstd[:, :],
                                                                 op0=ALU.mult, op1=ALU.mult), reads=[Xb, prmb, rstdb], writes=[Nb_])
                    dst = Kaug if which == 0 else Qaug
                    dstb = Kb if which == 0 else Qb
                    for h in range(2):
                        d0, d1 = DAT[h]
                        p.op("act", lambda e, h=h, d0=d0, d1=d1, dst=dst, N_=N_: e.copy(out=dst[h][d0:d1, t0:t0 + 512], in_=N_[d0:d1, :]),
                             reads=[Nb_], writes=[dstb[h][pc]])
                    if which == 0:
                        for h in range(2):
                            p.op("dve", lambda e, h=h, N_=N_: e.tensor_reduce(
                                out=kms[h * 64:(h + 1) * 64, h * 32 + 2 * pc:h * 32 + 2 * pc + 2],
                                in_=N_[h * 64:(h + 1) * 64, :].rearrange("q (b k) -> q b k", k=256), axis=AX.X, op=ALU.add), reads=[Nb_], writes=[kmsb])
                if pc == 1:
                    ag2(1)

            def gatestage(pc):
                t0 = pc * 512
                QN, QNb = xnq[pc % 2], xnqb[pc % 2]
                nq = pc % 2
                for qb in range(4):
                    own = 2 * pc + qb // 2
                    gq = cnt["g"] % 2; cnt["g"] += 1
                    p.mm([lambda e, gq=gq, qb=qb: e.matmul(psGt[gq][:, 0:64], lhsT=QN[:, qb * 128:(qb + 1) * 128], rhs=kms[:, :], start=True, stop=True)],
                         reads=[QNb, kmsb], writes=[psGtb[gq]])
                    p.op("dve", lambda e, gq=gq: e.memset(gsb[gq][:, :], -1e30), writes=[gsbb[gq]])
                    if own > 0:
                        p.op("dve", lambda e, gq=gq, own=own: e.tensor_copy(out=gsb[gq].rearrange("q (h n) -> q h n", h=2)[:, :, 0:own],
                                                                            in_=psGt[gq][:, 0:64].rearrange("q (h n) -> q h n", h=2)[:, :, 0:own]),
                             reads=[psGtb[gq]], writes=[gsbb[gq]])
                    for h in range(2):
                        p.op("dve", lambda e, gq=gq, h=h: e.max(out=top8[gq][:, h * 8:(h + 1) * 8], in_=gsb[gq][:, h * 32:(h + 1) * 32]),
                             reads=[gsbb[gq]], writes=[top8b[gq]])
                    for h in range(2):
                        p.op("dve", lambda e, gq=gq, h=h: e.tensor_scalar(out=gsb[gq][:, h * 32:(h + 1) * 32], in0=gsb[gq][:, h * 32:(h + 1) * 32],
                                                                          scalar1=top8[gq][:, h * 8 + 2:h * 8 + 3], scalar2=None, op0=ALU.is_ge),
                             reads=[gsbb[gq], top8b[gq]], writes=[gsbb[gq]])
                    for h in range(2):
                        cc0 = 64 if h == 0 else 96
                        p.op("dve", lambda e, gq=gq, h=h, cc0=cc0, qb=qb: e.tensor_scalar(out=nm[nq][:, qb, cc0:cc0 + 32], in0=gsb[gq][:, h * 32:(h + 1) * 32],
                                                                                      scalar1=-1.0, scalar2=-NEG, op0=ALU.add, op1=ALU.mult),
                             reads=[gsbb[gq]], writes=[nmb[nq]])
                        p.op("dve", lambda e, cc0=cc0, qb=qb, own=own: e.memset(nm[nq][:, qb, cc0 + own:cc0 + own + 1], 0.0), writes=[nmb[nq]])
                        if own < 31:
                            p.op("dve", lambda e, cc0=cc0, qb=qb, own=own: e.memset(nm[nq][:, qb, cc0 + own + 1:cc0 + 32], NEG), writes=[nmb[nq]])

            def transstage(pc):
                t0 = pc * 512
                nq = pc % 2
                tq = pc % 2
                p.mm([lambda e, qb=qb: e.matmul(psT[tq][0:96, qb * 128:(qb + 1) * 128], lhsT=nm[nq][:, qb, 0:96], rhs=ident_bf[:, :], start=True, stop=True)
                      for qb in range(4)], reads=[nmb[nq], identb], writes=[psTb[tq]])
                p.op("act", lambda e: e.copy(out=Qaug[0][64:96, t0:t0 + 512], in_=psT[tq][64:96, :]), reads=[psTb[tq]], writes=[Qb[0][pc]])
                p.mm([lambda e, qb=qb: e.matmul(psT[tq][0:32, qb * 128:(qb + 1) * 128], lhsT=nm[nq][:, qb, 96:128], rhs=ident_bf[:, :], start=True, stop=True)
                      for qb in range(4)], reads=[nmb[nq], identb], writes=[psTb[tq]])
                p.op("act", lambda e: e.copy(out=Qaug[1][0:32, t0:t0 + 512], in_=psT[tq][0:32, :]), reads=[psTb[tq]], writes=[Qb[1][pc]])

            normstage(0)
            for pc in range(NP):
                if pc + 1 < NP:
                    normstage(pc + 1)
                if pc > 0:
                    transstage(pc - 1)
                gatestage(pc)
            transstage(NP - 1)
        p.barrier(exclude=("cc",))
        with ExitStack() as st3:
            c3 = Ctx(nc, st3, p)
            NPT = 4
            NPS = 2
            PT = [c3.sb([128, 1024], BF16) for _ in range(NPT)]; PTb = p.bufs(NPT)
            psS = [c3.ps([128, 1024]) for _ in range(NPS)]; psSb = p.bufs(NPS)
            psO = [c3.ps() for _ in range(2)]; psOb = p.bufs(2)
            psR = [c3.ps() for _ in range(2)]; psRb = p.bufs(2)
            rr = [c3.sb([64, 512], F32) for _ in range(2)]; rrb = p.bufs(2)
            ao = [c3.sb([64, 512], BF16) for _ in range(2)]; aob = p.bufs(2)
            tiles = []
            for pc in range(NP):
                for h in range(2):
                    for kt in range(0, 4 * (pc + 1), 2):
                        tiles.append((pc, h, kt))

            def stage1(i):
                pc, h, kt0 = tiles[i]
                rows = 96 if h == 0 else 128
                t0 = pc * 512
                sq_ = i % NPS
                pq = i % NPT
                p.mm([lambda e, u=u: e.matmul(psS[sq_][:, u * 512:(u + 1) * 512], lhsT=Kaug[h][0:rows, (kt0 + u) * 128:(kt0 + u + 1) * 128],
                                              rhs=Qaug[h][0:rows, t0:t0 + 512], start=True, stop=True) for u in range(2)],
                     reads=[Kb[h][kt0 // 4], Qb[h][pc]], writes=[psSb[sq_]])
                p.op("act", lambda e: e.activation(out=PT[pq][:, :], in_=psS[sq_][:, :], func=AF.Exp, scale=0.125),
                     reads=[psSb[sq_]], writes=[PTb[pq]])
                if kt0 >= 4 * pc:
                    r = kt0 - 4 * pc
                    p.op("dve", lambda e: e.tensor_tensor(out=PT[pq][:, :], in0=PT[pq][:, :], in1=caus[:, r * 512:(r + 2) * 512], op=ALU.mult),
                         reads=[PTb[pq], causb], writes=[PTb[pq]])

            def stage2(i):
                pc, h, kt0 = tiles[i]
                nkt = 4 * (pc + 1)
                t0 = pc * 512
                pq = i % NPT
                oq = (2 * pc + h) % 2
                p.mm([lambda e, u=u: e.matmul(psO[oq][0:64, :], lhsT=vaug[:, kt0 + u, h * 64:(h + 1) * 64], rhs=PT[pq][:, u * 512:(u + 1) * 512],
                                              start=(kt0 + u == 0), stop=(kt0 + u == nkt - 1)) for u in range(2)],
                     reads=[vb, PTb[pq]], writes=[psOb[oq]])
                p.mm([lambda e, u=u: e.matmul(psR[oq][0:64, :], lhsT=ones_bf[:, 0:64], rhs=PT[pq][:, u * 512:(u + 1) * 512],
                                              start=(kt0 + u == 0), stop=(kt0 + u == nkt - 1)) for u in range(2)],
                     reads=[onesbb, PTb[pq]], writes=[psRb[oq]])
                if kt0 + 2 == nkt:
                    p.op("dve", lambda e: e.reciprocal(out=rr[oq][:, :], in_=psR[oq][0:64, :]), reads=[psRb[oq]], writes=[rrb[oq]])
                    p.op("dve", lambda e: e.tensor_tensor(out=ao[oq][:, :], in0=psO[oq][0:64, :], in1=rr[oq][:, :], op=ALU.mult),
                         reads=[psOb[oq], rrb[oq]], writes=[aob[oq]])
                    out_piece("ao%d" % oq, ao[oq][:, :], aob[oq], 128 + h * 64, 64, t0)
                    if h == 1 and pc == 7:
                        ag2(2)

            LOOK = 1
            n = len(tiles)
            for i in range(min(LOOK, n)):
                stage1(i)
            for i in range(n):
                if i + LOOK < n:
                    stage1(i + LOOK)
                stage2(i)
            ag2(3)
            ag2(4)


def fused_C(nc, p, l, gather, x_sb, xb, ones_bf, onesb, w_out, g2, w_up, convw, w_down, zall16, zall_cb, mall512, mall2, mall_cb):
    with ExitStack() as st:
        c = Ctx(nc, st, p)
        wo_sb = c.sb([128, 8, D], BF16); wob = p.buf()
        wd_sb = c.sb([128, 22, D], BF16); wdb = p.buf()
        g_sb = c.sb([128, 8], F32); gb = p.buf()
        cw_sb = c.sb([128, 44, 3], F32); cwb = p.buf()
        xt = c.sb([128, 16], F32); xtb = p.buf()
        mh = c.sb([128, 8, 2], BF16); mhb = p.buf()
        mixb = [c.sb([128, 8, 512], BF16) for _ in range(2)]; mixbb = p.bufs(2)
        NWU = 3
        wu = [c.sb([128, 8, 256], BF16) for _ in range(NWU)]; wub = p.bufs(NWU)
        act = c.sb([128, 22, 512], BF16); actb = p.bufs(22, "act")
        hT = c.sb([128, 8, 512], BF16); hb = p.bufs(8, "h")
        rstd = c.sb([128, 512], F32); rstdb = p.buf()
        carry = c.sb([128, 22, 2, 2], F32); carryb = p.bufs(22, "cy")
        u2 = [c.sb([128, 2, 514], F32) for _ in range(2)]; u2b = p.bufs(2)
        tg = [c.sb([128, 512], F32) for _ in range(2)]; tgb = p.bufs(2)
        tv = [c.sb([128, 512], F32) for _ in range(2)]; tvb = p.bufs(2)
        psn = c.ps(); psnb = p.buf()
        psgv = [c.ps([128, 1024]) for _ in range(2)]; psgvb = p.bufs(2)
        pso = [c.ps() for _ in range(2)]; psob = p.bufs(2)
        p.dma("sp", "g", lambda e: e.dma_start(out=g_sb[:, :], in_=g2[l]), writes=[gb])
        p.dma("sp", "cw", lambda e: e.dma_start(out=cw_sb[:, :, :], in_=convw[l]), writes=[cwb])
        for k in range(8):
            p.dma("pool", "wo", lambda e, k=k: e.dma_start(out=wo_sb[:, k, :], in_=w_out[l, k * 128:(k + 1) * 128, :]), writes=[wob])
        p.op("dve", lambda e: e.memset(carry[:, :, :], 0.0), writes=list(carryb))
        gather("xt", xt[:, :], zall16, IDX_XT, 128, [zall_cb[11]], [xtb])
        p.op("act", lambda e: e.copy(out=x_sb[:, :, 0:2], in_=xt.rearrange("q (k t) -> q k t", t=2)), reads=[xtb], writes=list(xb))
        wd_loaded = False
        i_o = 0
        i_u = 0
        for ti, (col0, n) in enumerate(C_TILES):
            mb, mbb = mixb[ti % 2], mixbb[ti % 2]
            if ti == 0:
                for k in range(8):
                    gather("mh", mh[:, k, :], mall2, IDX_MH + k, 128, [mall_cb[4]], [mhb])
                p.op("act", lambda e, mb=mb: e.copy(out=mb[:, :, 0:2], in_=mh[:, :, :]), reads=[mhb], writes=[mbb])
            else:
                for k in range(8):
                    gather("mix%d" % (ti % 2), mb[:, k, :], mall512, IDX_MX + k * 4 + (ti - 1), 128, ([mall_cb[0]] if k < 2 else [mall_cb[1]] if k < 4 else mall_cb[2:4]), [mbb])
            for oc in range(8):
                ps, psb = pso[i_o % 2], psob[i_o % 2]; i_o += 1
                p.mm([lambda e, k=k, oc=oc, ps=ps, mb=mb, n=n: e.matmul(ps[:, :n], lhsT=wo_sb[:, k, oc * 128:(oc + 1) * 128], rhs=mb[:, k, :n],
                                                                       start=(k == 0), stop=(k == 7)) for k in range(8)],
                     reads=[wob, mbb], writes=[psb])
                p.op("dve", lambda e, oc=oc, ps=ps, col0=col0, n=n: e.tensor_tensor(
                    out=x_sb[:, oc, col0:col0 + n], in0=x_sb[:, oc, col0:col0 + n], in1=ps[:, :n], op=ALU.add),
                    reads=[psb, xb[oc]], writes=[xb[oc]])
            rmsnorm_fm(p, c, x_sb, None, col0, n, g_sb, gb, ones_bf, onesb, hT, hb, act, actb[:8], psn, psnb, rstd, rstdb, xbs=xb)
            if not wd_loaded:
                for fc in range(22):
                    p.dma("pool", "wd", lambda e, fc=fc: e.dma_start(out=wd_sb[:, fc, :], in_=w_down[l, fc * 128:(fc + 1) * 128, :]), writes=[wdb])
                wd_loaded = True
            for fc in range(22):
                q = i_u % 2
                w = i_u % NWU
                i_u += 1
                p.dma("pool", "wu%d" % w, lambda e, w=w, fc=fc: e.dma_start(out=wu[w][:, :, :], in_=w_up[l, fc]), writes=[wub[w]])
                p.mm([lambda e, k=k, q=q, w=w, n=n: e.matmul(psgv[q][:, 0:n], lhsT=wu[w][:, k, 0:128], rhs=hT[:, k, :n], start=(k == 0), stop=(k == 7))
                      for k in range(8)] +
                     [lambda e, k=k, q=q, w=w, n=n: e.matmul(psgv[q][:, 512:512 + n], lhsT=wu[w][:, k, 128:256], rhs=hT[:, k, :n], start=(k == 0), stop=(k == 7))
                      for k in range(8)], reads=[wub[w]] + list(hb), writes=[psgvb[q]])
                U, Ub = u2[q], u2b[q]
                p.op("act", lambda e, U=U, fc=fc: e.copy(out=U[:, :, 0:2], in_=carry[:, fc, :, :]), reads=[carryb[fc]], writes=[Ub])
                p.op("act", lambda e, U=U, q=q, n=n: e.copy(out=U[:, :, 2:2 + n], in_=psgv[q].rearrange("p (h w) -> p h w", h=2)[:, :, 0:n]),
                     reads=[psgvb[q]], writes=[Ub])
                p.op("act", lambda e, U=U, fc=fc, n=n: e.copy(out=carry[:, fc, :, :], in_=U[:, :, n:n + 2]), reads=[Ub], writes=[carryb[fc]])
                for (hh, t_, tb_, ch) in ((0, tg[q], tgb[q], fc), (1, tv[q], tvb[q], 22 + fc)):
                    p.op("dve", lambda e, U=U, t_=t_, hh=hh, ch=ch, n=n: e.tensor_scalar(
                        out=t_[:, :n], in0=U[:, hh, 0:n], scalar1=cw_sb[:, ch, 0:1], scalar2=None, op0=ALU.mult), reads=[Ub, cwb], writes=[tb_])
                for jj in (1, 2):
                    for (hh, t_, tb_, ch) in ((0, tg[q], tgb[q], fc), (1, tv[q], tvb[q], 22 + fc)):
                        p.op("dve", lambda e, U=U, t_=t_, hh=hh, ch=ch, n=n, jj=jj: e.scalar_tensor_tensor(
                            out=t_[:, :n], in0=U[:, hh, jj:jj + n], scalar=cw_sb[:, ch, jj:jj + 1], in1=t_[:, :n], op0=ALU.mult, op1=ALU.add),
                            reads=[Ub, cwb, tb_], writes=[tb_])
                p.op("act", lambda e, q=q, n=n: e.activation(out=tg[q][:, :n], in_=tg[q][:, :n], func=AF.Silu), reads=[tgb[q]], writes=[tgb[q]])
                p.op("dve", lambda e, q=q, fc=fc, n=n: e.tensor_tensor(out=act[:, fc, :n], in0=tg[q][:, :n], in1=tv[q][:, :n], op=ALU.mult),
                     reads=[tgb[q], tvb[q]], writes=[actb[fc]])
            for oc in range(8):
                ps, psb = pso[i_o % 2], psob[i_o % 2]; i_o += 1
                p.mm([lambda e, fc=fc, oc=oc, ps=ps, n=n: e.matmul(ps[:, :n], lhsT=wd_sb[:, fc, oc * 128:(oc + 1) * 128], rhs=act[:, fc, :n],
                                                                   start=(fc == 0), stop=(fc == 21)) for fc in range(22)],
                     reads=[wdb] + list(actb), writes=[psb])
                p.op("dve", lambda e, oc=oc, ps=ps, col0=col0, n=n: e.tensor_tensor(
                    out=x_sb[:, oc, col0:col0 + n], in0=x_sb[:, oc, col0:col0 + n], in1=ps[:, :n], op=ALU.add),
                    reads=[psb, xb[oc]], writes=[xb[oc]])


def fused_w_in_cols():
    fm, _ = w_in_cols()
    tm = list(range(768, 1024)) + list(range(2312, 2824))
    for g in range(4):
        tm += [1280 + g, 1284 + g]
    return np.array(fm), np.array(tm)


def ag_row(r, rho, nrows):
    rho = np.asarray(rho, dtype=np.int64)
    ch = rho // CHR
    n = np.minimum(CHR, nrows - ch * CHR)
    return 4 * ch * CHR + r * n + (rho - ch * CHR)


def fused_idx(cidx):
    r_ = cidx % 4
    g = j = r_
    pp = np.arange(128, dtype=np.int64)
    idx = np.zeros((128, NIDX), np.int64)
    gp = g * 128 + pp
    for r in range(4):
        idx[:, IDX_TM + r] = ag_row(r, gp // 2, RZ) * 2 + gp % 2
        idx[:, IDX_AV + r] = ag_row(r, 256 + gp, RZ)
        idx[:, IDX_GT + r] = (r * RF + gp // 64) * 64 + gp % 64
        idx[:, IDX_POOL + r] = ag_row(r, 768 + g * 64 + (pp % 64), RZ)
        idx[:, IDX_MQ + r] = ag_row(r, 768 + 256 + g * 64 + (pp % 64), RZ)
        idx[:, IDX_MK + r] = ag_row(r, 768 + 512 + g * 64 + (pp % 64), RZ)
    for pc in range(16):
        r, f = pc // 4, pc % 4
        idx[:, IDX_MO + pc] = ag_row(r, 768 + 768 + g * 64 + (pp % 64), RZ) * 4 + f
        idx[:, IDX_AQ + pc] = ag_row(r, 768 + 1024 + g * 128 + pp, RZ) * 4 + f
        idx[:, IDX_AK + pc] = ag_row(r, 768 + 1536 + g * 128 + pp, RZ) * 4 + f
    if j > 0:
        idx[:, IDX_XT] = ((j - 1) * RF + 8) * 128 + pp
    else:
        idx[:, IDX_XT] = (0 * RF + 9) * 128 + pp
    for k in range(8):
        f = k * 128 + pp
        gq = np.where(f < 256, f // 64, np.where(f < 512, (f - 256) // 64, (f - 512) // 128))
        rr = np.where(f < 256, f % 64, np.where(f < 512, 64 + (f - 256) % 64, 128 + (f - 512) % 128))
        srow = np.where(rr < 128, rr * 4 + j, 512 + j * 128 + (rr - 128))
        row = np.array([ag_row(int(a), int(b), RM) for a, b in zip(gq, srow)])
        hrow = np.array([ag_row(int(a), 1024, RM) for a in gq])
        for q in range(4):
            idx[:, IDX_MX + k * 4 + q] = row * 4 + q
        idx[:, IDX_MH + k] = hrow * 1024 + rr * 4 + j
    return idx.astype(np.uint32)


_FUSED = {}


def kernel(**inputs):
    inp = {k: np.ascontiguousarray(np.asarray(v), dtype=np.float32) for k, v in inputs.items()}
    x = inp["x"]
    fm, tm = fused_w_in_cols()
    wfm = np.ascontiguousarray(inp["w_in"][:, :, fm])
    wtm = np.ascontiguousarray(inp["w_in"][:, :, tm])
    g1 = np.ascontiguousarray(inp["ln1_g"].reshape(4, 8, 128).transpose(0, 2, 1))
    g2 = np.ascontiguousarray(inp["ln2_g"].reshape(4, 8, 128).transpose(0, 2, 1))
    cw = np.ascontiguousarray(inp["ffn_conv"].reshape(4, 3, 44, 128).transpose(0, 3, 2, 1))
    triU, ident, caus, ind = static_consts()
    w_up_p = np.ascontiguousarray(inp["w_up"].reshape(4, 8, 128, 2, 22, 128).transpose(0, 4, 2, 1, 3, 5).reshape(4, 22, 128, 8, 256))
    maps = []
    for cidx in range(NCORE):
        b, g = cidx // 4, cidx % 4
        j = g
        w = POOL_WINDOWS[g]
        prm = np.zeros((4, 128, NPRM), np.float32)
        for l in range(4):
            prm[l, 0:64, 0] = inp["pool_scale"][l, g * 64:(g + 1) * 64]
            prm[l, :, 1 + g] = 1.0 / w
            prm[l, 0:64, 5:9] = inp["m_conv"][l][:, g * 64:(g + 1) * 64].T
            prm[l, 0:64, 9:13] = inp["m_conv"][l][:, 256 + g * 64:256 + (g + 1) * 64].T
            prm[l, :, 13] = inp["m_b_i"][l, g]
            prm[l, :, 14] = inp["m_b_f"][l, g]
            prm[l, 0:64, 15] = inp["m_norm_g"][l, g * 64:(g + 1) * 64]
            prm[l, :, 16] = np.tile(inp["a_q_g"][l], 2)
            prm[l, :, 17] = np.tile(inp["a_k_g"][l], 2)
            prm[l, :, 19:35] = (w / np.minimum(np.arange(16) + 1, w)).astype(np.float32)[None, :]
        maps.append({
            "xT": np.ascontiguousarray(x[b, j * TQ:(j + 1) * TQ, :].T),
            "wfm": wfm, "wtm": wtm, "g1": g1, "w_out": inp["w_out"], "g2": g2, "w_up": w_up_p, "convw": cw, "w_down": inp["w_down"],
            "prm": prm, "poolw": np.ascontiguousarray(inp["pool_w"][:, g]),
            "triU": triU, "ident": ident, "caus": caus, "ind": ind, "idx": fused_idx(cidx),
        })
    if "nc" not in _FUSED:
        _FUSED["nc"] = build_fused()
    res = run_bass_kernel_spmd(_FUSED["nc"], maps, core_ids=list(range(NCORE))).results
    out = np.empty_like(x)
    for cidx in range(NCORE):
        b, j = cidx // 4, cidx % 4
        out[b, j * TQ:(j + 1) * TQ, :] = res[cidx]["xoT"].T
    return out
```

```python
from contextlib import ExitStack
import numpy as np
import concourse.bass as bass
import concourse.mybir as mybir
from concourse.bass_utils import run_bass_kernel_spmd

F32 = mybir.dt.float32
BF16 = mybir.dt.bfloat16
AF = mybir.ActivationFunctionType
ALU = mybir.AluOpType
AX = mybir.AxisListType

D = 1024
T = 8192
NCORE = 8
TQ = 2048
DFF = 2816
EPS = 1e-6
NEG = -10000.0


class Buf:
    __slots__ = ("name", "w", "r")

    def __init__(self, name):
        self.name = name
        self.w = None
        self.r = {}


class Prog:
    ENG = ("pe", "act", "dve", "pool", "sp")

    def __init__(self, nc, same_engine_sync=True):
        self.nc = nc
        self.lists = {e: [] for e in self.ENG}
        self.cnt = {}
        self.seen = {e: {} for e in self.ENG}
        self.same = same_engine_sync
        self.nbuf = 0

    def buf(self, name=None):
        self.nbuf += 1
        return Buf(name or f"b{self.nbuf}")

    def bufs(self, n, name="b"):
        return [self.buf(f"{name}{i}") for i in range(n)]

    def _dep(self, eng, tok):
        if tok is None:
            return
        k, v = tok
        if k == eng and (eng == "pe" or not self.same):
            return
        if self.seen[eng].get(k, 0) >= v:
            return
        self.seen[eng][k] = v
        self.lists[eng].append(("wait", k, v))

    def _issue(self, eng, key, inc, fn, reads, writes):
        for b in reads:
            self._dep(eng, b.w)
        for b in writes:
            self._dep(eng, b.w)
            for k, v in b.r.items():
                self._dep(eng, (k, v))
        self.cnt[key] = self.cnt.get(key, 0) + inc
        tok = (key, self.cnt[key])
        self.lists[eng].append(("op", fn, key, inc))
        for b in reads:
            if b.r.get(key, 0) < tok[1]:
                b.r[key] = tok[1]
        for b in writes:
            b.w = tok
            b.r = {}
        return tok

    def op(self, eng, fn, reads=(), writes=()):
        return self._issue(eng, eng, 1, fn, reads, writes)

    def dma(self, eng, ch, fn, reads=(), writes=()):
        return self._issue(eng, "d_" + ch, 16, fn, reads, writes)

    def cc(self, fn, reads=(), writes=()):
        return self._issue("pool", "cc", 1, fn, reads, writes)

    def mm(self, fns, reads=(), writes=()):
        for b in reads:
            self._dep("pe", b.w)
        for b in writes:
            self._dep("pe", b.w)
            for k, v in b.r.items():
                self._dep("pe", (k, v))
        for fn in fns[:-1]:
            self.lists["pe"].append(("op", fn, None, 0))
        self.cnt["pe"] = self.cnt.get("pe", 0) + 1
        tok = ("pe", self.cnt["pe"])
        self.lists["pe"].append(("op", fns[-1], "pe", 1))
        for b in reads:
            if b.r.get("pe", 0) < tok[1]:
                b.r["pe"] = tok[1]
        for b in writes:
            b.w = tok
            b.r = {}
        return tok

    def finish(self, eng, bufs):
        for b in bufs:
            self._dep(eng, b.w)
        self.finish_all(eng)

    def barrier(self, exclude=()):
        for e in self.ENG:
            for k, v in list(self.cnt.items()):
                if k not in exclude:
                    self._dep(e, (k, v))

    def wait_dma_all(self, eng):
        for k, v in list(self.cnt.items()):
            if k.startswith("d_"):
                self._dep(eng, (k, v))

    def finish_all(self, eng):
        for k, v in list(self.cnt.items()):
            if k.startswith("d_"):
                self._dep(eng, (k, v))

    def emit(self):
        nc = self.nc
        keys = sorted(self.cnt.keys())
        with ExitStack() as st:
            sems = {k: st.enter_context(nc.semaphore("s_" + k)) for k in keys}
            block = st.enter_context(nc.Block())

            def run(e, lst):
                for it in lst:
                    if it[0] == "wait":
                        e.wait_ge(sems[it[1]], it[2])
                    else:
                        ins = it[1](e)
                        if it[3]:
                            if it[2].startswith("cc"):
                                ins.then_inc(sems[it[2]])
                            else:
                                ins.then_inc(sems[it[2]], it[3])

            lists = self.lists

            @block.tensor
            def _(e):
                run(e, lists["pe"])

            @block.scalar
            def _(e):
                run(e, lists["act"])

            @block.vector
            def _(e):
                run(e, lists["dve"])

            @block.gpsimd
            def _(e):
                run(e, lists["pool"])

            @block.sync
            def _(e):
                run(e, lists["sp"])


class Ctx:
    N = [0]

    def __init__(self, nc, st, p):
        self.nc, self.st, self.p = nc, st, p

    def sb(self, shape, dt, name=None):
        Ctx.N[0] += 1
        return self.st.enter_context(self.nc.sbuf_tensor(name or f"sb{Ctx.N[0]}", list(shape), dt))

    def ps(self, shape=(128, 512), dt=F32, name=None):
        Ctx.N[0] += 1
        return self.st.enter_context(self.nc.psum_tensor(name or f"ps{Ctx.N[0]}", list(shape), dt))

    def dram_in(self, name, shape, dt=F32):
        return self.nc.dram_tensor(name, list(shape), dt, kind="ExternalInput").ap()

    def dram_out(self, name, shape, dt=F32):
        return self.nc.dram_tensor(name, list(shape), dt, kind="ExternalOutput").ap()


def rmsnorm_fm(p, c, x_sb, xb, col0, n, g_sb, gb, ones_bf, onesb, hT, hb, sq, sqb, ps, psb, rstd, rstdb, xbs=None):
    if xbs is None:
        xbs = [xb] * 8
    for k in range(8):
        p.op("act", lambda e, k=k: e.activation(out=sq[:, k, :n], in_=x_sb[:, k, col0:col0 + n], func=AF.Square),
             reads=[xbs[k]], writes=[sqb[k]])
    p.mm([lambda e, k=k: e.matmul(ps[:, :n], lhsT=ones_bf[:, :], rhs=sq[:, k, :n], start=(k == 0), stop=(k == 7))
          for k in range(8)], reads=[onesb] + list(sqb), writes=[psb])
    p.op("act", lambda e: e.activation(out=rstd[:, :n], in_=ps[:, :n], func=AF.Ln, bias=EPS), reads=[psb], writes=[rstdb])
    p.op("act", lambda e: e.activation(out=rstd[:, :n], in_=rstd[:, :n], func=AF.Exp, scale=-0.5), reads=[rstdb], writes=[rstdb])
    for k in range(8):
        p.op("dve", lambda e, k=k: e.scalar_tensor_tensor(out=hT[:, k, :n], in0=x_sb[:, k, col0:col0 + n],
                                                          scalar=g_sb[:, k:k + 1], in1=rstd[:, :n],
                                                          op0=ALU.mult, op1=ALU.mult),
             reads=[xbs[k], gb, rstdb], writes=[hb[k]])


NFM = 2048
NTM = 776


def build_A():
    nc = bass.Bass("TRN2", target_bir_lowering=False)
    p = Prog(nc)
    with ExitStack() as st:
        c = Ctx(nc, st, p)
        xT = c.dram_in("xT", [D, TQ])
        wfm = c.dram_in("wfm", [D, NFM])
        wtm = c.dram_in("wtm", [D, NTM])
        g1 = c.dram_in("g1", [128, 8])
        zfm = c.dram_out("zfm", [NFM, TQ])
        ztm = c.dram_out("ztm", [TQ, NTM])

        x_sb = c.sb([128, 8, TQ], F32); xb = p.buf()
        wfm_sb = c.sb([128, 8, NFM], BF16); wfmb = p.buf()
        wtm_sb = c.sb([128, 8, NTM], BF16); wtmb = p.buf()
        g_sb = c.sb([128, 8], F32); gb = p.buf()
        ones_bf = c.sb([128, 128], BF16); onesb = p.buf()
        sq = c.sb([128, 8, 512], BF16); sqb = p.bufs(8)
        hT = c.sb([128, 8, 512], BF16); hb = p.bufs(8)
        rstd = c.sb([128, 512], F32); rstdb = p.buf()
        NST = 4
        stg = [c.sb([128, 512], F32) for _ in range(NST)]; stgb = p.bufs(NST)
        psn = c.ps(); psnb = p.buf()
        NPS = 4
        pss = [c.ps() for _ in range(NPS)]; pssb = p.bufs(NPS)
        zfmb = p.buf(); ztmb = p.buf()

        p.dma("sp", "x", lambda e: e.dma_start(out=x_sb[:, :, :], in_=xT.rearrange("(k q) t -> q k t", q=128)), writes=[xb])
        p.dma("sp", "g", lambda e: e.dma_start(out=g_sb[:, :], in_=g1), writes=[gb])
        for k in range(8):
            p.dma("pool", "wfm", lambda e, k=k: e.dma_start(out=wfm_sb[:, k, :], in_=wfm[k * 128:(k + 1) * 128, :]), writes=[wfmb])
        for k in range(8):
            p.dma("pool", "wtm", lambda e, k=k: e.dma_start(out=wtm_sb[:, k, :], in_=wtm[k * 128:(k + 1) * 128, :]), writes=[wtmb])
        p.op("dve", lambda e: e.memset(ones_bf[:, :], 1.0 / D), writes=[onesb])

        i_ps = 0
        i_st = 0
        for tt in range(TQ // 512):
            col0 = tt * 512
            rmsnorm_fm(p, c, x_sb, xb, col0, 512, g_sb, gb, ones_bf, onesb, hT, hb, sq, sqb, psn, psnb, rstd, rstdb)
            for oc in range(NFM // 128):
                ps, psb = pss[i_ps % NPS], pssb[i_ps % NPS]; i_ps += 1
                p.mm([lambda e, k=k, oc=oc, ps=ps: e.matmul(ps[:, :], lhsT=wfm_sb[:, k, oc * 128:(oc + 1) * 128], rhs=hT[:, k, :],
                                                             start=(k == 0), stop=(k == 7)) for k in range(8)],
                     reads=[wfmb] + list(hb), writes=[psb])
                sg, sgb = stg[i_st % NST], stgb[i_st % NST]; i_st += 1
                p.op("act", lambda e, ps=ps, sg=sg: e.copy(out=sg[:, :], in_=ps[:, :]), reads=[psb], writes=[sgb])
                p.dma("sp", "o%d" % (i_st % NST), lambda e, sg=sg, oc=oc, col0=col0: e.dma_start(
                    out=zfm[oc * 128:(oc + 1) * 128, col0:col0 + 512], in_=sg[:, :]), reads=[sgb], writes=[zfmb])
            for s in range(4):
                for (c0, n) in ((0, 512), (512, NTM - 512)):
                    ps, psb = pss[i_ps % NPS], pssb[i_ps % NPS]; i_ps += 1
                    p.mm([lambda e, k=k, s=s, c0=c0, n=n, ps=ps: e.matmul(ps[:, :n], lhsT=hT[:, k, s * 128:(s + 1) * 128],
                                                                         rhs=wtm_sb[:, k, c0:c0 + n], start=(k == 0), stop=(k == 7))
                          for k in range(8)], reads=[wtmb] + list(hb), writes=[psb])
                    sg, sgb = stg[i_st % NST], stgb[i_st % NST]; i_st += 1
                    p.op("act", lambda e, ps=ps, sg=sg, n=n: e.copy(out=sg[:, :n], in_=ps[:, :n]), reads=[psb], writes=[sgb])
                    r0 = col0 + s * 128
                    p.dma("sp", "o%d" % (i_st % NST), lambda e, sg=sg, r0=r0, c0=c0, n=n: e.dma_start(
                        out=ztm[r0:r0 + 128, c0:c0 + n], in_=sg[:, :n]), reads=[sgb], writes=[ztmb])
        p.finish("sp", [zfmb, ztmb])
        p.emit()
    return nc


def build_moba(nc, p, prm_sb, prmb, ident_bf, identb, ones_bf, onesbb, aqT, akT, av, caus_d, ind_d, mixT, outb):
    import os
    STAGE = int(os.environ.get('MOBA_STAGE', '9'))
    NP = T // 512
    with ExitStack() as st:
        c = Ctx(nc, st, p)
        Kaug = [c.sb([128, T], BF16, name="Kaug%d" % _) for _ in range(2)]; Kb = [p.bufs(NP, "K%d" % h) for h in range(2)]
        Qaug = [c.sb([128, T], BF16, name="Qaug%d" % _) for _ in range(2)]; Qb = [p.bufs(NP, "Q%d" % h) for h in range(2)]
        vaug = c.sb([128, T // 128, 128], BF16, name="vaug"); vb = p.buf()
        caus = c.sb([128, 2048], BF16, name="caus_sb"); causb = p.buf()
        kms = c.sb([128, 64], F32, name="kms_sb"); kmsb = p.buf()
        AUX = ((64, 96), (0, 32))
        DAT = ((0, 64), (64, 128))
        for h in range(2):
            p.op("dve", lambda e, h=h: e.memset(Kaug[h][:, :], 0.0), writes=Kb[h])
            p.op("pool", lambda e, h=h: e.memset(Qaug[h][:, :], 0.0), writes=Qb[h])
        for h in range(2):
            a0, a1 = AUX[h]
            for q in range(T // 2048):
                p.dma("pool", "ind%d" % h, lambda e, h=h, a0=a0, a1=a1, q=q: e.dma_start(out=Kaug[h][a0:a1, q * 2048:(q + 1) * 2048],
                                                                               in_=ind_d[:, q * 2048:(q + 1) * 2048]), writes=Kb[h])
        p.dma("pool", "caus", lambda e: e.dma_start(out=caus[:, :], in_=caus_d), writes=[causb])
        for q in range(T // 2048):
            p.dma("pool", "vaug", lambda e, q=q: e.dma_start(out=vaug[:, q * 16:(q + 1) * 16, :],
                                                           in_=av.rearrange("(c q) e -> q c e", q=128)[:, q * 16:(q + 1) * 16, :]), writes=[vb])
        p.op("dve", lambda e: e.memset(kms[:, :], 0.0), writes=[kmsb])
        with ExitStack() as st2:
            c2 = Ctx(nc, st2, p)
            blk = c2.sb([128, 128], BF16); blkb = p.buf()
            p.op("dve", lambda e: e.memset(blk[:, :], 0.0), writes=[blkb])
            p.op("dve", lambda e: e.memset(blk[0:64, 0:64], 1.0), writes=[blkb])
            p.op("dve", lambda e: e.memset(blk[64:128, 64:128], 1.0), writes=[blkb])
            xin = [c2.sb([128, 512], F32) for _ in range(2)]; xinb = p.bufs(2)
            sq = c2.sb([128, 512], BF16); sqb = p.buf()
            rstd = c2.sb([128, 512], F32); rstdb = p.buf()
            xn = [c2.sb([128, 512], F32) for _ in range(2)]; xnb = p.bufs(2)
            gsb = [c2.sb([128, 64], F32) for _ in range(2)]; gsbb = p.bufs(2)
            top8 = [c2.sb([128, 16], F32) for _ in range(2)]; top8b = p.bufs(2)
            nm = [c2.sb([128, 4, 128], BF16) for _ in range(2)]; nmb = p.bufs(2)
            psM = c2.ps(); psMb = p.buf()
            psGt = [c2.ps() for _ in range(2)]; psGtb = p.bufs(2)
            psT = [c2.ps() for _ in range(2)]; psTb = p.bufs(2)
            for q in range(2):
                p.op("pool", lambda e, q=q: e.memset(nm[q][:, :, :], 0.0), writes=[nmb[q]])
            n_in = 0
            n_g = 0
            for pc in range(NP if STAGE >= 1 else 0):
                t0 = pc * 512
                for which in range(2):
                    src = akT if which == 0 else aqT
                    gcol = 17 if which == 0 else 16
                    X, Xb = xin[n_in % 2], xinb[n_in % 2]
                    p.dma("sp", "ax%d" % (n_in % 2), lambda e, X=X, src=src, t0=t0: e.dma_start(out=X[:, :], in_=src[:, t0:t0 + 512]), writes=[Xb])
                    n_in += 1
                    p.op("act", lambda e, X=X: e.activation(out=sq[:, :], in_=X[:, :], func=AF.Square), reads=[Xb], writes=[sqb])
                    p.mm([lambda e: e.matmul(psM[:, :], lhsT=blk[:, :], rhs=sq[:, :], start=True, stop=True)], reads=[blkb, sqb], writes=[psMb])
                    p.op("act", lambda e: e.activation(out=rstd[:, :], in_=psM[:, :], func=AF.Ln, bias=EPS, scale=1.0 / 64), reads=[psMb], writes=[rstdb])
                    p.op("act", lambda e: e.activation(out=rstd[:, :], in_=rstd[:, :], func=AF.Exp, scale=-0.5), reads=[rstdb], writes=[rstdb])
                    N_, Nb_ = xn[which], xnb[which]
                    p.op("dve", lambda e, X=X, N_=N_, gcol=gcol: e.scalar_tensor_tensor(out=N_[:, :], in0=X[:, :], scalar=prm_sb[:, gcol:gcol + 1], in1=rstd[:, :],
                                                                                         op0=ALU.mult, op1=ALU.mult), reads=[Xb, prmb, rstdb], writes=[Nb_])
                    dst = Kaug if which == 0 else Qaug
                    dstb = Kb if which == 0 else Qb
                    for h in range(2):
                        d0, d1 = DAT[h]
                        p.op("act", lambda e, h=h, d0=d0, d1=d1, dst=dst, N_=N_, t0=t0: e.copy(out=dst[h][d0:d1, t0:t0 + 512], in_=N_[d0:d1, :]),
                             reads=[Nb_], writes=[dstb[h][pc]])
                    if which == 0:
                        for h in range(2):
                            p.op("dve", lambda e, N_=N_, pc=pc, h=h: e.tensor_reduce(
                                out=kms[h * 64:(h + 1) * 64, h * 32 + 2 * pc:h * 32 + 2 * pc + 2],
                                in_=N_[h * 64:(h + 1) * 64, :].rearrange("q (b k) -> q b k", k=256), axis=AX.X, op=ALU.add), reads=[Nb_], writes=[kmsb])
                if STAGE < 2:
                    continue
                QN, QNb = xn[1], xnb[1]
                nq = pc % 2
                for qb in range(4):
                    own = 2 * pc + qb // 2
                    gq = n_g % 2; n_g += 1
                    p.mm([lambda e, gq=gq, qb=qb: e.matmul(psGt[gq][:, 0:64], lhsT=QN[:, qb * 128:(qb + 1) * 128], rhs=kms[:, :], start=True, stop=True)],
                         reads=[QNb, kmsb], writes=[psGtb[gq]])
                    p.op("dve", lambda e, gq=gq: e.memset(gsb[gq][:, :], -1e30), writes=[gsbb[gq]])
                    if own > 0:
                        p.op("dve", lambda e, gq=gq, own=own: e.tensor_copy(out=gsb[gq].rearrange("q (h n) -> q h n", h=2)[:, :, 0:own],
                                                                            in_=psGt[gq][:, 0:64].rearrange("q (h n) -> q h n", h=2)[:, :, 0:own]),
                             reads=[psGtb[gq]], writes=[gsbb[gq]])
                    for h in range(2):
                        p.op("dve", lambda e, gq=gq, h=h: e.max(out=top8[gq][:, h * 8:(h + 1) * 8], in_=gsb[gq][:, h * 32:(h + 1) * 32]),
                             reads=[gsbb[gq]], writes=[top8b[gq]])
                        c0 = 64 if h == 0 else 96
                        p.op("dve", lambda e, gq=gq, h=h: e.tensor_scalar(out=gsb[gq][:, h * 32:(h + 1) * 32], in0=gsb[gq][:, h * 32:(h + 1) * 32],
                                                                          scalar1=top8[gq][:, h * 8 + 2:h * 8 + 3], scalar2=None, op0=ALU.is_ge),
                             reads=[gsbb[gq], top8b[gq]], writes=[gsbb[gq]])
                        p.op("dve", lambda e, gq=gq, h=h, c0=c0, qb=qb, nq=nq: e.tensor_scalar(out=nm[nq][:, qb, c0:c0 + 32], in0=gsb[gq][:, h * 32:(h + 1) * 32],
                                                                                       scalar1=-1.0, scalar2=-NEG, op0=ALU.add, op1=ALU.mult),
                             reads=[gsbb[gq]], writes=[nmb[nq]])
                        p.op("dve", lambda e, h=h, c0=c0, qb=qb, nq=nq, own=own: e.memset(nm[nq][:, qb, c0 + own:c0 + own + 1], 0.0), writes=[nmb[nq]])
                        if own < 31:
                            p.op("dve", lambda e, h=h, c0=c0, qb=qb, nq=nq, own=own: e.memset(nm[nq][:, qb, c0 + own + 1:c0 + 32], NEG), writes=[nmb[nq]])
                if STAGE < 3:
                    continue
                tq = pc % 2
                p.mm([lambda e, tq=tq, qb=qb, nq=nq: e.matmul(psT[tq][0:96, qb * 128:(qb + 1) * 128], lhsT=nm[nq][:, qb, 0:96], rhs=ident_bf[:, :], start=True, stop=True)
                      for qb in range(4)], reads=[nmb[nq], identb], writes=[psTb[tq]])
                p.op("act", lambda e, tq=tq, t0=t0: e.copy(out=Qaug[0][64:96, t0:t0 + 512], in_=psT[tq][64:96, :]), reads=[psTb[tq]], writes=[Qb[0][pc]])
                p.mm([lambda e, tq=tq, qb=qb, nq=nq: e.matmul(psT[tq][0:32, qb * 128:(qb + 1) * 128], lhsT=nm[nq][:, qb, 96:128], rhs=ident_bf[:, :], start=True, stop=True)
                      for qb in range(4)], reads=[nmb[nq], identb], writes=[psTb[tq]])
                p.op("act", lambda e, tq=tq, t0=t0: e.copy(out=Qaug[1][0:32, t0:t0 + 512], in_=psT[tq][0:32, :]), reads=[psTb[tq]], writes=[Qb[1][pc]])
        p.barrier()
        with ExitStack() as st3:
            c3 = Ctx(nc, st3, p)
            NPT = 3
            PT = [c3.sb([128, 512], BF16) for _ in range(NPT)]; PTb = p.bufs(NPT)
            psS = [c3.ps() for _ in range(2)]; psSb = p.bufs(2)
            psO = [c3.ps() for _ in range(2)]; psOb = p.bufs(2)
            psR = [c3.ps() for _ in range(2)]; psRb = p.bufs(2)
            rr = [c3.sb([64, 512], F32) for _ in range(2)]; rrb = p.bufs(2)
            ao = [c3.sb([64, 512], F32) for _ in range(2)]; aob = p.bufs(2)
            n_s = 0
            n_o = 0
            for pc in range(NP if STAGE >= 4 else 0):
                t0 = pc * 512
                nkt = 4 * (pc + 1)
                for h in range(2):
                    rows = 96 if h == 0 else 128
                    oq = n_o % 2; n_o += 1
                    for kt in range(nkt):
                        sq_ = n_s % 2
                        pq = n_s % NPT
                        n_s += 1
                        p.mm([lambda e, sq_=sq_, h=h, rows=rows, kt=kt, t0=t0: e.matmul(psS[sq_][:, :], lhsT=Kaug[h][0:rows, kt * 128:(kt + 1) * 128],
                                                                                         rhs=Qaug[h][0:rows, t0:t0 + 512], start=True, stop=True)],
                             reads=[Kb[h][kt // 4], Qb[h][pc]], writes=[psSb[sq_]])
                        p.op("act", lambda e, sq_=sq_, pq=pq: e.activation(out=PT[pq][:, :], in_=psS[sq_][:, :], func=AF.Exp, scale=0.125),
                             reads=[psSb[sq_]], writes=[PTb[pq]])
                        if kt >= 4 * pc:
                            r = kt - 4 * pc
                            p.op("dve", lambda e, pq=pq, r=r: e.tensor_tensor(out=PT[pq][:, :], in0=PT[pq][:, :], in1=caus[:, r * 512:(r + 1) * 512], op=ALU.mult),
                                 reads=[PTb[pq], causb], writes=[PTb[pq]])
                        p.mm([lambda e, oq=oq, h=h, kt=kt, pq=pq, nkt=nkt: e.matmul(psO[oq][0:64, :], lhsT=vaug[:, kt, h * 64:(h + 1) * 64], rhs=PT[pq][:, :],
                                                                                     start=(kt == 0), stop=(kt == nkt - 1))],
                             reads=[vb, PTb[pq]], writes=[psOb[oq]])
                        p.mm([lambda e, oq=oq, kt=kt, pq=pq, nkt=nkt: e.matmul(psR[oq][0:64, :], lhsT=ones_bf[:, 0:64], rhs=PT[pq][:, :],
                                                                                start=(kt == 0), stop=(kt == nkt - 1))],
                             reads=[onesbb, PTb[pq]], writes=[psRb[oq]])
                    p.op("dve", lambda e, oq=oq: e.reciprocal(out=rr[oq][:, :], in_=psR[oq][0:64, :]), reads=[psRb[oq]], writes=[rrb[oq]])
                    p.op("dve", lambda e, oq=oq: e.tensor_tensor(out=ao[oq][:, :], in0=psO[oq][0:64, :], in1=rr[oq][:, :], op=ALU.mult),
                         reads=[psOb[oq], rrb[oq]], writes=[aob[oq]])
                    p.dma("sp", "ao%d" % oq, lambda e, oq=oq, h=h, t0=t0: e.dma_start(out=mixT[128 + h * 64:192 + h * 64, t0:t0 + 512], in_=ao[oq][:, :]),
                          reads=[aob[oq]], writes=[outb])


def w_in_cols():
    fm = list(range(0, 256)) + list(range(256, 768)) + list(range(1024, 1280)) + list(range(1288, 1288 + 1024))
    tm = list(range(768, 1024)) + list(range(1288 + 1024, 1288 + 1536)) + list(range(1280, 1288))
    return np.array(fm), np.array(tm)


def split_w_in(w_in):
    fm, tm = w_in_cols()
    return np.ascontiguousarray(w_in[:, fm]), np.ascontiguousarray(w_in[:, tm])


TQH = TQ + 2
C_TILES = [(0, 2)] + [(2 + 512 * i, 512) for i in range(4)]


def build_C():
    nc = bass.Bass("TRN2", target_bir_lowering=False)
    p = Prog(nc)
    with ExitStack() as st:
        c = Ctx(nc, st, p)
        xT = c.dram_in("xT", [D, TQH])
        mixT = c.dram_in("mixT", [D, TQH])
        w_out = c.dram_in("w_out", [D, D])
        g2 = c.dram_in("g2", [128, 8])
        w_up = c.dram_in("w_up", [D, 2 * DFF])
        convw = c.dram_in("convw", [128, 44, 3])
        w_down = c.dram_in("w_down", [DFF, D])
        xoT = c.dram_out("xoT", [D, TQ])

        x_sb = c.sb([128, 8, TQH], F32); xb = p.bufs(8, "x")
        wo_sb = c.sb([128, 8, D], BF16); wob = p.buf()
        wd_sb = c.sb([128, 22, D], BF16); wdb = p.buf()
        g_sb = c.sb([128, 8], F32); gb = p.buf()
        cw_sb = c.sb([128, 44, 3], F32); cwb = p.buf()
        ones_bf = c.sb([128, 128], BF16); onesb = p.buf()
        mixb = [c.sb([128, 8, 512], BF16) for _ in range(2)]; mixbb = p.bufs(2)
        wu = [c.sb([128, 8, 256], BF16) for _ in range(2)]; wub = p.bufs(2)
        act = c.sb([128, 22, 512], BF16); actb = p.bufs(22, "act")
        hT = c.sb([128, 8, 512], BF16); hb = p.bufs(8, "h")
        rstd = c.sb([128, 512], F32); rstdb = p.buf()
        carry = c.sb([128, 44, 2], F32); carryb = p.bufs(44, "cy")
        ug = [c.sb([128, 514], F32) for _ in range(2)]; ugb = p.bufs(2)
        uv = [c.sb([128, 514], F32) for _ in range(2)]; uvb = p.bufs(2)
        tg = [c.sb([128, 512], F32) for _ in range(2)]; tgb = p.bufs(2)
        tv = [c.sb([128, 512], F32) for _ in range(2)]; tvb = p.bufs(2)
        psn = c.ps(); psnb = p.buf()
        psg = [c.ps() for _ in range(2)]; psgb = p.bufs(2)
        psv = [c.ps() for _ in range(2)]; psvb = p.bufs(2)
        pso = [c.ps() for _ in range(2)]; psob = p.bufs(2)
        outb = p.buf()

        for k in range(8):
            p.dma("sp", "x%d" % k, lambda e, k=k: e.dma_start(out=x_sb[:, k, :], in_=xT[k * 128:(k + 1) * 128, :]), writes=[xb[k]])
        p.dma("sp", "g", lambda e: e.dma_start(out=g_sb[:, :], in_=g2), writes=[gb])
        p.dma("sp", "cw", lambda e: e.dma_start(out=cw_sb[:, :, :], in_=convw), writes=[cwb])
        for k in range(8):
            p.dma("pool", "wo", lambda e, k=k: e.dma_start(out=wo_sb[:, k, :], in_=w_out[k * 128:(k + 1) * 128, :]), writes=[wob])
        p.op("dve", lambda e: e.memset(ones_bf[:, :], 1.0 / D), writes=[onesb])
        p.op("dve", lambda e: e.memset(carry[:, :, :], 0.0), writes=list(carryb))
        wd_loaded = False

        i_o = 0
        i_u = 0
        for ti, (col0, n) in enumerate(C_TILES):
            mb, mbb = mixb[ti % 2], mixbb[ti % 2]
            p.dma("pool", "mix%d" % (ti % 2), lambda e, mb=mb, col0=col0, n=n: e.dma_start(
                out=mb[:, :, :n], in_=mixT.rearrange("(k q) t -> q k t", q=128)[:, :, col0:col0 + n]), writes=[mbb])
            for oc in range(8):
                ps, psb = pso[i_o % 2], psob[i_o % 2]; i_o += 1
                p.mm([lambda e, k=k, oc=oc, ps=ps, mb=mb, n=n: e.matmul(ps[:, :n], lhsT=wo_sb[:, k, oc * 128:(oc + 1) * 128], rhs=mb[:, k, :n],
                                                                       start=(k == 0), stop=(k == 7)) for k in range(8)],
                     reads=[wob, mbb], writes=[psb])
                p.op("dve", lambda e, oc=oc, ps=ps, col0=col0, n=n: e.tensor_tensor(
                    out=x_sb[:, oc, col0:col0 + n], in0=x_sb[:, oc, col0:col0 + n], in1=ps[:, :n], op=ALU.add),
                    reads=[psb, xb[oc]], writes=[xb[oc]])
            rmsnorm_fm(p, c, x_sb, None, col0, n, g_sb, gb, ones_bf, onesb, hT, hb, act, actb[:8], psn, psnb, rstd, rstdb, xbs=xb)
            if not wd_loaded:
                for fc in range(22):
                    p.dma("pool", "wd", lambda e, fc=fc: e.dma_start(out=wd_sb[:, fc, :], in_=w_down[fc * 128:(fc + 1) * 128, :]), writes=[wdb])
                wd_loaded = True
            for fc in range(22):
                q = i_u % 2; i_u += 1
                p.dma("pool", "wu%d" % q, lambda e, q=q, fc=fc: e.dma_start(
                    out=wu[q][:, :, 0:128], in_=w_up.rearrange("(k q) c -> q k c", q=128)[:, :, fc * 128:(fc + 1) * 128]), writes=[wub[q]])
                p.dma("pool", "wu%d" % q, lambda e, q=q, fc=fc: e.dma_start(
                    out=wu[q][:, :, 128:256], in_=w_up.rearrange("(k q) c -> q k c", q=128)[:, :, DFF + fc * 128:DFF + (fc + 1) * 128]), writes=[wub[q]])
                p.mm([lambda e, k=k, q=q, n=n: e.matmul(psg[q][:, :n], lhsT=wu[q][:, k, 0:128], rhs=hT[:, k, :n], start=(k == 0), stop=(k == 7))
                      for k in range(8)], reads=[wub[q]] + list(hb), writes=[psgb[q]])
                p.mm([lambda e, k=k, q=q, n=n: e.matmul(psv[q][:, :n], lhsT=wu[q][:, k, 128:256], rhs=hT[:, k, :n], start=(k == 0), stop=(k == 7))
                      for k in range(8)], reads=[wub[q]] + list(hb), writes=[psvb[q]])
                for (ps_, psb_, u_, ub_, t_, tb_, ch, eng) in ((psg[q], psgb[q], ug[q], ugb[q], tg[q], tgb[q], fc, "dve"),
                                                              (psv[q], psvb[q], uv[q], uvb[q], tv[q], tvb[q], 22 + fc, "dve")):
                    p.op("act", lambda e, u_=u_, ch=ch: e.copy(out=u_[:, 0:2], in_=carry[:, ch, :]), reads=[carryb[ch]], writes=[ub_])
                    p.op("act", lambda e, u_=u_, ps_=ps_, n=n: e.copy(out=u_[:, 2:2 + n], in_=ps_[:, :n]), reads=[psb_], writes=[ub_])
                    p.op("act", lambda e, u_=u_, ch=ch, n=n: e.copy(out=carry[:, ch, :], in_=u_[:, n:n + 2]), reads=[ub_], writes=[carryb[ch]])
                    p.op(eng, lambda e, u_=u_, t_=t_, ch=ch, n=n: e.tensor_scalar(
                        out=t_[:, :n], in0=u_[:, 0:n], scalar1=cw_sb[:, ch, 0:1], scalar2=None, op0=ALU.mult), reads=[ub_, cwb], writes=[tb_])
                    for j in (1, 2):
                        p.op(eng, lambda e, u_=u_, t_=t_, ch=ch, n=n, j=j: e.scalar_tensor_tensor(
                            out=t_[:, :n], in0=u_[:, j:j + n], scalar=cw_sb[:, ch, j:j + 1], in1=t_[:, :n], op0=ALU.mult, op1=ALU.add),
                            reads=[ub_, cwb, tb_], writes=[tb_])
                p.op("act", lambda e, q=q, n=n: e.activation(out=tg[q][:, :n], in_=tg[q][:, :n], func=AF.Silu), reads=[tgb[q]], writes=[tgb[q]])
                p.op("dve", lambda e, q=q, fc=fc, n=n: e.tensor_tensor(out=act[:, fc, :n], in0=tg[q][:, :n], in1=tv[q][:, :n], op=ALU.mult),
                     reads=[tgb[q], tvb[q]], writes=[actb[fc]])
            for oc in range(8):
                ps, psb = pso[i_o % 2], psob[i_o % 2]; i_o += 1
                p.mm([lambda e, fc=fc, oc=oc, ps=ps, n=n: e.matmul(ps[:, :n], lhsT=wd_sb[:, fc, oc * 128:(oc + 1) * 128], rhs=act[:, fc, :n],
                                                                   start=(fc == 0), stop=(fc == 21)) for fc in range(22)],
                     reads=[wdb] + list(actb), writes=[psb])
                p.op("dve", lambda e, oc=oc, ps=ps, col0=col0, n=n: e.tensor_tensor(
                    out=x_sb[:, oc, col0:col0 + n], in0=x_sb[:, oc, col0:col0 + n], in1=ps[:, :n], op=ALU.add),
                    reads=[psb, xb[oc]], writes=[xb[oc]])
            if col0 >= 2:
                for k in range(8):
                    p.dma("sp", "out", lambda e, k=k, col0=col0, n=n: e.dma_start(
                        out=xoT[k * 128:(k + 1) * 128, col0 - 2:col0 - 2 + n], in_=x_sb[:, k, col0:col0 + n]), reads=[xb[k]], writes=[outb])
        p.finish("sp", [outb])
        p.emit()
    return nc


def halo_T(a, b, j):
    out = np.zeros((a.shape[2], TQH), np.float32)
    out[:, 2:] = a[b, j * TQ:(j + 1) * TQ, :].T
    if j > 0:
        out[:, :2] = a[b, j * TQ - 2:j * TQ, :].T
    return out


def host_C_inputs(x, mix, w_out, g2, w_up, ffn_conv, w_down):
    g2l = np.ascontiguousarray(g2.reshape(8, 128).T)
    cw = np.ascontiguousarray(ffn_conv.reshape(3, 44, 128).transpose(2, 1, 0))
    maps = []
    for c in range(NCORE):
        b, j = c // 4, c % 4
        maps.append({"xT": halo_T(x, b, j), "mixT": halo_T(mix, b, j), "w_out": w_out, "g2": g2l,
                     "w_up": w_up, "convw": cw, "w_down": w_down})
    return maps


NPRM = 40
POOL_WINDOWS = (2, 4, 8, 16)


def build_B(do_pool=True, do_mlstm=True, do_moba=True):
    nc = bass.Bass("TRN2", target_bir_lowering=False)
    p = Prog(nc)
    with ExitStack() as st0:
        c0 = Ctx(nc, st0, p)
        pT = c0.dram_in("pT", [64, T])
        mqT = c0.dram_in("mqT", [64, T])
        mkT = c0.dram_in("mkT", [64, T])
        moT = c0.dram_in("moT", [64, T])
        aqT = c0.dram_in("aqT", [128, T])
        akT = c0.dram_in("akT", [128, T])
        mv = c0.dram_in("mv", [T, 64])
        av = c0.dram_in("av", [T, 128])
        gates = c0.dram_in("gates", [T, 2])
        prm = c0.dram_in("prm", [128, NPRM])
        poolw = c0.dram_in("poolw", [64, 64])
        triU_d = c0.dram_in("triU", [128, 128])
        ident_d = c0.dram_in("ident", [128, 128])
        caus_d = c0.dram_in("caus", [128, 2048])
        ind_d = c0.dram_in("ind", [32, T])
        mixT = c0.dram_out("mixT", [256, T])
        outb = p.buf()

        prm_sb = c0.sb([128, NPRM], F32); prmb = p.buf()
        triU = c0.sb([128, 128], F32); triUb = p.buf()
        ident_bf = c0.sb([128, 128], BF16); identb = p.buf()
        ones_f = c0.sb([128, 128], F32); onesfb = p.buf()
        ones_bf = c0.sb([128, 128], BF16); onesbb = p.buf()
        p.dma("sp", "prm", lambda e: e.dma_start(out=prm_sb[:, :], in_=prm), writes=[prmb])
        p.dma("sp", "triU", lambda e: e.dma_start(out=triU[:, :], in_=triU_d), writes=[triUb])
        p.dma("pool", "ident", lambda e: e.dma_start(out=ident_bf[:, :], in_=ident_d), writes=[identb])
        p.op("dve", lambda e: e.memset(ones_f[:, :], 1.0), writes=[onesfb])
        p.op("dve", lambda e: e.memset(ones_bf[:, :], 1.0), writes=[onesbb])

        if do_pool:
            with ExitStack() as st:
                c = Ctx(nc, st, p)
                PW = 2048
                xa = [c.sb([64, 16 + PW], F32) for _ in range(2)]; xab = p.bufs(2)
                s_a = c.sb([64, 16 + PW], F32); sab = p.buf()
                s_b = c.sb([64, 16 + PW], F32); sbb = p.buf()
                acc = c.sb([64, 16 + PW], F32); accb = p.buf()
                d_bf = c.sb([64, PW], BF16); dbb = p.buf()
                wp_bf = c.sb([64, 64], BF16); wpb = p.buf()
                stg = [c.sb([64, 512], F32) for _ in range(2)]; stgb = p.bufs(2)
                pps = [c.ps() for _ in range(2)]; ppsb = p.bufs(2)
                p.dma("pool", "wp", lambda e: e.dma_start(out=wp_bf[:, :], in_=poolw), writes=[wpb])
                p.op("dve", lambda e: e.memset(xa[0][:, 0:16], 0.0), writes=[xab[0]])
                i_s = 0
                for pc in range(T // PW):
                    X, Xb = xa[pc % 2], xab[pc % 2]
                    if pc > 0:
                        Xp, Xpb = xa[(pc - 1) % 2], xab[(pc - 1) % 2]
                        p.op("act", lambda e, X=X, Xp=Xp: e.copy(out=X[:, 0:16], in_=Xp[:, PW:PW + 16]), reads=[Xpb], writes=[Xb])
                    p.dma("sp", "px%d" % (pc % 2), lambda e, X=X, pc=pc: e.dma_start(out=X[:, 16:16 + PW], in_=pT[:, pc * PW:(pc + 1) * PW]), writes=[Xb])
                    W = 16 + PW
                    p.op("dve", lambda e, X=X: e.tensor_tensor(out=s_a[:, 1:W], in0=X[:, 1:W], in1=X[:, 0:W - 1], op=ALU.add), reads=[Xb], writes=[sab])
                    p.op("dve", lambda e: e.tensor_scalar(out=acc[:, 16:W], in0=s_a[:, 16:W], scalar1=prm_sb[0:64, 1:2], scalar2=None, op0=ALU.mult),
                         reads=[sab, prmb], writes=[accb])
                    src, srcb, dst, dstb = s_a, sab, s_b, sbb
                    lo = 1
                    for k, sh in ((1, 2), (2, 4), (3, 8)):
                        lo2 = lo + sh
                        p.op("dve", lambda e, src=src, dst=dst, lo2=lo2, sh=sh: e.tensor_tensor(
                            out=dst[:, lo2:W], in0=src[:, lo2:W], in1=src[:, lo2 - sh:W - sh], op=ALU.add), reads=[srcb], writes=[dstb])
                        p.op("dve", lambda e, dst=dst, k=k: e.scalar_tensor_tensor(
                            out=acc[:, 16:W], in0=dst[:, 16:W], scalar=prm_sb[0:64, 1 + k:2 + k], in1=acc[:, 16:W], op0=ALU.mult, op1=ALU.add),
                            reads=[dstb, prmb, accb], writes=[accb])
                        src, srcb, dst, dstb = dst, dstb, src, srcb
                        lo = lo2
                    if pc == 0:
                        p.op("dve", lambda e: e.tensor_tensor(out=acc[:, 16:32], in0=acc[:, 16:32], in1=prm_sb[0:64, 19:35], op=ALU.mult),
                             reads=[accb, prmb], writes=[accb])
                    p.op("dve", lambda e, X=X: e.tensor_tensor(out=d_bf[:, :], in0=acc[:, 16:W], in1=X[:, 16:W], op=ALU.subtract), reads=[accb, Xb], writes=[dbb])
                    for q in range(PW // 512):
                        ps, psb = pps[i_s % 2], ppsb[i_s % 2]
                        sg, sgb = stg[i_s % 2], stgb[i_s % 2]
                        ch = "po%d" % (i_s % 2); i_s += 1
                        p.mm([lambda e, ps=ps, q=q: e.matmul(ps[0:64, :], lhsT=wp_bf[:, :], rhs=d_bf[:, q * 512:(q + 1) * 512], start=True, stop=True)],
                             reads=[wpb, dbb], writes=[psb])
                        p.op("act", lambda e, ps=ps, sg=sg: e.activation(out=sg[:, :], in_=ps[0:64, :], func=AF.Copy, scale=prm_sb[0:64, 0:1]),
                             reads=[psb, prmb], writes=[sgb])
                        t0 = pc * PW + q * 512
                        p.dma("sp", ch, lambda e, sg=sg, t0=t0: e.dma_start(out=mixT[0:64, t0:t0 + 512], in_=sg[:, :]), reads=[sgb], writes=[outb])

        p.barrier()
        if do_mlstm:
            with ExitStack() as st:
                c = Ctx(nc, st, p)
                NCH = T // 128
                qT_bf = c.sb([64, T], BF16); qTb = p.bufs(4, "qT")
                kT_bf = c.sb([64, T], BF16); kTb = p.bufs(4, "kT")
                vones = c.sb([128, NCH, 128], BF16); vonesb = p.buf()
                g_sb = c.sb([128, NCH, 2], F32); gsbb = p.buf()
                iv = c.sb([128, NCH], F32); ivb = p.buf()
                lf = c.sb([128, NCH], F32); lfb_ = p.buf()
                bias_s = c.sb([128, NCH], F32); biasb = p.buf()
                wk = c.sb([128, NCH], F32); wkb = p.buf()
                dec = c.sb([128, NCH], F32); decb = p.buf()
                St = c.sb([64, 128], F32); Stb = p.buf()
                St_bf = c.sb([64, 128], BF16); Stbfb = p.buf()
                PW = 2048
                xin = [c.sb([64, 3 + PW], F32) for _ in range(2)]; xinb = p.bufs(2)
                cacc = c.sb([64, PW], F32); caccb = p.buf()
                n_x = 0
                for (src_d, dst, dstbufs, c0col, scl) in ((mqT, qT_bf, qTb, 5, 1.0), (mkT, kT_bf, kTb, 9, 0.125)):
                    for pc in range(T // PW):
                        X, Xb = xin[n_x % 2], xinb[n_x % 2]
                        if pc == 0:
                            p.op("dve", lambda e, X=X: e.memset(X[:, 0:3], 0.0), writes=[Xb])
                        else:
                            Xp, Xpb = xin[(n_x - 1) % 2], xinb[(n_x - 1) % 2]
                            p.op("act", lambda e, X=X, Xp=Xp: e.copy(out=X[:, 0:3], in_=Xp[:, PW:PW + 3]), reads=[Xpb], writes=[Xb])
                        p.dma("sp", "mx%d" % (n_x % 2), lambda e, X=X, pc=pc, src_d=src_d: e.dma_start(
                            out=X[:, 3:3 + PW], in_=src_d[:, pc * PW:(pc + 1) * PW]), writes=[Xb])
                        n_x += 1
                        p.op("dve", lambda e, X=X, c0col=c0col: e.tensor_scalar(out=cacc[:, :], in0=X[:, 0:PW], scalar1=prm_sb[0:64, c0col:c0col + 1],
                                                                                 scalar2=None, op0=ALU.mult), reads=[Xb, prmb], writes=[caccb])
                        for j in (1, 2, 3):
                            p.op("dve", lambda e, X=X, c0col=c0col, j=j: e.scalar_tensor_tensor(
                                out=cacc[:, :], in0=X[:, j:j + PW], scalar=prm_sb[0:64, c0col + j:c0col + j + 1], in1=cacc[:, :],
                                op0=ALU.mult, op1=ALU.add), reads=[Xb, prmb, caccb], writes=[caccb])
                        p.op("act", lambda e: e.activation(out=cacc[:, :], in_=cacc[:, :], func=AF.Silu), reads=[caccb], writes=[caccb])
                        p.op("dve", lambda e, dst=dst, pc=pc, scl=scl: e.tensor_scalar(out=dst[:, pc * PW:(pc + 1) * PW], in0=cacc[:, :], scalar1=scl,
                                                                                       scalar2=None, op0=ALU.mult), reads=[caccb], writes=[dstbufs[pc]])
                p.op("dve", lambda e: e.memset(vones[:, :, 64:128], 1.0), writes=[vonesb])
                for q in range(4):
                    p.dma("pool", "vones", lambda e, q=q: e.dma_start(out=vones[:, q * 16:(q + 1) * 16, 0:64],
                                                                   in_=mv.rearrange("(c q) e -> q c e", q=128)[:, q * 16:(q + 1) * 16, :]), writes=[vonesb])
                for q in range(4):
                    p.dma("sp", "gates", lambda e, q=q: e.dma_start(out=g_sb[:, q * 16:(q + 1) * 16, :],
                                                                in_=gates.rearrange("(c q) two -> q c two", q=128)[:, q * 16:(q + 1) * 16, :]), writes=[gsbb])
                p.op("dve", lambda e: e.tensor_scalar(out=iv[:, :], in0=g_sb[:, :, 0], scalar1=prm_sb[:, 13:14], scalar2=None, op0=ALU.add),
                     reads=[gsbb, prmb], writes=[ivb])
                p.op("dve", lambda e: e.tensor_scalar(out=lf[:, :], in0=g_sb[:, :, 1], scalar1=prm_sb[:, 14:15], scalar2=None, op0=ALU.add),
                     reads=[gsbb, prmb], writes=[lfb_])
                p.op("act", lambda e: e.activation(out=lf[:, :], in_=lf[:, :], func=AF.Exp, scale=-1.0), reads=[lfb_], writes=[lfb_])
                p.op("act", lambda e: e.activation(out=lf[:, :], in_=lf[:, :], func=AF.Ln, bias=1.0), reads=[lfb_], writes=[lfb_])
                p.op("dve", lambda e: e.tensor_scalar(out=lf[:, :], in0=lf[:, :], scalar1=-1.0, scalar2=None, op0=ALU.mult), reads=[lfb_], writes=[lfb_])
                psA = c.ps(); psAb = p.buf()
                psB = c.ps(); psBb = p.buf()
                p.mm([lambda e: e.matmul(psA[:, 0:NCH], lhsT=triU[:, :], rhs=lf[:, :], start=True, stop=True)], reads=[triUb, lfb_], writes=[psAb])
                p.mm([lambda e: e.matmul(psB[:, 0:NCH], lhsT=ones_f[:, :], rhs=lf[:, :], start=True, stop=True)], reads=[onesfb, lfb_], writes=[psBb])
                p.op("dve", lambda e: e.tensor_tensor(out=bias_s[:, :], in0=iv[:, :], in1=psA[:, 0:NCH], op=ALU.subtract), reads=[ivb, psAb], writes=[biasb])
                p.op("dve", lambda e: e.tensor_tensor(out=wk[:, :], in0=bias_s[:, :], in1=psB[:, 0:NCH], op=ALU.add), reads=[biasb, psBb], writes=[wkb])
                p.op("act", lambda e: e.activation(out=wk[:, :], in_=wk[:, :], func=AF.Exp), reads=[wkb], writes=[wkb])
                p.op("act", lambda e: e.activation(out=dec[:, :], in_=psB[:, 0:NCH], func=AF.Exp), reads=[psBb], writes=[decb])
                p.op("dve", lambda e: e.memset(St[:, :], 0.0), writes=[Stb])
                p.op("dve", lambda e: e.memset(St_bf[:, :], 0.0), writes=[Stbfb])

                lfrep = [c.sb([128, 128], F32) for _ in range(2)]; lfrepb = p.bufs(2)
                DT = [c.sb([128, 128], F32) for _ in range(2)]; DTb = p.bufs(2)
                WT = [c.sb([128, 128], BF16) for _ in range(2)]; WTb = p.bufs(2)
                eG = [c.sb([64, 128], F32) for _ in range(2)]; eGb = p.bufs(2)
                qs = [c.sb([64, 128], BF16) for _ in range(2)]; qsb = p.bufs(2)
                ksc = [c.sb([128, 64], BF16) for _ in range(2)]; kscb = p.bufs(2)
                dn = [c.sb([64, 128], F32) for _ in range(2)]; dnb = p.bufs(2)
                hT = [c.sb([64, 512], F32) for _ in range(2)]; hTb = p.bufs(2)
                sq = c.sb([64, 512], BF16); sqb = p.buf()
                rstd = c.sb([64, 512], F32); rstdb = p.buf()
                mo_sb = [c.sb([64, 512], F32) for _ in range(2)]; mob = p.bufs(2)
                ho = [c.sb([64, 512], F32) for _ in range(2)]; hob = p.bufs(2)
                psG = [c.ps() for _ in range(2)]; psGb = p.bufs(2)
                psN = [c.ps() for _ in range(2)]; psNb = p.bufs(2)
                for ch in range(NCH):
                    q = ch % 2
                    t0 = ch * 128
                    pcq = ch // 16
                    p.op("dve", lambda e, q=q, ch=ch: e.tensor_scalar(out=lfrep[q][:, :], in0=ones_f[:, :], scalar1=lf[:, ch:ch + 1], scalar2=None, op0=ALU.mult),
                         reads=[onesfb, lfb_], writes=[lfrepb[q]])
                    p.mm([lambda e, q=q: e.matmul(psG[q][:, 0:128], lhsT=lfrep[q][:, :], rhs=triU[:, :], start=True, stop=True)],
                         reads=[lfrepb[q], triUb], writes=[psGb[q]])
                    p.mm([lambda e, q=q, t0=t0: e.matmul(psG[q][:, 128:256], lhsT=kT_bf[:, t0:t0 + 128], rhs=qT_bf[:, t0:t0 + 128], start=True, stop=True)],
                         reads=[kTb[pcq], qTb[pcq]], writes=[psGb[q]])
                    p.op("act", lambda e, q=q, ch=ch: e.activation(out=DT[q][:, :], in_=psG[q][:, 0:128], func=AF.Exp, bias=bias_s[:, ch:ch + 1]),
                         reads=[psGb[q], biasb], writes=[DTb[q]])
                    p.op("act", lambda e, q=q: e.activation(out=eG[q][:, :], in_=psG[q][0:64, 0:128], func=AF.Exp), reads=[psGb[q]], writes=[eGb[q]])
                    p.op("dve", lambda e, q=q: e.tensor_tensor(out=DT[q][:, :], in0=DT[q][:, :], in1=triU[:, :], op=ALU.mult), reads=[DTb[q], triUb], writes=[DTb[q]])
                    p.op("dve", lambda e, q=q: e.tensor_tensor(out=WT[q][:, :], in0=DT[q][:, :], in1=psG[q][:, 128:256], op=ALU.mult), reads=[DTb[q], psGb[q]], writes=[WTb[q]])
                    p.op("dve", lambda e, q=q, t0=t0: e.tensor_tensor(out=qs[q][:, :], in0=eG[q][:, :], in1=qT_bf[:, t0:t0 + 128], op=ALU.mult),
                         reads=[eGb[q], qTb[pcq]], writes=[qsb[q]])
                    p.mm([lambda e, q=q, ch=ch: e.matmul(psN[q][0:64, 0:128], lhsT=vones[:, ch, 0:64], rhs=WT[q][:, :], start=True, stop=False),
                          lambda e, q=q: e.matmul(psN[q][0:64, 0:128], lhsT=St_bf[:, 0:64], rhs=qs[q][:, :], start=False, stop=True),
                          lambda e, q=q, ch=ch: e.matmul(psN[q][0:64, 128:256], lhsT=vones[:, ch, 64:128], rhs=WT[q][:, :], start=True, stop=False),
                          lambda e, q=q: e.matmul(psN[q][0:64, 128:256], lhsT=St_bf[:, 64:128], rhs=qs[q][:, :], start=False, stop=True),
                          lambda e, q=q, t0=t0: e.matmul(psN[q][:, 256:320], lhsT=kT_bf[:, t0:t0 + 128], rhs=ident_bf[0:64, 0:64], start=True, stop=True)],
                         reads=[vonesb, WTb[q], Stbfb, qsb[q], kTb[pcq], identb], writes=[psNb[q]])
                    hq = (ch // 4) % 2
                    hc = (ch % 4) * 128
                    p.op("act", lambda e, q=q: e.activation(out=dn[q][:, :], in_=psN[q][0:64, 128:256], func=AF.Abs), reads=[psNb[q]], writes=[dnb[q]])
                    p.op("dve", lambda e, q=q: e.tensor_scalar(out=dn[q][:, :], in0=dn[q][:, :], scalar1=1.0, scalar2=None, op0=ALU.max), reads=[dnb[q]], writes=[dnb[q]])
                    p.op("dve", lambda e, q=q: e.reciprocal(out=dn[q][:, :], in_=dn[q][:, :]), reads=[dnb[q]], writes=[dnb[q]])
                    p.op("dve", lambda e, q=q, hq=hq, hc=hc: e.tensor_tensor(out=hT[hq][:, hc:hc + 128], in0=psN[q][0:64, 0:128], in1=dn[q][:, :], op=ALU.mult),
                         reads=[psNb[q], dnb[q]], writes=[hTb[hq]])
                    p.op("act", lambda e, q=q, ch=ch: e.activation(out=ksc[q][:, :], in_=psN[q][:, 256:320], func=AF.Copy, scale=wk[:, ch:ch + 1]),
                         reads=[psNb[q], wkb], writes=[kscb[q]])
                    p.mm([lambda e, q=q, ch=ch: e.matmul(psN[q][0:64, 320:448], lhsT=ksc[q][:, :], rhs=vones[:, ch, :], start=True, stop=True)],
                         reads=[kscb[q], vonesb], writes=[psNb[q]])
                    p.op("dve", lambda e, q=q, ch=ch: e.scalar_tensor_tensor(out=St[:, :], in0=St[:, :], scalar=dec[0:64, ch:ch + 1], in1=psN[q][0:64, 320:448],
                                                                             op0=ALU.mult, op1=ALU.add), reads=[Stb, decb, psNb[q]], writes=[Stb])
                    p.op("act", lambda e: e.copy(out=St_bf[:, :], in_=St[:, :]), reads=[Stb], writes=[Stbfb])
                    if ch % 4 == 3:
                        pt0 = (ch // 4) * 512
                        H_, Hb_ = hT[hq], hTb[hq]
                        p.dma("sp", "mo%d" % hq, lambda e, hq=hq, pt0=pt0: e.dma_start(out=mo_sb[hq][:, :], in_=moT[:, pt0:pt0 + 512]), writes=[mob[hq]])
                        p.op("act", lambda e, H_=H_: e.activation(out=sq[:, :], in_=H_[:, :], func=AF.Square), reads=[Hb_], writes=[sqb])
                        p.mm([lambda e: e.matmul(psA[0:64, :], lhsT=ones_bf[0:64, 0:64], rhs=sq[:, :], start=True, stop=True)], reads=[onesbb, sqb], writes=[psAb])
                        p.op("act", lambda e: e.activation(out=rstd[:, :], in_=psA[0:64, :], func=AF.Ln, bias=EPS, scale=1.0 / 64), reads=[psAb], writes=[rstdb])
                        p.op("act", lambda e: e.activation(out=rstd[:, :], in_=rstd[:, :], func=AF.Exp, scale=-0.5), reads=[rstdb], writes=[rstdb])
                        p.op("act", lambda e, hq=hq: e.activation(out=mo_sb[hq][:, :], in_=mo_sb[hq][:, :], func=AF.Sigmoid), reads=[mob[hq]], writes=[mob[hq]])
                        p.op("dve", lambda e, H_=H_, hq=hq: e.scalar_tensor_tensor(out=ho[hq][:, :], in0=H_[:, :], scalar=prm_sb[0:64, 15:16], in1=rstd[:, :],
                                                                                   op0=ALU.mult, op1=ALU.mult), reads=[Hb_, prmb, rstdb], writes=[hob[hq]])
                        p.op("dve", lambda e, hq=hq: e.tensor_tensor(out=ho[hq][:, :], in0=ho[hq][:, :], in1=mo_sb[hq][:, :], op=ALU.mult),
                             reads=[hob[hq], mob[hq]], writes=[hob[hq]])
                        p.dma("sp", "ho%d" % hq, lambda e, hq=hq, pt0=pt0: e.dma_start(out=mixT[64:128, pt0:pt0 + 512], in_=ho[hq][:, :]), reads=[hob[hq]], writes=[outb])

        p.barrier()
        if do_moba:
            build_moba(nc, p, prm_sb, prmb, ident_bf, identb, ones_bf, onesbb, aqT, akT, av, caus_d, ind_d, mixT, outb)
        p.finish_all("sp")
        p.emit()
    return nc


def static_consts():
    triU = np.triu(np.ones((128, 128), np.float32))
    ident = np.eye(128, dtype=np.float32)
    k = np.arange(128)[:, None]
    q = np.arange(512)[None, :]
    caus = np.concatenate([((r * 128 + k) <= q).astype(np.float32) for r in range(4)], axis=1)
    ind = (np.arange(T)[None, :] // 256 == np.arange(32)[:, None]).astype(np.float32)
    return triU, ident, np.ascontiguousarray(caus), np.ascontiguousarray(ind)


def host_B_inputs(zfm_b, ztm_b, prm_l):
    triU, ident, caus, ind = static_consts()
    maps = []
    for cidx in range(NCORE):
        b, g = cidx // 4, cidx % 4
        zf, zt = zfm_b[b], ztm_b[b]
        w = POOL_WINDOWS[g]
        prm = np.zeros((128, NPRM), np.float32)
        prm[0:64, 0] = prm_l["pool_scale"][g * 64:(g + 1) * 64]
        prm[:, 1 + g] = 1.0 / w
        prm[0:64, 5:9] = prm_l["m_conv"][:, g * 64:(g + 1) * 64].T
        prm[0:64, 9:13] = prm_l["m_conv"][:, 256 + g * 64:256 + (g + 1) * 64].T
        prm[:, 13] = prm_l["m_b_i"][g]
        prm[:, 14] = prm_l["m_b_f"][g]
        prm[0:64, 15] = prm_l["m_norm_g"][g * 64:(g + 1) * 64]
        prm[:, 16] = np.tile(prm_l["a_q_g"], 2)
        prm[:, 17] = np.tile(prm_l["a_k_g"], 2)
        prm[:, 19:35] = (w / np.minimum(np.arange(16) + 1, w)).astype(np.float32)[None, :]
        maps.append({
            "pT": np.ascontiguousarray(zf[g * 64:(g + 1) * 64]),
            "mqT": np.ascontiguousarray(zf[256 + g * 64:256 + (g + 1) * 64]),
            "mkT": np.ascontiguousarray(zf[512 + g * 64:512 + (g + 1) * 64]),
            "moT": np.ascontiguousarray(zf[768 + g * 64:768 + (g + 1) * 64]),
            "aqT": np.ascontiguousarray(zf[1024 + g * 128:1024 + (g + 1) * 128]),
            "akT": np.ascontiguousarray(zf[1536 + g * 128:1536 + (g + 1) * 128]),
            "mv": np.ascontiguousarray(zt[:, g * 64:(g + 1) * 64]),
            "av": np.ascontiguousarray(zt[:, 256 + g * 128:256 + (g + 1) * 128]),
            "gates": np.ascontiguousarray(zt[:, [768 + g, 772 + g]]),
            "prm": prm, "poolw": np.ascontiguousarray(prm_l["pool_w"][g]),
            "triU": triU, "ident": ident, "caus": caus, "ind": ind,
        })
    return maps


_PROGS = {}


def _prog(name):
    if name not in _PROGS:
        _PROGS[name] = {"A": build_A, "B": build_B, "C": build_C}[name]()
    return _PROGS[name]


def kernel_unfused(**inputs):
    inp = {k: np.ascontiguousarray(np.asarray(v), dtype=np.float32) for k, v in inputs.items()}
    x = inp["x"].copy()
    cores = list(range(NCORE))
    for l in range(4):
        wfm, wtm = split_w_in(inp["w_in"][l])
        g1l = np.ascontiguousarray(inp["ln1_g"][l].reshape(8, 128).T)
        maps = []
        for c in cores:
            b, j = c // 4, c % 4
            maps.append({"xT": np.ascontiguousarray(x[b, j * TQ:(j + 1) * TQ, :].T), "wfm": wfm, "wtm": wtm, "g1": g1l})
        res = run_bass_kernel_spmd(_prog("A"), maps, core_ids=cores).results
        zfm_b = [np.concatenate([res[b * 4 + j]["zfm"] for j in range(4)], axis=1) for b in range(2)]
        ztm_b = [np.concatenate([res[b * 4 + j]["ztm"] for j in range(4)], axis=0) for b in range(2)]
        prm_l = {k: inp[k][l] for k in ("pool_scale", "m_conv", "m_b_i", "m_b_f", "m_norm_g", "a_q_g", "a_k_g", "pool_w")}
        res = run_bass_kernel_spmd(_prog("B"), host_B_inputs(zfm_b, ztm_b, prm_l), core_ids=cores).results
        mix = np.empty((2, T, D), np.float32)
        for c in cores:
            b, g = c // 4, c % 4
            m = res[c]["mixT"]
            mix[b, :, g * 64:(g + 1) * 64] = m[0:64].T
            mix[b, :, 256 + g * 64:256 + (g + 1) * 64] = m[64:128].T
            mix[b, :, 512 + g * 128:512 + (g + 1) * 128] = m[128:256].T
        res = run_bass_kernel_spmd(_prog("C"), host_C_inputs(x, mix, inp["w_out"][l], inp["ln2_g"][l], inp["w_up"][l],
                                                             inp["ffn_conv"][l], inp["w_down"][l]), core_ids=cores).results
        xn = np.empty_like(x)
        for c in cores:
            b, j = c // 4, c % 4
            xn[b, j * TQ:(j + 1) * TQ, :] = res[c]["xoT"].T
        x = xn
    return x


U32 = mybir.dt.uint32
RZ = 2816
RF = 10
RM = 1025
CHR = 256
GROUPS = [[0, 1, 2, 3], [4, 5, 6, 7]]
IDX_POOL, IDX_MQ, IDX_MK = 0, 4, 8
IDX_MO, IDX_AQ, IDX_AK = 12, 28, 44
IDX_TM = 60
IDX_XT = 64
IDX_MX = 65
IDX_MH = 97
IDX_AV = 105
IDX_GT = 109
NIDX = 113


def build_fused(nlayers=4, upto=3):
    nc = bass.Bass("TRN2", target_bir_lowering=False)
    p = Prog(nc)
    with ExitStack() as st0:
        c0 = Ctx(nc, st0, p)
        xT = c0.dram_in("xT", [D, TQ])
        wfm = c0.dram_in("wfm", [4, D, NFM])
        wtm = c0.dram_in("wtm", [4, D, NTM])
        g1 = c0.dram_in("g1", [4, 128, 8])
        w_out = c0.dram_in("w_out", [4, D, D])
        g2 = c0.dram_in("g2", [4, 128, 8])
        w_up = c0.dram_in("w_up", [4, 22, 128, 8, 256])
        convw = c0.dram_in("convw", [4, 128, 44, 3])
        w_down = c0.dram_in("w_down", [4, DFF, D])
        prm = c0.dram_in("prm", [4, 128, NPRM])
        poolw = c0.dram_in("poolw", [4, 64, 64])
        triU_d = c0.dram_in("triU", [128, 128])
        ident_d = c0.dram_in("ident", [128, 128])
        caus_d = c0.dram_in("caus", [128, 2048])
        ind_d = c0.dram_in("ind", [32, T])
        idx_d = c0.dram_in("idx", [128, NIDX], U32)
        xoT = c0.dram_out("xoT", [D, TQ])
        zsrc_t = nc.dram_tensor("zsrc", [RZ, 2048], BF16, kind="Internal")
        zall_t = nc.dram_tensor("zall", [4 * RZ, 2048], BF16, kind="Internal")
        zsrcf_t = nc.dram_tensor("zsrcf", [RF, 2048], F32, kind="Internal")
        zallf_t = nc.dram_tensor("zallf", [4 * RF, 2048], F32, kind="Internal")
        msrc_t = nc.dram_tensor("msrc", [RM, 2048], BF16, kind="Internal")
        mall_t = nc.dram_tensor("mall", [4 * RM, 2048], BF16, kind="Internal")
        zsrcb, zallb, msrcb, mallb, outb = p.buf(), p.buf(), p.buf(), p.buf(), p.buf()
        zall2048 = zall_t.ap()
        zall512 = zall_t.ap().rearrange("r (f w) -> (r f) w", w=512)
        zall1024 = zall_t.ap().rearrange("r (f w) -> (r f) w", w=1024)
        zallf32 = zallf_t.ap().rearrange("r (f w) -> (r f) w", w=32)
        zall16 = zallf_t.ap().rearrange("r (f w) -> (r f) w", w=16)
        mall512 = mall_t.ap().rearrange("r (f w) -> (r f) w", w=512)
        mall2 = mall_t.ap().rearrange("r (f w) -> (r f) w", w=2)

        x_sb = c0.sb([128, 8, TQH], F32); xb = p.bufs(8, "x")
        idx_sb = c0.sb([128, NIDX], U32); idxb = p.buf()
        onesD_bf = c0.sb([128, 128], BF16); onesDb = p.buf()
        ones_f = c0.sb([128, 128], F32); onesfb = p.buf()
        ones_bf = c0.sb([128, 128], BF16); onesbb = p.buf()
        zero_sb = c0.sb([128, 16], F32); zerob = p.buf()
        zero_bf = c0.sb([128, 16], BF16); zerobb = p.buf()
        triU = c0.sb([128, 128], F32); triUb = p.buf()
        ident_bf = c0.sb([128, 128], BF16); identb = p.buf()
        caus = c0.sb([128, 2048], BF16); causb = p.buf()
        blk = c0.sb([128, 128], BF16); blkb = p.buf()
        for k in range(8):
            p.dma("sp", "x%d" % k, lambda e, k=k: e.dma_start(out=x_sb[:, k, 2:TQH], in_=xT[k * 128:(k + 1) * 128, :]), writes=[xb[k]])
        p.dma("sp", "idx", lambda e: e.dma_start(out=idx_sb[:, :], in_=idx_d), writes=[idxb])
        p.dma("sp", "triU", lambda e: e.dma_start(out=triU[:, :], in_=triU_d), writes=[triUb])
        p.dma("pool", "ident", lambda e: e.dma_start(out=ident_bf[:, :], in_=ident_d), writes=[identb])
        p.dma("pool", "caus", lambda e: e.dma_start(out=caus[:, :], in_=caus_d), writes=[causb])
        p.op("dve", lambda e: e.memset(onesD_bf[:, :], 1.0 / D), writes=[onesDb])
        p.op("dve", lambda e: e.memset(ones_f[:, :], 1.0), writes=[onesfb])
        p.op("dve", lambda e: e.memset(ones_bf[:, :], 1.0), writes=[onesbb])
        p.op("dve", lambda e: e.memset(zero_sb[:, :], 0.0), writes=[zerob])
        p.op("dve", lambda e: e.memset(zero_bf[:, :], 0.0), writes=[zerobb])
        p.op("dve", lambda e: e.memset(blk[:, :], 0.0), writes=[blkb])
        p.op("dve", lambda e: e.memset(blk[0:64, 0:64], 1.0), writes=[blkb])
        p.op("dve", lambda e: e.memset(blk[64:128, 64:128], 1.0), writes=[blkb])
        for k in range(8):
            p.op("dve", lambda e, k=k: e.memset(x_sb[:, k, 0:2], 0.0), writes=[xb[k]])

        zsrc_cb = p.bufs(12, "zs")
        zall_cb = p.bufs(12, "za")
        msrc_cb = p.bufs(5, "ms")
        mall_cb = p.bufs(5, "ma")

        def ag_chunk(src_t, dst_t, nrows, i, scb, dcb):
            r0 = i * CHR
            n = min(CHR, nrows - r0)
            p.wait_dma_all("pool")
            p.cc(lambda e: e.collective_compute("AllGather", ALU.bypass, replica_groups=GROUPS,
                                                ins=[src_t.ap()[r0:r0 + n, :]], outs=[dst_t.ap()[4 * r0:4 * r0 + 4 * n, :]]),
                 reads=[scb[i]], writes=[dcb[i]])

        def ag_f():
            p.wait_dma_all("pool")
            p.cc(lambda e: e.collective_compute("AllGather", ALU.bypass, replica_groups=GROUPS, ins=[zsrcf_t.ap()], outs=[zallf_t.ap()]),
                 reads=[zsrc_cb[11]], writes=[zall_cb[11]])

        def gather(ch, out_ap, view, col, nparts, srcbs, writes):
            p.dma("pool", ch, lambda e: e.indirect_dma_start(
                out=out_ap, out_offset=None, in_=view,
                in_offset=bass.IndirectOffsetOnAxis(ap=idx_sb[0:nparts, col:col + 1], axis=0)), reads=list(srcbs) + [idxb], writes=writes)

        def out_piece(ch, sg_ap, sgb, r0, nrows, t0):
            j, cc0 = t0 // 2048, t0 % 2048
            if r0 < 128:
                dst = bass.AP(msrc_t, (r0 * 4 + j) * 2048 + cc0, [[4 * 2048, nrows], [1, 512]])
                wb = [msrc_cb[0]] if r0 < 64 else [msrc_cb[1]]
            else:
                dst = bass.AP(msrc_t, (512 + j * 128 + (r0 - 128)) * 2048 + cc0, [[2048, nrows], [1, 512]])
                wb = [msrc_cb[2 + j // 2]]
            p.dma("sp", ch, lambda e: e.dma_start(out=dst, in_=sg_ap), reads=[sgb], writes=wb)
            if cc0 + 512 == 2048 and j < 3:
                hd = bass.AP(msrc_t, 1024 * 2048 + r0 * 8 + (j + 1) * 2, [[8, nrows], [1, 2]])
                p.dma("sp", ch, lambda e: e.dma_start(out=hd, in_=sg_ap[:, 510:512]), reads=[sgb], writes=[msrc_cb[4]])

        def phase_A(l):
            with ExitStack() as st:
                c = Ctx(nc, st, p)
                wfm_sb = c.sb([128, 8, NFM], BF16); wfmb = p.buf()
                wtm_sb = c.sb([128, 8, NTM], BF16); wtmb = p.buf()
                g_sb = c.sb([128, 8], F32); gb = p.buf()
                sq = c.sb([128, 8, 512], BF16); sqb = p.bufs(8)
                hTs = [c.sb([128, 8, 512], BF16) for _ in range(4)]; hbs = [p.bufs(8) for _ in range(4)]
                rstd = c.sb([128, 512], F32); rstdb = p.buf()
                NST = 12
                stg = [c.sb([128, 512], BF16) for _ in range(NST)]; stgb = p.bufs(NST)
                stgt = [c.sb([128, 768], BF16) for _ in range(4)]; stgtb = p.bufs(4)
                stgg = [c.sb([128, 8], F32) for _ in range(4)]; stggb = p.bufs(4)
                psn = c.ps(); psnb = p.buf()
                NPS = 7
                pss = [c.ps() for _ in range(NPS)]; pssb = p.bufs(NPS)
                p.dma("sp", "g", lambda e: e.dma_start(out=g_sb[:, :], in_=g1[l]), writes=[gb])
                for k in range(8):
                    p.dma("pool", "wtm", lambda e, k=k: e.dma_start(out=wtm_sb[:, k, :], in_=wtm[l, k * 128:(k + 1) * 128, :]), writes=[wtmb])
                for k in range(8):
                    p.dma("pool", "wfm", lambda e, k=k: e.dma_start(out=wfm_sb[:, k, :], in_=wfm[l, k * 128:(k + 1) * 128, :]), writes=[wfmb])
                p.dma("sp", "xt", lambda e: e.dma_start(out=bass.AP(zsrcf_t, 8 * 2048, [[16, 128], [2, 8], [1, 2]]), in_=x_sb[:, :, TQH - 2:TQH]),
                      reads=list(xb), writes=[zsrc_cb[11]])
                p.dma("sp", "xz", lambda e: e.dma_start(out=bass.AP(zsrcf_t, 9 * 2048, [[16, 128], [1, 16]]), in_=zero_sb[:, :]), reads=[zerob], writes=[zsrc_cb[11]])
                for tt in range(4):
                    rmsnorm_fm(p, c, x_sb, None, 2 + tt * 512, 512, g_sb, gb, onesD_bf, onesDb, hTs[tt], hbs[tt], sq, sqb, psn, psnb, rstd, rstdb, xbs=xb)
                i_ps = 0
                i_st = 0
                i_tt = 0
                for tt in range(4):
                    hT, hb = hTs[tt], hbs[tt]
                    for s in range(4):
                        q = i_tt % 4; i_tt += 1
                        sgt, sgtb = stgt[q], stgtb[q]
                        sgg, sggb = stgg[q], stggb[q]
                        ps, psb = pss[i_ps % NPS], pssb[i_ps % NPS]; i_ps += 1
                        p.mm([lambda e, k=k, s=s, ps=ps, hT=hT: e.matmul(ps[:, :512], lhsT=hT[:, k, s * 128:(s + 1) * 128],
                                                                        rhs=wtm_sb[:, k, 0:512], start=(k == 0), stop=(k == 7))
                              for k in range(8)], reads=[wtmb] + list(hb), writes=[psb])
                        p.op("act", lambda e, ps=ps, sgt=sgt: e.copy(out=sgt[:, 0:512], in_=ps[:, :512]), reads=[psb], writes=[sgtb])
                        ps, psb = pss[i_ps % NPS], pssb[i_ps % NPS]; i_ps += 1
                        p.mm([lambda e, k=k, s=s, ps=ps, hT=hT: e.matmul(ps[:, :NTM - 512], lhsT=hT[:, k, s * 128:(s + 1) * 128],
                                                                        rhs=wtm_sb[:, k, 512:NTM], start=(k == 0), stop=(k == 7))
                              for k in range(8)], reads=[wtmb] + list(hb), writes=[psb])
                        p.op("act", lambda e, ps=ps, sgt=sgt: e.copy(out=sgt[:, 512:768], in_=ps[:, 0:256]), reads=[psb], writes=[sgtb])
                        p.op("act", lambda e, ps=ps, sgg=sgg: e.copy(out=sgg[:, 0:8], in_=ps[:, 256:264]), reads=[psb], writes=[sggb])
                        cidx = tt * 4 + s
                        dmv = bass.AP(zsrc_t, cidx * 64, [[16 * 64, 128], [128 * 16 * 64, 4], [1, 64]])
                        p.dma("sp", "ot%d" % q, lambda e, dmv=dmv, sgt=sgt: e.dma_start(
                            out=dmv, in_=sgt[:, 0:256].rearrange("p (g e) -> p g e", g=4)), reads=[sgtb], writes=[zsrc_cb[0]])
                        dav = bass.AP(zsrc_t, 256 * 2048 + cidx * 128, [[16 * 128, 128], [128 * 16 * 128, 4], [1, 128]])
                        p.dma("sp", "ot%d" % q, lambda e, dav=dav, sgt=sgt: e.dma_start(
                            out=dav, in_=sgt[:, 256:768].rearrange("p (g e) -> p g e", g=4)), reads=[sgtb], writes=[zsrc_cb[1], zsrc_cb[2]])
                        dgt = bass.AP(zsrcf_t, cidx * 2, [[32, 128], [128 * 32, 4], [1, 2]])
                        p.dma("sp", "og%d" % q, lambda e, dgt=dgt, sgg=sgg: e.dma_start(
                            out=dgt, in_=sgg[:, 0:8].rearrange("p (g e) -> p g e", g=4)), reads=[sggb], writes=[zsrc_cb[11]])
                ag_f()
                for i in range(3):
                    ag_chunk(zsrc_t, zall_t, RZ, i, zsrc_cb, zall_cb)
                for oc in range(NFM // 128):
                    for tt in range(4):
                        hT, hb = hTs[tt], hbs[tt]
                        ps, psb = pss[i_ps % NPS], pssb[i_ps % NPS]; i_ps += 1
                        p.mm([lambda e, k=k, oc=oc, ps=ps, hT=hT: e.matmul(ps[:, :], lhsT=wfm_sb[:, k, oc * 128:(oc + 1) * 128], rhs=hT[:, k, :],
                                                                            start=(k == 0), stop=(k == 7)) for k in range(8)],
                             reads=[wfmb] + list(hb), writes=[psb])
                        q = i_st % NST; i_st += 1
                        sg, sgb = stg[q], stgb[q]
                        p.op("act", lambda e, ps=ps, sg=sg: e.copy(out=sg[:, :], in_=ps[:, :]), reads=[psb], writes=[sgb])
                        p.dma("sp", "o%d" % q, lambda e, sg=sg, oc=oc, tt=tt: e.dma_start(
                            out=zsrc_t.ap()[768 + oc * 128:768 + (oc + 1) * 128, tt * 512:(tt + 1) * 512], in_=sg[:, :]), reads=[sgb], writes=[zsrc_cb[3 + oc // 2]])

        def phase_B(l):
            with ExitStack() as stB:
                cB = Ctx(nc, stB, p)
                NCH = T // 128
                prm_sb = cB.sb([128, NPRM], F32); prmb = p.buf()
                vaug = cB.sb([128, NCH, 128], BF16); vb = p.buf()
                vones = cB.sb([128, NCH, 128], BF16); vonesb = p.buf()
                g_sb = cB.sb([128, NCH, 2], F32); gsbb = p.buf()
                p.dma("sp", "prm", lambda e, l=l: e.dma_start(out=prm_sb[:, :], in_=prm[l]), writes=[prmb])
                p.op("dve", lambda e: e.memset(vones[:, :, 64:128], 1.0), writes=[vonesb])
                with ExitStack() as st:
                    c = Ctx(nc, st, p)
                    stm = [c.sb([128, 1024], BF16) for _ in range(2)]; stmb = p.bufs(2)
                    for r in range(4):
                        gather("tm%d" % (r % 2), stm[r % 2][:, :], zall1024, IDX_TM + r, 128, [zall_cb[0]], [stmb[r % 2]])
                        p.op("act", lambda e, r=r: e.copy(out=vones[:, r * 16:(r + 1) * 16, 0:64], in_=stm[r % 2].rearrange("p (c e) -> p c e", e=64)),
                             reads=[stmb[r % 2]], writes=[vonesb])
                        gather("av", vaug.rearrange("p c e -> p (c e)")[:, r * 2048:(r + 1) * 2048], zall2048, IDX_AV + r, 128, zall_cb[1:3], [vb])
                        gather("gt", g_sb.rearrange("p c t -> p (c t)")[:, r * 32:(r + 1) * 32], zallf32, IDX_GT + r, 128, [zall_cb[11]], [gsbb])
                ag_chunk(zsrc_t, zall_t, RZ, 3, zsrc_cb, zall_cb)
                p.barrier(exclude=("cc",))
                for r0 in (0, 128):
                    p.dma("sp", "hz", lambda e, r0=r0: e.dma_start(out=bass.AP(msrc_t, 1024 * 2048 + r0 * 8, [[8, 128], [1, 2]]), in_=zero_bf[:, 0:2]),
                          reads=[zerobb], writes=[msrc_cb[4]])
                with ExitStack() as st:
                    c = Ctx(nc, st, p)
                    PW = 2048
                    W = 16 + PW
                    xa = [c.sb([64, W], BF16) for _ in range(2)]; xab = p.bufs(2)
                    s_a = c.sb([64, W], F32); sab = p.buf()
                    s_b = c.sb([64, W], F32); sbb = p.buf()
                    acc = c.sb([64, W], F32); accb = p.buf()
                    d_bf = c.sb([64, PW], BF16); dbb = p.buf()
                    wp_bf = c.sb([64, 64], BF16); wpb = p.buf()
                    stg = [c.sb([64, 512], BF16) for _ in range(2)]; stgb = p.bufs(2)
                    pps = [c.ps() for _ in range(2)]; ppsb = p.bufs(2)
                    p.dma("pool", "wp", lambda e, l=l: e.dma_start(out=wp_bf[:, :], in_=poolw[l]), writes=[wpb])
                    p.op("dve", lambda e: e.memset(xa[0][:, 0:16], 0.0), writes=[xab[0]])
                    i_s = 0
                    for pc in range(T // PW):
                        X, Xb = xa[pc % 2], xab[pc % 2]
                        if pc > 0:
                            Xp, Xpb = xa[(pc - 1) % 2], xab[(pc - 1) % 2]
                            p.op("act", lambda e, X=X, Xp=Xp: e.copy(out=X[:, 0:16], in_=Xp[:, PW:PW + 16]), reads=[Xpb], writes=[Xb])
                        gather("px%d" % (pc % 2), X[0:64, 16:16 + PW], zall2048, IDX_POOL + pc, 64, [zall_cb[3]], [Xb])
                        if pc < 2:
                            ag_chunk(zsrc_t, zall_t, RZ, 4 + pc, zsrc_cb, zall_cb)
                        p.op("dve", lambda e, X=X: e.tensor_tensor(out=s_a[:, 1:W], in0=X[:, 1:W], in1=X[:, 0:W - 1], op=ALU.add), reads=[Xb], writes=[sab])
                        p.op("dve", lambda e: e.tensor_scalar(out=acc[:, 16:W], in0=s_a[:, 16:W], scalar1=prm_sb[0:64, 1:2], scalar2=None, op0=ALU.mult),
                             reads=[sab, prmb], writes=[accb])
                        src, srcb, dst, dstb = s_a, sab, s_b, sbb
                        lo = 1
                        for k, sh in ((1, 2), (2, 4), (3, 8)):
                            lo2 = lo + sh
                            p.op("dve", lambda e, src=src, dst=dst, lo2=lo2, sh=sh: e.tensor_tensor(
                                out=dst[:, lo2:W], in0=src[:, lo2:W], in1=src[:, lo2 - sh:W - sh], op=ALU.add), reads=[srcb], writes=[dstb])
                            p.op("dve", lambda e, dst=dst, k=k: e.scalar_tensor_tensor(
                                out=acc[:, 16:W], in0=dst[:, 16:W], scalar=prm_sb[0:64, 1 + k:2 + k], in1=acc[:, 16:W], op0=ALU.mult, op1=ALU.add),
                                reads=[dstb, prmb, accb], writes=[accb])
                            src, srcb, dst, dstb = dst, dstb, src, srcb
                            lo = lo2
                        if pc == 0:
                            p.op("dve", lambda e: e.tensor_tensor(out=acc[:, 16:32], in0=acc[:, 16:32], in1=prm_sb[0:64, 19:35], op=ALU.mult),
                                 reads=[accb, prmb], writes=[accb])
                        p.op("dve", lambda e, X=X: e.tensor_tensor(out=d_bf[:, :], in0=acc[:, 16:W], in1=X[:, 16:W], op=ALU.subtract), reads=[accb, Xb], writes=[dbb])
                        for q in range(PW // 512):
                            ps, psb = pps[i_s % 2], ppsb[i_s % 2]
                            sg, sgb = stg[i_s % 2], stgb[i_s % 2]
                            ch = "po%d" % (i_s % 2); i_s += 1
                            p.mm([lambda e, ps=ps, q=q: e.matmul(ps[0:64, :], lhsT=wp_bf[:, :], rhs=d_bf[:, q * 512:(q + 1) * 512], start=True, stop=True)],
                                 reads=[wpb, dbb], writes=[psb])
                            p.op("act", lambda e, ps=ps, sg=sg: e.activation(out=sg[:, :], in_=ps[0:64, :], func=AF.Copy, scale=prm_sb[0:64, 0:1]),
                                 reads=[psb, prmb], writes=[sgb])
                            out_piece(ch, sg[:, :], sgb, 0, 64, pc * PW + q * 512)
                p.barrier(exclude=("cc",))
                with ExitStack() as st:
                    c = Ctx(nc, st, p)
                    qT_bf = c.sb([64, T], BF16); qTb = p.bufs(4, "qT")
                    kT_bf = c.sb([64, T], BF16); kTb = p.bufs(4, "kT")
                    iv = c.sb([128, NCH], F32); ivb = p.buf()
                    lf = c.sb([128, NCH], F32); lfb_ = p.buf()
                    bias_s = c.sb([128, NCH], F32); biasb = p.buf()
                    wk = c.sb([128, NCH], F32); wkb = p.buf()
                    dec = c.sb([128, NCH], F32); decb = p.buf()
                    St = c.sb([64, 128], F32); Stb = p.buf()
                    St_bf2 = [c.sb([64, 128], BF16) for _ in range(2)]; Stbfb2 = p.bufs(2)
                    PW = 2048
                    xin = [c.sb([64, 3 + PW], BF16) for _ in range(2)]; xinb = p.bufs(2)
                    cacc = c.sb([64, PW], F32); caccb = p.buf()
                    n_x = 0
                    ag_chunk(zsrc_t, zall_t, RZ, 6, zsrc_cb, zall_cb)
                    for (icol, zcb, dst, dstbufs, c0col, scl) in ((IDX_MQ, [zall_cb[4]], qT_bf, qTb, 5, 1.0), (IDX_MK, [zall_cb[5]], kT_bf, kTb, 9, 0.125)):
                        for pc in range(T // PW):
                            X, Xb = xin[n_x % 2], xinb[n_x % 2]
                            if pc == 0:
                                p.op("dve", lambda e, X=X: e.memset(X[:, 0:3], 0.0), writes=[Xb])
                            else:
                                Xp, Xpb = xin[(n_x - 1) % 2], xinb[(n_x - 1) % 2]
                                p.op("act", lambda e, X=X, Xp=Xp: e.copy(out=X[:, 0:3], in_=Xp[:, PW:PW + 3]), reads=[Xpb], writes=[Xb])
                            gather("mx%d" % (n_x % 2), X[0:64, 3:3 + PW], zall2048, icol + pc, 64, zcb, [Xb])
                            n_x += 1
                            p.op("dve", lambda e, X=X, c0col=c0col: e.tensor_scalar(out=cacc[:, :], in0=X[:, 0:PW], scalar1=prm_sb[0:64, c0col:c0col + 1],
                                                                                     scalar2=None, op0=ALU.mult), reads=[Xb, prmb], writes=[caccb])
                            for j in (1, 2, 3):
                                p.op("dve", lambda e, X=X, c0col=c0col, j=j: e.scalar_tensor_tensor(
                                    out=cacc[:, :], in0=X[:, j:j + PW], scalar=prm_sb[0:64, c0col + j:c0col + j + 1], in1=cacc[:, :],
                                    op0=ALU.mult, op1=ALU.add), reads=[Xb, prmb, caccb], writes=[caccb])
                            p.op("act", lambda e: e.activation(out=cacc[:, :], in_=cacc[:, :], func=AF.Silu), reads=[caccb], writes=[caccb])
                            p.op("dve", lambda e, dst=dst, pc=pc, scl=scl: e.tensor_scalar(out=dst[:, pc * PW:(pc + 1) * PW], in0=cacc[:, :], scalar1=scl,
                                                                                           scalar2=None, op0=ALU.mult), reads=[caccb], writes=[dstbufs[pc]])
                    p.op("dve", lambda e: e.tensor_scalar(out=iv[:, :], in0=g_sb[:, :, 0], scalar1=prm_sb[:, 13:14], scalar2=None, op0=ALU.add),
                         reads=[gsbb, prmb], writes=[ivb])
                    p.op("dve", lambda e: e.tensor_scalar(out=lf[:, :], in0=g_sb[:, :, 1], scalar1=prm_sb[:, 14:15], scalar2=None, op0=ALU.add),
                         reads=[gsbb, prmb], writes=[lfb_])
                    p.op("act", lambda e: e.activation(out=lf[:, :], in_=lf[:, :], func=AF.Exp, scale=-1.0), reads=[lfb_], writes=[lfb_])
                    p.op("act", lambda e: e.activation(out=lf[:, :], in_=lf[:, :], func=AF.Ln, bias=1.0), reads=[lfb_], writes=[lfb_])
                    p.op("dve", lambda e: e.tensor_scalar(out=lf[:, :], in0=lf[:, :], scalar1=-1.0, scalar2=None, op0=ALU.mult), reads=[lfb_], writes=[lfb_])
                    psA = c.ps(); psAb = p.buf()
                    psB = c.ps(); psBb = p.buf()
                    p.mm([lambda e: e.matmul(psA[:, 0:NCH], lhsT=triU[:, :], rhs=lf[:, :], start=True, stop=True)], reads=[triUb, lfb_], writes=[psAb])
                    p.mm([lambda e: e.matmul(psB[:, 0:NCH], lhsT=ones_f[:, :], rhs=lf[:, :], start=True, stop=True)], reads=[onesfb, lfb_], writes=[psBb])
                    p.op("dve", lambda e: e.tensor_tensor(out=bias_s[:, :], in0=iv[:, :], in1=psA[:, 0:NCH], op=ALU.subtract), reads=[ivb, psAb], writes=[biasb])
                    p.op("dve", lambda e: e.tensor_tensor(out=wk[:, :], in0=bias_s[:, :], in1=psB[:, 0:NCH], op=ALU.add), reads=[biasb, psBb], writes=[wkb])
                    p.op("act", lambda e: e.activation(out=wk[:, :], in_=wk[:, :], func=AF.Exp), reads=[wkb], writes=[wkb])
                    p.op("act", lambda e: e.activation(out=dec[:, :], in_=psB[:, 0:NCH], func=AF.Exp), reads=[psBb], writes=[decb])
                    p.op("dve", lambda e: e.memset(St[:, :], 0.0), writes=[Stb])
                    p.op("dve", lambda e: e.memset(St_bf2[0][:, :], 0.0), writes=[Stbfb2[0]])
                    lfrep = [c.sb([128, 128], F32) for _ in range(2)]; lfrepb = p.bufs(2)
                    DT = [c.sb([128, 128], F32) for _ in range(2)]; DTb = p.bufs(2)
                    WT = [c.sb([128, 128], BF16) for _ in range(2)]; WTb = p.bufs(2)
                    eG = [c.sb([64, 128], F32) for _ in range(2)]; eGb = p.bufs(2)
                    qs = [c.sb([64, 128], BF16) for _ in range(2)]; qsb = p.bufs(2)
                    ksc = [c.sb([128, 64], BF16) for _ in range(2)]; kscb = p.bufs(2)
                    dn = [c.sb([64, 128], F32) for _ in range(2)]; dnb = p.bufs(2)
                    hTm = [c.sb([64, 512], F32) for _ in range(2)]; hTmb = p.bufs(2)
                    sqm = c.sb([64, 512], BF16); sqmb = p.buf()
                    rstdm = c.sb([64, 512], F32); rstdmb = p.buf()
                    mo_sb = [c.sb([64, 512], BF16) for _ in range(2)]; mob = p.bufs(2)
                    mo_f = [c.sb([64, 512], F32) for _ in range(2)]; mofb = p.bufs(2)
                    ho = [c.sb([64, 512], F32) for _ in range(2)]; hob = p.bufs(2)
                    ho_bf = [c.sb([64, 512], BF16) for _ in range(2)]; hobfb = p.bufs(2)
                    psG = [c.ps() for _ in range(2)]; psGb = p.bufs(2)
                    psN = [c.ps() for _ in range(2)]; psNb = p.bufs(2)
                    psK = [c.ps() for _ in range(2)]; psKb = p.bufs(2)

                    def stage1(ch):
                        q = ch % 2
                        t0 = ch * 128
                        pcq = ch // 16
                        p.op("dve", lambda e: e.tensor_scalar(out=lfrep[q][:, :], in0=ones_f[:, :], scalar1=lf[:, ch:ch + 1], scalar2=None, op0=ALU.mult),
                             reads=[onesfb, lfb_], writes=[lfrepb[q]])
                        p.mm([lambda e: e.matmul(psG[q][:, 0:128], lhsT=lfrep[q][:, :], rhs=triU[:, :], start=True, stop=True),
                              lambda e: e.matmul(psG[q][:, 128:256], lhsT=kT_bf[:, t0:t0 + 128], rhs=qT_bf[:, t0:t0 + 128], start=True, stop=True),
                              lambda e: e.matmul(psK[q][:, 0:64], lhsT=kT_bf[:, t0:t0 + 128], rhs=ident_bf[0:64, 0:64], start=True, stop=True)],
                             reads=[lfrepb[q], triUb, kTb[pcq], qTb[pcq], identb], writes=[psGb[q], psKb[q]])
                        p.op("act", lambda e: e.activation(out=DT[q][:, :], in_=psG[q][:, 0:128], func=AF.Exp, bias=bias_s[:, ch:ch + 1]),
                             reads=[psGb[q], biasb], writes=[DTb[q]])
                        p.op("act", lambda e: e.activation(out=eG[q][:, :], in_=psG[q][0:64, 0:128], func=AF.Exp), reads=[psGb[q]], writes=[eGb[q]])
                        p.op("act", lambda e: e.activation(out=ksc[q][:, :], in_=psK[q][:, 0:64], func=AF.Copy, scale=wk[:, ch:ch + 1]),
                             reads=[psKb[q], wkb], writes=[kscb[q]])
                        p.op("dve", lambda e: e.tensor_tensor(out=DT[q][:, :], in0=DT[q][:, :], in1=triU[:, :], op=ALU.mult), reads=[DTb[q], triUb], writes=[DTb[q]])
                        p.op("dve", lambda e: e.tensor_tensor(out=qs[q][:, :], in0=eG[q][:, :], in1=qT_bf[:, t0:t0 + 128], op=ALU.mult),
                             reads=[eGb[q], qTb[pcq]], writes=[qsb[q]])
                        p.op("dve", lambda e: e.tensor_tensor(out=WT[q][:, :], in0=DT[q][:, :], in1=psG[q][:, 128:256], op=ALU.mult), reads=[DTb[q], psGb[q]], writes=[WTb[q]])
                        p.mm([lambda e: e.matmul(psK[q][0:64, 64:192], lhsT=ksc[q][:, :], rhs=vones[:, ch, :], start=True, stop=True)],
                             reads=[kscb[q], vonesb], writes=[psKb[q]])

                    def stage2(ch):
                        q = ch % 2
                        St_bf, Stbfb = St_bf2[q], Stbfb2[q]
                        St_nx, Stnxb = St_bf2[1 - q], Stbfb2[1 - q]
                        p.mm([lambda e: e.matmul(psN[q][0:64, 0:128], lhsT=vones[:, ch, 0:64], rhs=WT[q][:, :], start=True, stop=False),
                              lambda e: e.matmul(psN[q][0:64, 0:128], lhsT=St_bf[:, 0:64], rhs=qs[q][:, :], start=False, stop=True),
                              lambda e: e.matmul(psN[q][0:64, 128:256], lhsT=vones[:, ch, 64:128], rhs=WT[q][:, :], start=True, stop=False),
                              lambda e: e.matmul(psN[q][0:64, 128:256], lhsT=St_bf[:, 64:128], rhs=qs[q][:, :], start=False, stop=True)],
                             reads=[vonesb, WTb[q], Stbfb, qsb[q]], writes=[psNb[q]])
                        p.op("dve", lambda e: e.scalar_tensor_tensor(out=St[:, :], in0=St[:, :], scalar=dec[0:64, ch:ch + 1], in1=psK[q][0:64, 64:192],
                                                                     op0=ALU.mult, op1=ALU.add), reads=[Stb, decb, psKb[q]], writes=[Stb])
                        p.op("act", lambda e: e.copy(out=St_nx[:, :], in_=St[:, :]), reads=[Stb], writes=[Stnxb])
                        hq = (ch // 4) % 2
                        hc = (ch % 4) * 128
                        p.op("act", lambda e: e.activation(out=dn[q][:, :], in_=psN[q][0:64, 128:256], func=AF.Abs), reads=[psNb[q]], writes=[dnb[q]])
                        p.op("dve", lambda e: e.tensor_scalar(out=dn[q][:, :], in0=dn[q][:, :], scalar1=1.0, scalar2=None, op0=ALU.max), reads=[dnb[q]], writes=[dnb[q]])
                        p.op("dve", lambda e: e.reciprocal(out=dn[q][:, :], in_=dn[q][:, :]), reads=[dnb[q]], writes=[dnb[q]])
                        p.op("dve", lambda e: e.tensor_tensor(out=hTm[hq][:, hc:hc + 128], in0=psN[q][0:64, 0:128], in1=dn[q][:, :], op=ALU.mult),
                             reads=[psNb[q], dnb[q]], writes=[hTmb[hq]])
                        if ch % 4 == 3:
                            pci = ch // 4
                            pt0 = pci * 512
                            H_, Hb_ = hTm[hq], hTmb[hq]
                            gather("mo%d" % hq, mo_sb[hq][0:64, :], zall512, IDX_MO + pci, 64, [zall_cb[6]], [mob[hq]])
                            p.op("act", lambda e: e.activation(out=sqm[:, :], in_=H_[:, :], func=AF.Square), reads=[Hb_], writes=[sqmb])
                            p.mm([lambda e: e.matmul(psA[0:64, :], lhsT=ones_bf[0:64, 0:64], rhs=sqm[:, :], start=True, stop=True)], reads=[onesbb, sqmb], writes=[psAb])
                            p.op("act", lambda e: e.activation(out=rstdm[:, :], in_=psA[0:64, :], func=AF.Ln, bias=EPS, scale=1.0 / 64), reads=[psAb], writes=[rstdmb])
                            p.op("act", lambda e: e.activation(out=rstdm[:, :], in_=rstdm[:, :], func=AF.Exp, scale=-0.5), reads=[rstdmb], writes=[rstdmb])
                            p.op("act", lambda e: e.activation(out=mo_f[hq][:, :], in_=mo_sb[hq][:, :], func=AF.Sigmoid), reads=[mob[hq]], writes=[mofb[hq]])
                            p.op("dve", lambda e: e.scalar_tensor_tensor(out=ho[hq][:, :], in0=H_[:, :], scalar=prm_sb[0:64, 15:16], in1=rstdm[:, :],
                                                                         op0=ALU.mult, op1=ALU.mult), reads=[Hb_, prmb, rstdmb], writes=[hob[hq]])
                            p.op("dve", lambda e: e.tensor_tensor(out=ho_bf[hq][:, :], in0=ho[hq][:, :], in1=mo_f[hq][:, :], op=ALU.mult),
                                 reads=[hob[hq], mofb[hq]], writes=[hobfb[hq]])
                            out_piece("ho%d" % hq, ho_bf[hq][:, :], hobfb[hq], 64, 64, pt0)
                            if pci < 4:
                                ag_chunk(zsrc_t, zall_t, RZ, 7 + pci, zsrc_cb, zall_cb)
                            elif pci == 4:
                                ag_chunk(msrc_t, mall_t, RM, 0, msrc_cb, mall_cb)

                    stage1(0)
                    for ch in range(NCH):
                        if ch + 1 < NCH:
                            stage1(ch + 1)
                        stage2(ch)
                p.barrier(exclude=("cc",))
                fused_moba(nc, p, gather, out_piece, prm_sb, prmb, ident_bf, identb, ones_bf, onesbb, blk, blkb, caus, causb, vaug, vb,
                           ind_d, zall512, zall_cb, lambda i: ag_chunk(msrc_t, mall_t, RM, i, msrc_cb, mall_cb),
                           None)

        for l in range(nlayers):
            phase_A(l)
            p.barrier(exclude=("cc",))
            if upto < 2:
                continue
            phase_B(l)
            p.barrier(exclude=("cc",))
            if upto < 3:
                continue
            fused_C(nc, p, l, gather, x_sb, xb, onesD_bf, onesDb, w_out, g2, w_up, convw, w_down, zall16, zall_cb, mall512, mall2, mall_cb)
            p.barrier(exclude=("cc",))
        for k in range(8):
            p.dma("sp", "out", lambda e, k=k: e.dma_start(out=xoT[k * 128:(k + 1) * 128, :], in_=x_sb[:, k, 2:TQH]), reads=[xb[k]], writes=[outb])
        p.finish("sp", [outb])
        p.emit()
    return nc


def fused_moba(nc, p, gather, out_piece, prm_sb, prmb, ident_bf, identb, ones_bf, onesbb, blk, blkb, caus, causb, vaug, vb, ind_d, zall512, zall_cb, ag2, agz24):
    NP = T // 512
    with ExitStack() as st:
        c = Ctx(nc, st, p)
        Kaug = [c.sb([128, T], BF16) for _ in range(2)]; Kb = [p.bufs(NP, "K%d" % h) for h in range(2)]
        Qaug = [c.sb([128, T], BF16) for _ in range(2)]; Qb = [p.bufs(NP, "Q%d" % h) for h in range(2)]
        kms = c.sb([128, 64], F32); kmsb = p.buf()
        AUX = ((64, 96), (0, 32))
        DAT = ((0, 64), (64, 128))
        for h in range(2):
            p.op("dve", lambda e, h=h: e.memset(Kaug[h][:, :], 0.0), writes=Kb[h])
            p.op("pool", lambda e, h=h: e.memset(Qaug[h][:, :], 0.0), writes=Qb[h])
        for h in range(2):
            a0, a1 = AUX[h]
            for q in range(T // 2048):
                p.dma("pool", "ind%d" % h, lambda e, h=h, a0=a0, a1=a1, q=q: e.dma_start(out=Kaug[h][a0:a1, q * 2048:(q + 1) * 2048],
                                                                                       in_=ind_d[:, q * 2048:(q + 1) * 2048]), writes=Kb[h])
        p.op("dve", lambda e: e.memset(kms[:, :], 0.0), writes=[kmsb])
        with ExitStack() as st2:
            c2 = Ctx(nc, st2, p)
            xin = [c2.sb([128, 512], BF16) for _ in range(2)]; xinb = p.bufs(2)
            sq = c2.sb([128, 512], BF16); sqb = p.buf()
            rstd = c2.sb([128, 512], F32); rstdb = p.buf()
            xn = [c2.sb([128, 512], F32) for _ in range(2)]; xnb = p.bufs(2)
            gsb = [c2.sb([128, 64], F32) for _ in range(2)]; gsbb = p.bufs(2)
            top8 = [c2.sb([128, 16], F32) for _ in range(2)]; top8b = p.bufs(2)
            nm = [c2.sb([128, 4, 128], BF16) for _ in range(2)]; nmb = p.bufs(2)
            psM = c2.ps(); psMb = p.buf()
            psGt = [c2.ps() for _ in range(2)]; psGtb = p.bufs(2)
            psT = [c2.ps() for _ in range(2)]; psTb = p.bufs(2)
            for q in range(2):
                p.op("pool", lambda e, q=q: e.memset(nm[q][:, :, :], 0.0), writes=[nmb[q]])
            xnq = [c2.sb([128, 512], F32) for _ in range(2)]; xnqb = p.bufs(2)
            cnt = {"in": 0, "g": 0}

            def normstage(pc):
                t0 = pc * 512
                for which in range(2):
                    icol = IDX_AK if which == 0 else IDX_AQ
                    zcb = zall_cb[9:11] if which == 0 else zall_cb[7:9]
                    gcol = 17 if which == 0 else 16
                    X, Xb = xin[cnt["in"] % 2], xinb[cnt["in"] % 2]
                    gather("ax%d" % (cnt["in"] % 2), X[:, :], zall512, icol + pc, 128, zcb, [Xb])
                    cnt["in"] += 1
                    p.op("act", lambda e, X=X: e.activation(out=sq[:, :], in_=X[:, :], func=AF.Square), reads=[Xb], writes=[sqb])
                    p.mm([lambda e: e.matmul(psM[:, :], lhsT=blk[:, :], rhs=sq[:, :], start=True, stop=True)], reads=[blkb, sqb], writes=[psMb])
                    p.op("act", lambda e: e.activation(out=rstd[:, :], in_=psM[:, :], func=AF.Ln, bias=EPS, scale=1.0 / 64), reads=[psMb], writes=[rstdb])
                    p.op("act", lambda e: e.activation(out=rstd[:, :], in_=rstd[:, :], func=AF.Exp, scale=-0.5), reads=[rstdb], writes=[rstdb])
                    if which == 0:
                        N_, Nb_ = xn[0], xnb[0]
                    else:
                        N_, Nb_ = xnq[pc % 2], xnqb[pc % 2]
                    p.op("dve", lambda e, X=X, N_=N_, gcol=gcol: e.scalar_tensor_tensor(out=N_[:, :], in0=X[:, :], scalar=prm_sb[:, gcol:gcol + 1], in1=rstd[:, :],
                                                                 op0=ALU.mult, op1=ALU.mult), reads=[Xb, prmb, rstdb], writes=[Nb_])
                    dst = Kaug if which == 0 else Qaug
                    dstb = Kb if which == 0 else Qb
                    for h in range(2):
                        d0, d1 = DAT[h]
                        p.op("act", lambda e, h=h, d0=d0, d1=d1, dst=dst, N_=N_: e.copy(out=dst[h][d0:d1, t0:t0 + 512], in_=N_[d0:d1, :]),
                             reads=[Nb_], writes=[dstb[h][pc]])
                    if which == 0:
                        for h in range(2):
                            p.op("dve", lambda e, h=h, N_=N_: e.tensor_reduce(
                                out=kms[h * 64:(h + 1) * 64, h * 32 + 2 * pc:h * 32 + 2 * pc + 2],
                                in_=N_[h * 64:(h + 1) * 64, :].rearrange("q (b k) -> q b k", k=256), axis=AX.X, op=ALU.add), reads=[Nb_], writes=[kmsb])
                if pc == 1:
                    ag2(1)

            def gatestage(pc):
                t0 = pc * 512
                QN, QNb = xnq[pc % 2], xnqb[pc % 2]
                nq = pc % 2
                for qb in range(4):
                    own = 2 * pc + qb // 2
                    gq = cnt["g"] % 2; cnt["g"] += 1
                    p.mm([lambda e, gq=gq, qb=qb: e.matmul(psGt[gq][:, 0:64], lhsT=QN[:, qb * 128:(qb + 1) * 128], rhs=kms[:, :], start=True, stop=True)],
                         reads=[QNb, kmsb], writes=[psGtb[gq]])
                    p.op("dve", lambda e, gq=gq: e.memset(gsb[gq][:, :], -1e30), writes=[gsbb[gq]])
                    if own > 0:
                        p.op("dve", lambda e, gq=gq, own=own: e.tensor_copy(out=gsb[gq].rearrange("q (h n) -> q h n", h=2)[:, :, 0:own],
                                                                            in_=psGt[gq][:, 0:64].rearrange("q (h n) -> q h n", h=2)[:, :, 0:own]),
                             reads=[psGtb[gq]], writes=[gsbb[gq]])
                    for h in range(2):
                        p.op("dve", lambda e, gq=gq, h=h: e.max(out=top8[gq][:, h * 8:(h + 1) * 8], in_=gsb[gq][:, h * 32:(h + 1) * 32]),
                             reads=[gsbb[gq]], writes=[top8b[gq]])
                    for h in range(2):
                        p.op("dve", lambda e, gq=gq, h=h: e.tensor_scalar(out=gsb[gq][:, h * 32:(h + 1) * 32], in0=gsb[gq][:, h * 32:(h + 1) * 32],
                                                                          scalar1=top8[gq][:, h * 8 + 2:h * 8 + 3], scalar2=None, op0=ALU.is_ge),
                             reads=[gsbb[gq], top8b[gq]], writes=[gsbb[gq]])
                    for h in range(2):
                        cc0 = 64 if h == 0 else 96
                        p.op("dve", lambda e, gq=gq, h=h, cc0=cc0, qb=qb: e.tensor_scalar(out=nm[nq][:, qb, cc0:cc0 + 32], in0=gsb[gq][:, h * 32:(h + 1) * 32],
                                                                                      scalar1=-1.0, scalar2=-NEG, op0=ALU.add, op1=ALU.mult),
                             reads=[gsbb[gq]], writes=[nmb[nq]])
                        p.op("dve", lambda e, cc0=cc0, qb=qb, own=own: e.memset(nm[nq][:, qb, cc0 + own:cc0 + own + 1], 0.0), writes=[nmb[nq]])
                        if own < 31:
                            p.op("dve", lambda e, cc0=cc0, qb=qb, own=own: e.memset(nm[nq][:, qb, cc0 + own + 1:cc0 + 32], NEG), writes=[nmb[nq]])

            def transstage(pc):
                t0 = pc * 512
                nq = pc % 2
                tq = pc % 2
                p.mm([lambda e, qb=qb: e.matmul(psT[tq][0:96, qb * 128:(qb + 1) * 128], lhsT=nm[nq][:, qb, 0:96], rhs=ident_bf[:, :], start=True, stop=True)
                      for qb in range(4)], reads=[nmb[nq], identb], writes=[psTb[tq]])
                p.op("act", lambda e: e.copy(out=Qaug[0][64:96, t0:t0 + 512], in_=psT[tq][64:96, :]), reads=[psTb[tq]], writes=[Qb[0][pc]])
                p.mm([lambda e, qb=qb: e.matmul(psT[tq][0:32, qb * 128:(qb + 1) * 128], lhsT=nm[nq][:, qb, 96:128], rhs=ident_bf[:, :], start=True, stop=True)
                      for qb in range(4)], reads=[nmb[nq], identb], writes=[psTb[tq]])
                p.op("act", lambda e: e.copy(out=Qaug[1][0:32, t0:t0 + 512], in_=psT[tq][0:32, :]), reads=[psTb[tq]], writes=[Qb[1][pc]])

            normstage(0)
            for pc in range(NP):
                if pc + 1 < NP:
                    normstage(pc + 1)
                if pc > 0:
                    transstage(pc - 1)
                gatestage(pc)
            transstage(NP - 1)
        p.barrier(exclude=("cc",))
        with ExitStack() as st3:
            c3 = Ctx(nc, st3, p)
            NPT = 3
            NPS = 2
            PT = [c3.sb([128, 1024], BF16) for _ in range(NPT)]; PTb = p.bufs(NPT)
            psS = [c3.ps([128, 1024]) for _ in range(NPS)]; psSb = p.bufs(NPS)
            psO = [c3.ps() for _ in range(2)]; psOb = p.bufs(2)
            psR = [c3.ps() for _ in range(2)]; psRb = p.bufs(2)
            rr = [c3.sb([64, 512], F32) for _ in range(2)]; rrb = p.bufs(2)
            ao = [c3.sb([64, 512], BF16) for _ in range(2)]; aob = p.bufs(2)
            tiles = []
            for pc in range(NP):
                for h in range(2):
                    for kt in range(0, 4 * (pc + 1), 2):
                        tiles.append((pc, h, kt))

            def stage1(i):
                pc, h, kt0 = tiles[i]
                rows = 96 if h == 0 else 128
                t0 = pc * 512
                sq_ = i % NPS
                pq = i % NPT
                p.mm([lambda e, u=u: e.matmul(psS[sq_][:, u * 512:(u + 1) * 512], lhsT=Kaug[h][0:rows, (kt0 + u) * 128:(kt0 + u + 1) * 128],
                                              rhs=Qaug[h][0:rows, t0:t0 + 512], start=True, stop=True) for u in range(2)],
                     reads=[Kb[h][kt0 // 4], Qb[h][pc]], writes=[psSb[sq_]])
                p.op("act", lambda e: e.activation(out=PT[pq][:, :], in_=psS[sq_][:, :], func=AF.Exp, scale=0.125),
                     reads=[psSb[sq_]], writes=[PTb[pq]])
                if kt0 >= 4 * pc:
                    r = kt0 - 4 * pc
                    p.op("dve", lambda e: e.tensor_tensor(out=PT[pq][:, :], in0=PT[pq][:, :], in1=caus[:, r * 512:(r + 2) * 512], op=ALU.mult),
                         reads=[PTb[pq], causb], writes=[PTb[pq]])

            def stage2(i):
                pc, h, kt0 = tiles[i]
                nkt = 4 * (pc + 1)
                t0 = pc * 512
                pq = i % NPT
                oq = (2 * pc + h) % 2
                p.mm([lambda e, u=u: e.matmul(psO[oq][0:64, :], lhsT=vaug[:, kt0 + u, h * 64:(h + 1) * 64], rhs=PT[pq][:, u * 512:(u + 1) * 512],
                                              start=(kt0 + u == 0), stop=(kt0 + u == nkt - 1)) for u in range(2)],
                     reads=[vb, PTb[pq]], writes=[psOb[oq]])
                p.mm([lambda e, u=u: e.matmul(psR[oq][0:64, :], lhsT=ones_bf[:, 0:64], rhs=PT[pq][:, u * 512:(u + 1) * 512],
                                              start=(kt0 + u == 0), stop=(kt0 + u == nkt - 1)) for u in range(2)],
                     reads=[onesbb, PTb[pq]], writes=[psRb[oq]])
                if kt0 + 2 == nkt:
                    p.op("dve", lambda e: e.reciprocal(out=rr[oq][:, :], in_=psR[oq][0:64, :]), reads=[psRb[oq]], writes=[rrb[oq]])
                    p.op("dve", lambda e: e.tensor_tensor(out=ao[oq][:, :], in0=psO[oq][0:64, :], in1=rr[oq][:, :], op=ALU.mult),
                         reads=[psOb[oq], rrb[oq]], writes=[aob[oq]])
                    out_piece("ao%d" % oq, ao[oq][:, :], aob[oq], 128 + h * 64, 64, t0)
                    if h == 1 and pc == 7:
                        ag2(2)

            LOOK = 1
            n = len(tiles)
            for i in range(min(LOOK, n)):
                stage1(i)
            for i in range(n):
                if i + LOOK < n:
                    stage1(i + LOOK)
                stage2(i)
            ag2(3)
            ag2(4)


def fused_C(nc, p, l, gather, x_sb, xb, ones_bf, onesb, w_out, g2, w_up, convw, w_down, zall16, zall_cb, mall512, mall2, mall_cb):
    with ExitStack() as st:
        c = Ctx(nc, st, p)
        wo_sb = c.sb([128, 8, D], BF16); wob = p.buf()
        wd_sb = c.sb([128, 22, D], BF16); wdb = p.buf()
        g_sb = c.sb([128, 8], F32); gb = p.buf()
        cw_sb = c.sb([128, 44, 3], F32); cwb = p.buf()
        xt = c.sb([128, 16], F32); xtb = p.buf()
        mh = c.sb([128, 8, 2], BF16); mhb = p.buf()
        mixb = [c.sb([128, 8, 512], BF16) for _ in range(2)]; mixbb = p.bufs(2)
        NWU = 3
        wu = [c.sb([128, 8, 256], BF16) for _ in range(NWU)]; wub = p.bufs(NWU)
        act = c.sb([128, 22, 512], BF16); actb = p.bufs(22, "act")
        hT = c.sb([128, 8, 512], BF16); hb = p.bufs(8, "h")
        rstd = c.sb([128, 512], F32); rstdb = p.buf()
        carry = c.sb([128, 22, 2, 2], F32); carryb = p.bufs(22, "cy")
        u2 = [c.sb([128, 2, 514], F32) for _ in range(2)]; u2b = p.bufs(2)
        tg = [c.sb([128, 512], F32) for _ in range(2)]; tgb = p.bufs(2)
        tv = [c.sb([128, 512], F32) for _ in range(2)]; tvb = p.bufs(2)
        psn = c.ps(); psnb = p.buf()
        psgv = [c.ps([128, 1024]) for _ in range(2)]; psgvb = p.bufs(2)
        pso = [c.ps() for _ in range(2)]; psob = p.bufs(2)
        p.dma("sp", "g", lambda e: e.dma_start(out=g_sb[:, :], in_=g2[l]), writes=[gb])
        p.dma("sp", "cw", lambda e: e.dma_start(out=cw_sb[:, :, :], in_=convw[l]), writes=[cwb])
        for k in range(8):
            p.dma("pool", "wo", lambda e, k=k: e.dma_start(out=wo_sb[:, k, :], in_=w_out[l, k * 128:(k + 1) * 128, :]), writes=[wob])
        p.op("dve", lambda e: e.memset(carry[:, :, :], 0.0), writes=list(carryb))
        gather("xt", xt[:, :], zall16, IDX_XT, 128, [zall_cb[11]], [xtb])
        p.op("act", lambda e: e.copy(out=x_sb[:, :, 0:2], in_=xt.rearrange("q (k t) -> q k t", t=2)), reads=[xtb], writes=list(xb))
        wd_loaded = False
        i_o = 0
        i_u = 0
        for ti, (col0, n) in enumerate(C_TILES):
            mb, mbb = mixb[ti % 2], mixbb[ti % 2]
            if ti == 0:
                for k in range(8):
                    gather("mh", mh[:, k, :], mall2, IDX_MH + k, 128, [mall_cb[4]], [mhb])
                p.op("act", lambda e, mb=mb: e.copy(out=mb[:, :, 0:2], in_=mh[:, :, :]), reads=[mhb], writes=[mbb])
            else:
                for k in range(8):
                    gather("mix%d" % (ti % 2), mb[:, k, :], mall512, IDX_MX + k * 4 + (ti - 1), 128, ([mall_cb[0]] if k < 2 else [mall_cb[1]] if k < 4 else mall_cb[2:4]), [mbb])
            for oc in range(8):
                ps, psb = pso[i_o % 2], psob[i_o % 2]; i_o += 1
                p.mm([lambda e, k=k, oc=oc, ps=ps, mb=mb, n=n: e.matmul(ps[:, :n], lhsT=wo_sb[:, k, oc * 128:(oc + 1) * 128], rhs=mb[:, k, :n],
                                                                       start=(k == 0), stop=(k == 7)) for k in range(8)],
                     reads=[wob, mbb], writes=[psb])
                p.op("dve", lambda e, oc=oc, ps=ps, col0=col0, n=n: e.tensor_tensor(
                    out=x_sb[:, oc, col0:col0 + n], in0=x_sb[:, oc, col0:col0 + n], in1=ps[:, :n], op=ALU.add),
                    reads=[psb, xb[oc]], writes=[xb[oc]])
            rmsnorm_fm(p, c, x_sb, None, col0, n, g_sb, gb, ones_bf, onesb, hT, hb, act, actb[:8], psn, psnb, rstd, rstdb, xbs=xb)
            if not wd_loaded:
                for fc in range(22):
                    p.dma("pool", "wd", lambda e, fc=fc: e.dma_start(out=wd_sb[:, fc, :], in_=w_down[l, fc * 128:(fc + 1) * 128, :]), writes=[wdb])
                wd_loaded = True
            for fc in range(22):
                q = i_u % 2
                w = i_u % NWU
                i_u += 1
                p.dma("pool", "wu%d" % w, lambda e, w=w, fc=fc: e.dma_start(out=wu[w][:, :, :], in_=w_up[l, fc]), writes=[wub[w]])
                p.mm([lambda e, k=k, q=q, w=w, n=n: e.matmul(psgv[q][:, 0:n], lhsT=wu[w][:, k, 0:128], rhs=hT[:, k, :n], start=(k == 0), stop=(k == 7))
                      for k in range(8)] +
                     [lambda e, k=k, q=q, w=w, n=n: e.matmul(psgv[q][:, 512:512 + n], lhsT=wu[w][:, k, 128:256], rhs=hT[:, k, :n], start=(k == 0), stop=(k == 7))
                      for k in range(8)], reads=[wub[w]] + list(hb), writes=[psgvb[q]])
                U, Ub = u2[q], u2b[q]
                p.op("act", lambda e, U=U, fc=fc: e.copy(out=U[:, :, 0:2], in_=carry[:, fc, :, :]), reads=[carryb[fc]], writes=[Ub])
                p.op("act", lambda e, U=U, q=q, n=n: e.copy(out=U[:, :, 2:2 + n], in_=psgv[q].rearrange("p (h w) -> p h w", h=2)[:, :, 0:n]),
                     reads=[psgvb[q]], writes=[Ub])
                p.op("act", lambda e, U=U, fc=fc, n=n: e.copy(out=carry[:, fc, :, :], in_=U[:, :, n:n + 2]), reads=[Ub], writes=[carryb[fc]])
                for (hh, t_, tb_, ch) in ((0, tg[q], tgb[q], fc), (1, tv[q], tvb[q], 22 + fc)):
                    p.op("dve", lambda e, U=U, t_=t_, hh=hh, ch=ch, n=n: e.tensor_scalar(
                        out=t_[:, :n], in0=U[:, hh, 0:n], scalar1=cw_sb[:, ch, 0:1], scalar2=None, op0=ALU.mult), reads=[Ub, cwb], writes=[tb_])
                for jj in (1, 2):
                    for (hh, t_, tb_, ch) in ((0, tg[q], tgb[q], fc), (1, tv[q], tvb[q], 22 + fc)):
                        p.op("dve", lambda e, U=U, t_=t_, hh=hh, ch=ch, n=n, jj=jj: e.scalar_tensor_tensor(
                            out=t_[:, :n], in0=U[:, hh, jj:jj + n], scalar=cw_sb[:, ch, jj:jj + 1], in1=t_[:, :n], op0=ALU.mult, op1=ALU.add),
                            reads=[Ub, cwb, tb_], writes=[tb_])
                p.op("act", lambda e, q=q, n=n: e.activation(out=tg[q][:, :n], in_=tg[q][:, :n], func=AF.Silu), reads=[tgb[q]], writes=[tgb[q]])
                p.op("dve", lambda e, q=q, fc=fc, n=n: e.tensor_tensor(out=act[:, fc, :n], in0=tg[q][:, :n], in1=tv[q][:, :n], op=ALU.mult),
                     reads=[tgb[q], tvb[q]], writes=[actb[fc]])
            for oc in range(8):
                ps, psb = pso[i_o % 2], psob[i_o % 2]; i_o += 1
                p.mm([lambda e, fc=fc, oc=oc, ps=ps, n=n: e.matmul(ps[:, :n], lhsT=wd_sb[:, fc, oc * 128:(oc + 1) * 128], rhs=act[:, fc, :n],
                                                                   start=(fc == 0), stop=(fc == 21)) for fc in range(22)],
                     reads=[wdb] + list(actb), writes=[psb])
                p.op("dve", lambda e, oc=oc, ps=ps, col0=col0, n=n: e.tensor_tensor(
                    out=x_sb[:, oc, col0:col0 + n], in0=x_sb[:, oc, col0:col0 + n], in1=ps[:, :n], op=ALU.add),
                    reads=[psb, xb[oc]], writes=[xb[oc]])


def fused_w_in_cols():
    fm, _ = w_in_cols()
    tm = list(range(768, 1024)) + list(range(2312, 2824))
    for g in range(4):
        tm += [1280 + g, 1284 + g]
    return np.array(fm), np.array(tm)


def ag_row(r, rho, nrows):
    rho = np.asarray(rho, dtype=np.int64)
    ch = rho // CHR
    n = np.minimum(CHR, nrows - ch * CHR)
    return 4 * ch * CHR + r * n + (rho - ch * CHR)


def fused_idx(cidx):
    r_ = cidx % 4
    g = j = r_
    pp = np.arange(128, dtype=np.int64)
    idx = np.zeros((128, NIDX), np.int64)
    gp = g * 128 + pp
    for r in range(4):
        idx[:, IDX_TM + r] = ag_row(r, gp // 2, RZ) * 2 + gp % 2
        idx[:, IDX_AV + r] = ag_row(r, 256 + gp, RZ)
        idx[:, IDX_GT + r] = (r * RF + gp // 64) * 64 + gp % 64
        idx[:, IDX_POOL + r] = ag_row(r, 768 + g * 64 + (pp % 64), RZ)
        idx[:, IDX_MQ + r] = ag_row(r, 768 + 256 + g * 64 + (pp % 64), RZ)
        idx[:, IDX_MK + r] = ag_row(r, 768 + 512 + g * 64 + (pp % 64), RZ)
    for pc in range(16):
        r, f = pc // 4, pc % 4
        idx[:, IDX_MO + pc] = ag_row(r, 768 + 768 + g * 64 + (pp % 64), RZ) * 4 + f
        idx[:, IDX_AQ + pc] = ag_row(r, 768 + 1024 + g * 128 + pp, RZ) * 4 + f
        idx[:, IDX_AK + pc] = ag_row(r, 768 + 1536 + g * 128 + pp, RZ) * 4 + f
    if j > 0:
        idx[:, IDX_XT] = ((j - 1) * RF + 8) * 128 + pp
    else:
        idx[:, IDX_XT] = (0 * RF + 9) * 128 + pp
    for k in range(8):
        f = k * 128 + pp
        gq = np.where(f < 256, f // 64, np.where(f < 512, (f - 256) // 64, (f - 512) // 128))
        rr = np.where(f < 256, f % 64, np.where(f < 512, 64 + (f - 256) % 64, 128 + (f - 512) % 128))
        srow = np.where(rr < 128, rr * 4 + j, 512 + j * 128 + (rr - 128))
        row = np.array([ag_row(int(a), int(b), RM) for a, b in zip(gq, srow)])
        hrow = np.array([ag_row(int(a), 1024, RM) for a in gq])
        for q in range(4):
            idx[:, IDX_MX + k * 4 + q] = row * 4 + q
        idx[:, IDX_MH + k] = hrow * 1024 + rr * 4 + j
    return idx.astype(np.uint32)


_FUSED = {}


def kernel(**inputs):
    inp = {k: np.ascontiguousarray(np.asarray(v), dtype=np.float32) for k, v in inputs.items()}
    x = inp["x"]
    fm, tm = fused_w_in_cols()
    wfm = np.ascontiguousarray(inp["w_in"][:, :, fm])
    wtm = np.ascontiguousarray(inp["w_in"][:, :, tm])
    g1 = np.ascontiguousarray(inp["ln1_g"].reshape(4, 8, 128).transpose(0, 2, 1))
    g2 = np.ascontiguousarray(inp["ln2_g"].reshape(4, 8, 128).transpose(0, 2, 1))
    cw = np.ascontiguousarray(inp["ffn_conv"].reshape(4, 3, 44, 128).transpose(0, 3, 2, 1))
    triU, ident, caus, ind = static_consts()
    w_up_p = np.ascontiguousarray(inp["w_up"].reshape(4, 8, 128, 2, 22, 128).transpose(0, 4, 2, 1, 3, 5).reshape(4, 22, 128, 8, 256))
    maps = []
    for cidx in range(NCORE):
        b, g = cidx // 4, cidx % 4
        j = g
        w = POOL_WINDOWS[g]
        prm = np.zeros((4, 128, NPRM), np.float32)
        for l in range(4):
            prm[l, 0:64, 0] = inp["pool_scale"][l, g * 64:(g + 1) * 64]
            prm[l, :, 1 + g] = 1.0 / w
            prm[l, 0:64, 5:9] = inp["m_conv"][l][:, g * 64:(g + 1) * 64].T
            prm[l, 0:64, 9:13] = inp["m_conv"][l][:, 256 + g * 64:256 + (g + 1) * 64].T
            prm[l, :, 13] = inp["m_b_i"][l, g]
            prm[l, :, 14] = inp["m_b_f"][l, g]
            prm[l, 0:64, 15] = inp["m_norm_g"][l, g * 64:(g + 1) * 64]
            prm[l, :, 16] = np.tile(inp["a_q_g"][l], 2)
            prm[l, :, 17] = np.tile(inp["a_k_g"][l], 2)
            prm[l, :, 19:35] = (w / np.minimum(np.arange(16) + 1, w)).astype(np.float32)[None, :]
        maps.append({
            "xT": np.ascontiguousarray(x[b, j * TQ:(j + 1) * TQ, :].T),
            "wfm": wfm, "wtm": wtm, "g1": g1, "w_out": inp["w_out"], "g2": g2, "w_up": w_up_p, "convw": cw, "w_down": inp["w_down"],
            "prm": prm, "poolw": np.ascontiguousarray(inp["pool_w"][:, g]),
            "triU": triU, "ident": ident, "caus": caus, "ind": ind, "idx": fused_idx(cidx),
        })
    if "nc" not in _FUSED:
        _FUSED["nc"] = build_fused()
    res = run_bass_kernel_spmd(_FUSED["nc"], maps, core_ids=list(range(NCORE))).results
    out = np.empty_like(x)
    for cidx in range(NCORE):
        b, j = cidx // 4, cidx % 4
        out[b, j * TQ:(j + 1) * TQ, :] = res[cidx]["xoT"].T
    return out
```
